# Optimizing a Trainium2 kernel written in Bass

```python
import math
import jax
import jax.numpy as jnp
from jax import lax
import numpy as np

D_MODEL = 1024
BATCH = 8
SEQ = 2048
DEPTH = 4

GRID_W = 64
CTX_LEN = 256
N_BRANCH = 4
BRANCH_W = D_MODEL // 4
S5_GROUP = 16
S5_GROUPS = BRANCH_W // S5_GROUP
S5_STATE = 64
NA_HEAD_DIM = 64
NA_HEADS = BRANCH_W // NA_HEAD_DIM
NA_WIN_ROWS = 8
NA_WIN_COLS = 16
NA_SCALE = NA_HEAD_DIM ** -0.5
CONV_W = 3
CHUNK = 128
SGU_GROUPS = 4
SGU_GROUP_W = BRANCH_W // SGU_GROUPS
D_FF = 4 * D_MODEL
N_MOD = 6
EPS = 1e-6
NEG_INF = -1e30
N_SLOTS = 9
IN_W = N_SLOTS * BRANCH_W + N_BRANCH * D_MODEL
SPLIT_IDX = tuple(BRANCH_W * k for k in range(1, N_SLOTS + 1))

kernel_name = 'hybrid_diffusion_parallel_mixer'


def rms_norm(x, g):
    xf = x.astype(jnp.float32)
    y = xf * lax.rsqrt(jnp.mean(xf * xf, axis=-1, keepdims=True) + EPS)
    return (y * g.astype(jnp.float32)).astype(x.dtype)


def layer_norm(x, g, b):
    xf = x.astype(jnp.float32)
    mu = jnp.mean(xf, axis=-1, keepdims=True)
    var = jnp.mean(jnp.square(xf - mu), axis=-1, keepdims=True)
    y = (xf - mu) * lax.rsqrt(var + EPS)
    return (y * g.astype(jnp.float32) + b.astype(jnp.float32)).astype(x.dtype)


def modulate(h, shift, scale):
    return h * (1 + scale) + shift


def s5_discretize(lam_re, lam_im, log_dt, b_re, b_im):
    dt = jnp.exp(log_dt)[:, None]
    zr, zi = lam_re * dt, lam_im * dt
    mag = jnp.exp(zr)
    ar, ai = mag * jnp.cos(zi), mag * jnp.sin(zi)
    nr, ni = ar - 1.0, ai
    den = lam_re * lam_re + lam_im * lam_im
    cr = (nr * lam_re + ni * lam_im) / den
    ci = (ni * lam_re - nr * lam_im) / den
    bbr = cr[..., None] * b_re - ci[..., None] * b_im
    bbi = cr[..., None] * b_im + ci[..., None] * b_re
    return ar, ai, bbr, bbi


def _cplx_combine(e1, e2):
    a1r, a1i, b1r, b1i = e1
    a2r, a2i, b2r, b2i = e2
    return (a2r * a1r - a2i * a1i, a2r * a1i + a2i * a1r,
            a2r * b1r - a2i * b1i + b2r, a2r * b1i + a2i * b1r + b2i)


def s5_scan(u, ar, ai, bbr, bbi, s0, reverse):
    bur = jnp.einsum('blgh,gph->blgp', u, bbr)
    bui = jnp.einsum('blgh,gph->blgp', u, bbi)
    acr = jnp.broadcast_to(ar, bur.shape)
    aci = jnp.broadcast_to(ai, bur.shape)
    pr, pi_, hr, hi = lax.associative_scan(_cplx_combine, (acr, aci, bur, bui), axis=1, reverse=reverse)
    if s0 is not None:
        s0r, s0i = s0[0][:, None], s0[1][:, None]
        hr = hr + pr * s0r - pi_ * s0i
        hi = hi + pr * s0i + pi_ * s0r
    return hr, hi


def s5_readout(hr, hi, cr, ci):
    y = jnp.einsum('blgp,ghp->blgh', hr, cr) - jnp.einsum('blgp,ghp->blgh', hi, ci)
    return y.reshape(y.shape[0], y.shape[1], BRANCH_W)


def s5_glu(y, w, b):
    g = jax.nn.gelu(y)
    return g * jax.nn.sigmoid(g @ w.astype(jnp.float32) + b.astype(jnp.float32))


def s5_branch(u_lat, u_ctx, lam_re, lam_im, log_dt, b_re, b_im, c_re, c_im, d_skip, glu_w, glu_b, ctx_out):
    f32 = jnp.float32
    bn, seq_len, _ = u_lat.shape
    ctx_len = u_ctx.shape[1]
    ul = u_lat.astype(f32).reshape(bn, seq_len, S5_GROUPS, S5_GROUP)
    uc = u_ctx.astype(f32).reshape(bn, ctx_len, S5_GROUPS, S5_GROUP)
    d32 = d_skip.astype(f32)
    yl = u_lat.astype(f32) * d32
    yc = u_ctx.astype(f32) * d32 if ctx_out else None
    for direction in range(2):
        reverse = direction == 1
        ar, ai, bbr, bbi = s5_discretize(lam_re[direction].astype(f32), lam_im[direction].astype(f32),
                                         log_dt[direction].astype(f32), b_re[direction].astype(f32),
                                         b_im[direction].astype(f32))
        cr, ci = c_re[direction].astype(f32), c_im[direction].astype(f32)
        hcr, hci = s5_scan(uc, ar, ai, bbr, bbi, None, reverse)
        end = 0 if reverse else ctx_len - 1
        hlr, hli = s5_scan(ul, ar, ai, bbr, bbi, (hcr[:, end], hci[:, end]), reverse)
        yl = yl + s5_readout(hlr, hli, cr, ci)
        if ctx_out:
            yc = yc + s5_readout(hcr, hci, cr, ci)
    out_l = s5_glu(yl, glu_w, glu_b).astype(u_lat.dtype)
    out_c = s5_glu(yc, glu_w, glu_b).astype(u_ctx.dtype) if ctx_out else None
    return out_l, out_c


def neighborhood_attention(q, k, v, kc, vc, rpb):
    bn, seq_len, _ = q.shape
    rows = seq_len // GRID_W
    wr = min(NA_WIN_ROWS, rows)
    qg = q.reshape(bn, rows, GRID_W, NA_HEADS, NA_HEAD_DIM) * NA_SCALE
    kg = k.reshape(bn, rows, GRID_W, NA_HEADS, NA_HEAD_DIM)
    vg = v.reshape(bn, rows, GRID_W, NA_HEADS, NA_HEAD_DIM)
    kc = kc.reshape(bn, -1, NA_HEADS, NA_HEAD_DIM)
    vc = vc.reshape(bn, -1, NA_HEADS, NA_HEAD_DIM)
    r = jnp.arange(rows)
    row_start = jnp.clip(r - wr // 2, 0, rows - wr)
    row_idx = row_start[:, None] + jnp.arange(wr)[None, :]
    k_band = kg[:, row_idx]
    v_band = vg[:, row_idx]
    j = jnp.arange(GRID_W)
    col_start = jnp.clip(j - NA_WIN_COLS // 2, 0, GRID_W - NA_WIN_COLS)
    col_ok = (j[None, :] >= col_start[:, None]) & (j[None, :] < col_start[:, None] + NA_WIN_COLS)
    dr = row_idx - r[:, None] + NA_WIN_ROWS - 1
    dc = jnp.clip(j[None, :] - j[:, None] + NA_WIN_COLS - 1, 0, 2 * NA_WIN_COLS - 2)
    bias = rpb[:, dr[:, None, :, None], dc[None, :, None, :]]
    s_win = jnp.einsum('brqhd,brskhd->bhrqsk', qg, k_band).astype(jnp.float32) + bias.astype(jnp.float32)[None]
    s_win = jnp.where(col_ok[:, None, :], s_win, NEG_INF)
    n_win = wr * GRID_W
    s_win = s_win.reshape(bn, NA_HEADS, rows, GRID_W, n_win)
    s_ctx = jnp.einsum('brqhd,bchd->bhrqc', qg, kc).astype(jnp.float32)
    p = jax.nn.softmax(jnp.concatenate([s_win, s_ctx], axis=-1), axis=-1).astype(v.dtype)
    p_win = p[..., :n_win].reshape(bn, NA_HEADS, rows, GRID_W, wr, GRID_W)
    p_ctx = p[..., n_win:]
    o = jnp.einsum('bhrqsk,brskhd->brqhd', p_win, v_band) + jnp.einsum('bhrqc,bchd->brqhd', p_ctx, vc)
    return o.reshape(bn, seq_len, BRANCH_W)


def context_attention(qc, kc, vc):
    bn, ctx_len, _ = qc.shape
    q = qc.reshape(bn, ctx_len, NA_HEADS, NA_HEAD_DIM) * NA_SCALE
    k = kc.reshape(bn, ctx_len, NA_HEADS, NA_HEAD_DIM)
    v = vc.reshape(bn, ctx_len, NA_HEADS, NA_HEAD_DIM)
    s = jnp.einsum('bqhd,bkhd->bhqk', q, k).astype(jnp.float32)
    p = jax.nn.softmax(s, axis=-1).astype(v.dtype)
    return jnp.einsum('bhqk,bkhd->bqhd', p, v).reshape(bn, ctx_len, BRANCH_W)


def short_conv(b_gate, c_gate, xin, w):
    z = c_gate * xin
    y = lax.conv_general_dilated(z, w[:, None, :], window_strides=(1,),
                                 padding=((CONV_W // 2, CONV_W // 2),),
                                 dimension_numbers=('NWC', 'WIO', 'NWC'),
                                 feature_group_count=BRANCH_W)
    return b_gate * y


def spatial_gating(u, v, ln_g, ln_b, w_s, b_s):
    bn, seq_len, _ = v.shape
    vn = layer_norm(v, ln_g, ln_b)
    vg = vn.reshape(bn, seq_len // CHUNK, CHUNK, SGU_GROUPS, SGU_GROUP_W)
    z = jnp.einsum('gts,bnsgc->bntgc', w_s, vg) + b_s.T[None, None, :, :, None]
    return u * z.reshape(bn, seq_len, BRANCH_W)


def merge_branches(ys, gate_logits, w_branch, w_out):
    y = jnp.stack(ys, axis=-2)
    yp = jnp.einsum('blkw,kwd->blkd', y, w_branch)
    g = jax.nn.sigmoid(gate_logits.reshape(gate_logits.shape[0], gate_logits.shape[1], N_BRANCH, D_MODEL))
    return jnp.sum(g * yp, axis=-2) @ w_out


def token_mixers(al, ac, w_in, w_branch, w_out, s5_lam_re, s5_lam_im, s5_log_dt, s5_b_re, s5_b_im,
                 s5_c_re, s5_c_im, s5_d, s5_glu_w, s5_glu_b, na_rpb, conv_w, sgu_ln_g, sgu_ln_b,
                 sgu_w, sgu_b, ctx_out):
    pl = jnp.split(al @ w_in, SPLIT_IDX, axis=-1)
    if ctx_out:
        pc = jnp.split(ac @ w_in, SPLIT_IDX, axis=-1)
    else:
        pc = jnp.split(ac @ w_in[:, :4 * BRANCH_W], SPLIT_IDX[:3], axis=-1)
    ya_l, ya_c = s5_branch(pl[0], pc[0], s5_lam_re, s5_lam_im, s5_log_dt, s5_b_re, s5_b_im,
                           s5_c_re, s5_c_im, s5_d, s5_glu_w, s5_glu_b, ctx_out)
    yb_l = neighborhood_attention(pl[1], pl[2], pl[3], pc[2], pc[3], na_rpb)
    yc_l = short_conv(pl[4], pl[5], pl[6], conv_w)
    yd_l = spatial_gating(pl[7], pl[8], sgu_ln_g, sgu_ln_b, sgu_w, sgu_b)
    out_l = merge_branches((ya_l, yb_l, yc_l, yd_l), pl[9], w_branch, w_out)
    if not ctx_out:
        return out_l, None
    yb_c = context_attention(pc[1], pc[2], pc[3])
    yc_c = short_conv(pc[4], pc[5], pc[6], conv_w)
    yd_c = spatial_gating(pc[7], pc[8], sgu_ln_g, sgu_ln_b, sgu_w, sgu_b)
    out_c = merge_branches((ya_c, yb_c, yc_c, yd_c), pc[9], w_branch, w_out)
    return out_l, out_c


def sq_relu_mlp(h, w1, w2):
    return jnp.square(jax.nn.relu(h @ w1)) @ w2


def setup_inputs(seed: int = 0) -> dict:
    key = jax.random.key(seed)
    ks = jax.random.split(key, 32)
    f32 = jnp.float32

    def nrm(k, shape, scale):
        return jax.random.normal(k, shape, f32) * scale

    n_idx = jnp.arange(S5_STATE, dtype=f32)
    s5_shape = (DEPTH, 2, S5_GROUPS, S5_STATE)
    return {
        'x': nrm(ks[0], (BATCH, SEQ, D_MODEL), 1.0),
        'c': nrm(ks[1], (BATCH, D_MODEL), 1.0),
        'ctx': nrm(ks[2], (BATCH, CTX_LEN, D_MODEL), 1.0),
        'c_ctx': nrm(ks[3], (D_MODEL,), 1.0),
        'ada_w': nrm(ks[4], (DEPTH, D_MODEL, N_MOD * D_MODEL), 0.5 * D_MODEL ** -0.5),
        'ada_b': nrm(ks[5], (DEPTH, N_MOD * D_MODEL), 0.02),
        'norm_g': 1.0 + nrm(ks[6], (DEPTH, 2, D_MODEL), 0.02),
        'final_g': 1.0 + nrm(ks[7], (D_MODEL,), 0.02),
        'w_in': nrm(ks[8], (DEPTH, D_MODEL, IN_W), D_MODEL ** -0.5),
        'w_branch': nrm(ks[9], (DEPTH, N_BRANCH, BRANCH_W, D_MODEL), BRANCH_W ** -0.5),
        'w_out': nrm(ks[10], (DEPTH, D_MODEL, D_MODEL), D_MODEL ** -0.5),
        's5_lam_re': -0.5 * jnp.exp(nrm(ks[11], s5_shape, 0.05)),
        's5_lam_im': math.pi * n_idx + nrm(ks[12], s5_shape, 0.01),
        's5_log_dt': jax.random.uniform(ks[13], (DEPTH, 2, S5_GROUPS), f32, math.log(1e-3), math.log(1e-1)),
        's5_b_re': nrm(ks[14], (DEPTH, 2, S5_GROUPS, S5_STATE, S5_GROUP), (2 * S5_GROUP) ** -0.5),
        's5_b_im': nrm(ks[15], (DEPTH, 2, S5_GROUPS, S5_STATE, S5_GROUP), (2 * S5_GROUP) ** -0.5),
        's5_c_re': nrm(ks[16], (DEPTH, 2, S5_GROUPS, S5_GROUP, S5_STATE), (2 * S5_STATE) ** -0.5),
        's5_c_im': nrm(ks[17], (DEPTH, 2, S5_GROUPS, S5_GROUP, S5_STATE), (2 * S5_STATE) ** -0.5),
        's5_d': nrm(ks[18], (DEPTH, BRANCH_W), 1.0),
        's5_glu_w': nrm(ks[19], (DEPTH, BRANCH_W, BRANCH_W), BRANCH_W ** -0.5),
        's5_glu_b': nrm(ks[20], (DEPTH, BRANCH_W), 0.02),
        'na_rpb': nrm(ks[21], (DEPTH, NA_HEADS, 2 * NA_WIN_ROWS - 1, 2 * NA_WIN_COLS - 1), 0.1),
        'conv_w': nrm(ks[22], (DEPTH, CONV_W, BRANCH_W), CONV_W ** -0.5),
        'sgu_ln_g': 1.0 + nrm(ks[23], (DEPTH, BRANCH_W), 0.02),
        'sgu_ln_b': nrm(ks[24], (DEPTH, BRANCH_W), 0.02),
        'sgu_w': nrm(ks[25], (DEPTH, SGU_GROUPS, CHUNK, CHUNK), CHUNK ** -0.5),
        'sgu_b': 1.0 + nrm(ks[26], (DEPTH, SGU_GROUPS, CHUNK), 0.02),
        'w_ff1': nrm(ks[27], (DEPTH, D_MODEL, D_FF), D_MODEL ** -0.5),
        'w_ff2': nrm(ks[28], (DEPTH, D_FF, D_MODEL), D_FF ** -0.5),
    }


def reference(x, c, ctx, c_ctx, ada_w, ada_b, norm_g, final_g, w_in, w_branch, w_out,
              s5_lam_re, s5_lam_im, s5_log_dt, s5_b_re, s5_b_im, s5_c_re, s5_c_im, s5_d,
              s5_glu_w, s5_glu_b, na_rpb, conv_w, sgu_ln_g, sgu_ln_b, sgu_w, sgu_b, w_ff1, w_ff2):
    silu_c = jax.nn.silu(c)
    silu_cc = jax.nn.silu(c_ctx)
    hl, hc = x, ctx
    for i in range(DEPTH):
        ctx_out = i < DEPTH - 1
        ml = jnp.split((silu_c @ ada_w[i] + ada_b[i])[:, None, :], N_MOD, axis=-1)
        mc = jnp.split(silu_cc @ ada_w[i] + ada_b[i], N_MOD, axis=-1)
        al = modulate(rms_norm(hl, norm_g[i, 0]), ml[0], ml[1])
        ac = modulate(rms_norm(hc, norm_g[i, 0]), mc[0], mc[1])
        yl, yc = token_mixers(al, ac, w_in[i], w_branch[i], w_out[i], s5_lam_re[i], s5_lam_im[i],
                              s5_log_dt[i], s5_b_re[i], s5_b_im[i], s5_c_re[i], s5_c_im[i], s5_d[i],
                              s5_glu_w[i], s5_glu_b[i], na_rpb[i], conv_w[i], sgu_ln_g[i], sgu_ln_b[i],
                              sgu_w[i], sgu_b[i], ctx_out)
        hl = hl + ml[2] * yl
        bl = modulate(rms_norm(hl, norm_g[i, 1]), ml[3], ml[4])
        hl = hl + ml[5] * sq_relu_mlp(bl, w_ff1[i], w_ff2[i])
        if ctx_out:
            hc = hc + mc[2] * yc
            bc = modulate(rms_norm(hc, norm_g[i, 1]), mc[3], mc[4])
            hc = hc + mc[5] * sq_relu_mlp(bc, w_ff1[i], w_ff2[i])
    return rms_norm(hl, final_g)
```

```python
import os
import numpy as np
from contextlib import ExitStack
CUT = int(os.environ.get('CUT', '0'))
import concourse.bass as bass
import concourse.mybir as mybir
from concourse.bass_utils import run_bass_kernel_spmd

F32 = mybir.dt.float32
BF16 = mybir.dt.bfloat16
I32 = mybir.dt.int32
AF = mybir.ActivationFunctionType
ALU = mybir.AluOpType

NDSEM = 8
SAME_ENGINE_SYNC = True
PI = float(np.pi)

NTOK = 2304
NTILE = 18
MT = [(0, 512), (512, 512), (1024, 512), (1536, 512), (2048, 256)]
NEG = -30000.0


class Buf:
    __slots__ = ("w", "r")

    def __init__(self):
        self.w = None
        self.r = {}


class T:
    def __init__(self, h, shape):
        self.h = h
        self.shape = list(shape)
        self.F = int(np.prod(shape[1:]))

    def __getitem__(self, idx):
        return self.h[idx]

    def v(self, off, dims, p0=0, np_=None):
        if np_ is None:
            np_ = self.shape[0] - p0
        return bass.AP(self.h, p0 * self.F + off, [[self.F, np_]] + [list(d) for d in dims])


def DA(h, off, dims):
    return bass.AP(h, off, [list(d) for d in dims])


class K:
    ENGS = ("pe", "act", "dve", "pool", "sp")
    DMAQ = ("sp", "act", "pool")

    def __init__(self, nc, stack):
        self.nc = nc
        self.stack = stack
        self.prog = {e: [] for e in self.ENGS}
        self.sem = {e: stack.enter_context(nc.semaphore("s_" + e)) for e in self.ENGS}
        self.cnt = {e: 0 for e in self.ENGS}
        self.seen = {e: {f: 0 for f in self.ENGS} for e in self.ENGS}
        self.dsem = {q: [stack.enter_context(nc.semaphore("d_%s%d" % (q, j))) for j in range(NDSEM)]
                     for q in self.DMAQ}
        self.dtarget = {q: [0] * NDSEM for q in self.DMAQ}
        self.dnext = {q: 0 for q in self.DMAQ}
        self.dseen = {e: {} for e in self.ENGS}
        self.bufs = {}
        self.sb_off = 16640
        self.sb_marks = []
        self.uid = 0
        self.nbank = 0
        self.banks = []

    def sb(self, name, shape, dtype, parts=None):
        esz = 2 if dtype == BF16 else 4
        nbytes = int(np.prod(shape[1:])) * esz
        off = (self.sb_off + 63) // 64 * 64
        self.uid += 1
        h = self.nc.alloc_sbuf_tensor_at("%s_%d" % (name, self.uid), list(shape), dtype, offset=off)
        self.sb_off = off + nbytes
        assert self.sb_off <= 229376, ("SBUF overflow", name, self.sb_off)
        return T(h, shape)

    def mark(self):
        self.sb_marks.append(self.sb_off)

    def release(self):
        self.barrier()
        self.sb_off = self.sb_marks.pop()

    def bank(self):
        i = self.nbank % len(self.banks)
        self.nbank += 1
        return self.banks[i], ("ps", i)

    def _buf(self, k):
        b = self.bufs.get(k)
        if b is None:
            b = self.bufs[k] = Buf()
        return b

    def _wait(self, e, tok):
        if tok[0] == "eng":
            _, f, n = tok
            if f == e and (e == "pe" or not SAME_ENGINE_SYNC):
                return
            if self.seen[e][f] >= n:
                return
            self.seen[e][f] = n
            sem = self.sem[f]
            self.prog[e].append(lambda eng, sem=sem, n=n: eng.wait_ge(sem, n))
        else:
            _, q, j, tgt = tok
            if self.dseen[e].get((q, j), 0) >= tgt:
                return
            self.dseen[e][(q, j)] = tgt
            sem = self.dsem[q][j]
            self.prog[e].append(lambda eng, sem=sem, tgt=tgt: eng.wait_ge(sem, tgt))

    def _deps(self, e, reads, writes):
        toks = []
        for k in reads:
            b = self._buf(k)
            if b.w is not None:
                toks.append(b.w)
        for k in writes:
            b = self._buf(k)
            if b.w is not None:
                toks.append(b.w)
            toks.extend(b.r.values())
        for t in toks:
            self._wait(e, t)

    def _commit(self, tok, reads, writes):
        for k in reads:
            b = self._buf(k)
            if tok[0] == "eng":
                b.r[("eng", tok[1])] = tok
            else:
                b.r[("dma", tok[1], tok[2])] = tok
        for k in writes:
            b = self._buf(k)
            b.w = tok
            b.r = {}

    def op(self, e, fn, reads=(), writes=()):
        psr = [r for r in reads if isinstance(r, tuple) and r[0] in ("ps", "psT")]
        if psr:
            reads = [r for r in reads if r not in psr]
            writes = list(writes) + psr
        self._deps(e, reads, writes)
        self.cnt[e] += 1
        n = self.cnt[e]
        sem = self.sem[e]
        self.prog[e].append(lambda eng, fn=fn, sem=sem: fn(eng).then_inc(sem, 1))
        self._commit(("eng", e, n), reads, writes)

    def dma(self, q, out, in_, reads=(), writes=(), **kw):
        self._deps(q, reads, writes)
        j = self.dnext[q]
        self.dnext[q] = (j + 1) % NDSEM
        prev = self.dtarget[q][j]
        if prev > 0:
            self._wait(q, ("dma", q, j, prev))
        self.dtarget[q][j] = prev + 16
        sem = self.dsem[q][j]
        self.prog[q].append(lambda eng, out=out, in_=in_, sem=sem, kw=kw:
                            eng.dma_start(out=out, in_=in_, **kw).then_inc(sem, 16))
        self._commit(("dma", q, j, prev + 16), reads, writes)

    def barrier(self):
        for e in self.ENGS:
            for f in self.ENGS:
                if f != e and self.cnt[f] > 0:
                    self._wait(e, ("eng", f, self.cnt[f]))
            for q in self.DMAQ:
                for j in range(NDSEM):
                    if self.dtarget[q][j] > 0:
                        self._wait(e, ("dma", q, j, self.dtarget[q][j]))
        self.bufs = {}

    def emit(self):
        self.barrier()
        nc = self.nc
        prog = self.prog
        with nc.Block() as block:
            @block.tensor
            def _(eng):
                for f in prog["pe"]:
                    f(eng)

            @block.scalar
            def _(eng):
                for f in prog["act"]:
                    f(eng)

            @block.vector
            def _(eng):
                for f in prog["dve"]:
                    f(eng)

            @block.gpsimd
            def _(eng):
                for f in prog["pool"]:
                    f(eng)

            @block.sync
            def _(eng):
                for f in prog["sp"]:
                    f(eng)


INPUT_SHAPES = {
    "x": [2048, 1024], "c": [1024], "ctx": [256, 1024], "c_ctx": [1024],
    "ada_w": [4, 1024, 6144], "ada_b": [4, 6144], "norm_g": [4, 2, 1024], "final_g": [1024],
    "w_in": [4, 1024, 6400], "w_branch": [4, 4, 256, 1024], "w_out": [4, 1024, 1024],
    "s5_lam_re": [4, 2, 16, 64], "s5_lam_im": [4, 2, 16, 64], "s5_log_dt": [4, 2, 16],
    "s5_b_re": [4, 2, 16, 64, 16], "s5_b_im": [4, 2, 16, 64, 16],
    "s5_c_re": [4, 2, 16, 16, 64], "s5_c_im": [4, 2, 16, 16, 64],
    "s5_d": [4, 256], "s5_glu_w": [4, 256, 256], "s5_glu_b": [4, 256],
    "na_rpb": [4, 4, 15, 31], "conv_w": [4, 3, 256], "sgu_ln_g": [4, 256], "sgu_ln_b": [4, 256],
    "sgu_w": [4, 4, 128, 128], "sgu_b": [4, 4, 128], "w_ff1": [4, 1024, 4096], "w_ff2": [4, 4096, 1024],
    "k_ident": [128, 128], "k_colmask": [128, 64], "k_trimask": [2, 128, 128], "k_s5nt": [2, 25],
}


def host_constants():
    ident = np.eye(128, dtype=np.float32)
    j = np.arange(64)
    col_start = np.clip(j - 8, 0, 48)
    col_ok = (j[None, :] >= col_start[:, None]) & (j[None, :] < col_start[:, None] + 16)
    cm = np.where(col_ok.T, 0.0, NEG).astype(np.float32)
    colmask = np.concatenate([cm, cm], 0)
    r = np.arange(128) // 16
    tri = np.stack([(r[None, :] >= r[:, None]), (r[:, None] >= r[None, :])]).astype(np.float32)
    nt = np.zeros((2, 25), np.float32)
    rr = np.arange(8)
    nt[0, 0:8] = 7 - rr; nt[0, 8:16] = -1 - rr; nt[0, 16:24] = rr + 1; nt[0, 24] = 1
    nt[1, 0:8] = rr; nt[1, 8:16] = rr - 8; nt[1, 16:24] = 8 - rr; nt[1, 24] = 1
    return {"k_ident": ident, "k_colmask": colmask, "k_trimask": tri, "k_s5nt": nt}


def build(nl=4, dbg=(), stop=None, skip=()):
    nc = bass.Bass("TRN2", target_bir_lowering=False)
    I = {n: nc.dram_tensor(n, s, F32, kind="ExternalInput") for n, s in INPUT_SHAPES.items()}
    OUT = nc.dram_tensor("out", [2048, 1024], F32, kind="ExternalOutput")
    dbg_out = {}

    def scratch(name, shape, dt=F32):
        return nc.dram_tensor(name, list(shape), dt)

    H = scratch("H", [NTOK, 1024])
    MODROW = scratch("MODROW", [4, 2, 6144])
    PWD = scratch("PWD", [2, 64, 128 * 25])
    BBD = scratch("BBD", [2, 64, 128 * 16])
    CD = scratch("CD", [2, 64, 128 * 16])
    UDP = scratch("UDP", [256, NTOK], BF16)
    YDP = scratch("YDP", [256, NTOK])
    YT = scratch("YT", [4, 256, NTOK], BF16)
    RP = scratch("RP", [4 * 4 * 15 + 1, 160])

    with ExitStack() as st:
        k = K(nc, st)
        for i in range(6):
            k.banks.append(T(st.enter_context(nc.psum_tensor("psb%d" % i, [128, 512], F32)), [128, 512]))
        psT = [T(st.enter_context(nc.psum_tensor("psT%d" % i, [128, 1024], BF16)), [128, 1024]) for i in range(2)]

        def dump(name, src_ap, shape, dt=F32, reads=()):
            if name not in dbg:
                return
            t = nc.dram_tensor("dbg_" + name, list(shape), dt, kind="ExternalOutput")
            dbg_out[name] = t
            k.barrier()
            k.dma("sp", t.ap(), src_ap, reads=reads)
            k.barrier()

        ident_f = k.sb("ident_f", [128, 128], F32)
        ident_b = k.sb("ident_b", [128, 128], BF16)
        k.dma("sp", ident_f[:], I["k_ident"].ap(), writes=["ident_f"])
        k.dma("pool", ident_b[:], I["k_ident"].ap(), writes=["ident_b"])
        MODC = k.sb("MODC", [128, 4, 4, 8, 2], F32)
        GS = k.sb("GS", [128, 4, 2, 8, 2], F32)
        WB = [k.sb("WB%d" % i, [128, 8, 256], BF16) for i in range(2)]
        alT = k.sb("alT", [128, 8, NTOK], BF16)

        def phase_M():
            k.dma("sp", DA(H, 0, [[1024, 256], [1, 1024]]), I["ctx"].ap(), writes=[("H", t) for t in range(2)])
            k.dma("sp", DA(H, 256 * 1024, [[1024, 2048], [1, 1024]]), I["x"].ap(), writes=[("H", t) for t in range(2, 18)])

            if CUT == 1:
                return
            k.mark()
            scraw = k.sb("scraw", [128, 8, 2], F32)
            SC = k.sb("SC", [128, 8, 2], F32)
            k.dma("sp", scraw.v(0, [[2, 8], [1, 1]]), DA(I["c"], 0, [[1, 128], [128, 8], [1, 1]]), writes=["scraw"], allow_slow_non_contiguous=True)
            k.dma("sp", scraw.v(1, [[2, 8], [1, 1]]), DA(I["c_ctx"], 0, [[1, 128], [128, 8], [1, 1]]), writes=["scraw"], allow_slow_non_contiguous=True)
            k.op("act", lambda e: e.activation(SC[:], scraw[:], AF.Silu), reads=["scraw"], writes=["SC"])
            NG = k.sb("NG", [128, 8, 8], F32)
            for lj in range(8):
                k.dma("sp", NG.v(lj * 8, [[1, 8], [1, 1]]), DA(I["norm_g"], lj * 1024, [[1, 128], [128, 8], [1, 1]]),
                      writes=["NG"], allow_slow_non_contiguous=True)
            if CUT == 2:
                k.release(); return
            modrow = k.sb("modrow", [2, 6144], F32)
            adab = k.sb("adab", [2, 6144], F32)
            adaw = [k.sb("adaw%d" % i, [128, 8, 512], F32) for i in range(2)]
            for l in range(nl):
                k.dma("sp", adab[:], DA(I["ada_b"], l * 6144, [[0, 2], [1, 6144]]), writes=["adab"])
                for n in range(12):
                    wt = adaw[n % 2]
                    wk = ("adaw", n % 2)
                    k.dma("sp", wt[:], DA(I["ada_w"], l * 1024 * 6144 + n * 512, [[6144, 128], [128 * 6144, 8], [1, 512]]),
                          writes=[wk])
                    ps, pk = k.bank()
                    for kc in range(8):
                        k.op("pe", lambda e, ps=ps, wt=wt, kc=kc: e.matmul(ps[0:2, 0:512], SC[:, kc, :], wt[:, kc, :],
                                                                          start=(kc == 0), stop=(kc == 7)),
                             reads=[wk, "SC"], writes=[pk])
                    k.op("dve", lambda e, ps=ps, n=n: e.tensor_tensor(modrow[0:2, n * 512:(n + 1) * 512], ps[0:2, 0:512],
                                                                     adab[0:2, n * 512:(n + 1) * 512], ALU.add),
                         reads=[pk, "adab"], writes=["modrow"])
                if CUT == 3:
                    k.release(); return
                k.dma("sp", DA(MODROW, l * 2 * 6144, [[6144, 2], [1, 6144]]), modrow[:], reads=["modrow"],
                      writes=[("MODROW", l)])
                if CUT == 4:
                    k.release(); return
                ps, pk = k.bank()
                for mi, m in enumerate([0, 1, 3, 4]):
                    for kc in range(8):
                        k.op("pe", lambda e, ps=ps, mi=mi, m=m, kc=kc: e.matmul(
                            ps[:, (mi * 8 + kc) * 2:(mi * 8 + kc) * 2 + 2],
                            modrow[0:2, m * 1024 + kc * 128:m * 1024 + kc * 128 + 128], ident_f[0:2, 0:2],
                            start=True, stop=True), reads=["modrow", "ident_f"], writes=[pk])
                k.op("dve", lambda e, ps=ps, l=l: e.tensor_copy(MODC.v(l * 64, [[1, 64]]), ps[:, 0:64]),
                     reads=[pk], writes=["MODC"])
                if CUT == 5:
                    k.release(); return
                for j in range(2):
                    k.op("dve", lambda e, l=l, j=j: e.scalar_tensor_tensor(
                        GS.v((l * 2 + j) * 16, [[2, 8], [1, 2]]), MODC.v(l * 64 + (2 * j + 1) * 16, [[2, 8], [1, 2]]), 1.0,
                        NG.v((l * 2 + j) * 8, [[1, 8], [0, 2]]), ALU.add, ALU.mult),
                        reads=["MODC", "NG"], writes=["GS"])
            k.release()


        def SHv(l, j, kc, w):
            return MODC.v(l * 64 + (2 * j) * 16 + kc * 2 + w, [[1, 1]])

        def GSv(l, j, kc, w):
            return GS.v((l * 2 + j) * 16 + kc * 2 + w, [[1, 1]])

        def norm_phase(l, j, dst, dkey):
            k.mark()
            XT = [k.sb("XT%d" % i, [128, 1024], F32) for i in range(2)]
            xn = [k.sb("xn%d" % i, [128, 1024], BF16) for i in range(2)]
            junk = k.sb("junk", [128, 1024], BF16)
            ss = k.sb("ss", [128, NTILE], F32)
            rs = k.sb("rs", [128, NTILE], F32)
            k.op("dve", lambda e: e.memset(ss[:], 0.0), writes=["ss"])
            for t in range(NTILE):
                b = t % 2
                xt = XT[b]
                k.dma("sp", xt[:], DA(H, t * 128 * 1024, [[1024, 128], [1, 1024]]), reads=[("H", t)], writes=[("xt", b)])
                k.op("act", lambda e, xt=xt, t=t: e.activation(junk[:], xt[:], AF.Square, accum_out=ss[:, t:t + 1]),
                     reads=[("xt", b), "ss"], writes=["junk", "ss"])
                k.op("dve", lambda e, t=t: e.tensor_scalar(rs[:, t:t + 1], ss[:, t:t + 1], 1.0 / 1024, 1e-6, ALU.mult, ALU.add),
                     reads=["ss"], writes=["rs"])
                k.op("act", lambda e, t=t: e.activation(rs[:, t:t + 1], rs[:, t:t + 1], AF.Sqrt), reads=["rs"], writes=["rs"])
                k.op("dve", lambda e, t=t: e.reciprocal(rs[:, t:t + 1], rs[:, t:t + 1]), reads=["rs"], writes=["rs"])
                k.op("dve", lambda e, xt=xt, b=b, t=t: e.tensor_scalar(xn[b][:], xt[:], rs[:, t:t + 1], None, ALU.mult),
                     reads=[("xt", b), "rs"], writes=[("xn", b)])
                if CUT == 11:
                    continue
                for kc in range(8):
                    k.op("pe", lambda e, b=b, kc=kc: e.transpose(psT[b][:, kc * 128:(kc + 1) * 128],
                                                                 xn[b][:, kc * 128:(kc + 1) * 128], ident_b[:]),
                         reads=[("xn", b), "ident_b"], writes=[("psT", b)])
                w = 1 if t < 2 else 0
                if CUT == 12:
                    continue
                for kc in range(8):
                    o = dst[:, kc, t * 128:(t + 1) * 128]
                    i_ = psT[b][:, kc * 128:(kc + 1) * 128]
                    if True:
                        k.op("dve", lambda e, o=o, i_=i_, kc=kc, w=w: e.tensor_scalar(
                            o, i_, GSv(l, j, kc, w), SHv(l, j, kc, w), ALU.mult, ALU.add),
                            reads=[("psT", b), "GS", "MODC"], writes=[(dkey, t, 0)])
                    else:
                        k.op("act", lambda e, o=o, i_=i_, kc=kc, w=w: e.activation(
                            o, i_, AF.Identity, bias=SHv(l, j, kc, w), scale=GSv(l, j, kc, w)),
                            reads=[("psT", b), "GS", "MODC"], writes=[(dkey, t, kc % 2)])
            k.release()

        def tkeys(key, tok0, ntok):
            return [(key, t, p) for t in range(tok0 // 128, (tok0 + ntok + 127) // 128) for p in range(2)]

        def both(key):
            return [(key, 0), (key, 1)]

        def wload(wb_ap, handle, off, row_stride, nk, ncols, key, kstride=None):
            if kstride is None:
                kstride = 128 * row_stride
            k.dma("pool", wb_ap, DA(handle, off, [[row_stride, 128], [kstride, nk], [1, ncols]]), writes=[key])

        wbi = [0]

        def proj_fm(l, col0, ncols, evac, srcT=alT, skey="alT"):
            for cb in range(0, ncols, 256):
                nb = min(256, ncols - cb)
                i = wbi[0] % 2
                wbi[0] += 1
                wb = WB[i]
                wload(wb.v(0, [[256, 8], [1, nb]]), I["w_in"], l * 1024 * 6400 + col0 + cb, 6400, 8, nb, ("WB", i))
                for sub in range(0, nb, 128):
                    ci = (cb + sub) // 128
                    for (tok0, ntok) in MT:
                        ps, pk = k.bank()
                        for kc in range(8):
                            k.op("pe", lambda e, ps=ps, wb=wb, kc=kc, sub=sub, tok0=tok0, ntok=ntok: e.matmul(
                                ps[:, 0:ntok], wb[:, kc, sub:sub + 128], srcT[:, kc, tok0:tok0 + ntok],
                                start=(kc == 0), stop=(kc == 7)),
                                reads=[("WB", i)] + tkeys(skey, tok0, ntok), writes=[pk])
                        evac(ci, tok0, ntok, ps, pk)

        evi = [0]

        def copy_evac(out_ap, in_ap, reads, writes, scale=None):
            evi[0] += 1
            writes = [(w_, evi[0] % 2) for w_ in writes]
            fe = os.environ.get("EVAC", "")
            if (evi[0] % 2 == 0 or fe == "dve") and fe != "act":
                if scale is None:
                    k.op("dve", lambda e: e.tensor_copy(out_ap, in_ap), reads=reads, writes=writes)
                else:
                    k.op("dve", lambda e: e.tensor_scalar(out_ap, in_ap, scale, None, ALU.mult), reads=reads, writes=writes)
            else:
                if scale is None:
                    k.op("act", lambda e: e.activation(out_ap, in_ap, AF.Copy), reads=reads, writes=writes)
                else:
                    k.op("act", lambda e: e.activation(out_ap, in_ap, AF.Copy, scale=scale), reads=reads, writes=writes)

        def s5_setup():
            k.mark()
            nat = k.sb("nat", [128, 64], F32)
            lam = [k.sb("lam%d" % i, [64, 128], F32) for i in range(2)]
            for i, nm in enumerate(["s5_lam_re", "s5_lam_im"]):
                k.dma("sp", nat[:], DA(I[nm], 0, [[64, 128], [1, 64]]), writes=["nat"])
                ps, pk = k.bank()
                k.op("pe", lambda e, ps=ps: e.matmul(ps[0:64, 0:128], nat[:], ident_f[:], start=True, stop=True),
                     reads=["nat", "ident_f"], writes=[pk])
                k.op("dve", lambda e, ps=ps, i=i: e.tensor_copy(lam[i][:], ps[0:64, 0:128]), reads=[pk], writes=[("lam", i)])
            dt = k.sb("dt", [64, 128], F32)
            k.dma("sp", dt[:], DA(I["s5_log_dt"], 0, [[0, 64], [1, 128]]), writes=["dt"])
            k.op("act", lambda e: e.activation(dt[:], dt[:], AF.Exp), reads=["dt"], writes=["dt"])
            z = [k.sb("z%d" % i, [64, 128], F32) for i in range(2)]
            for i in range(2):
                k.op("dve", lambda e, i=i: e.tensor_tensor(z[i][:], lam[i][:], dt[:], ALU.mult),
                     reads=[("lam", i), "dt"], writes=[("z", i)])
            ntb = k.sb("ntb", [64, 50], F32)
            k.dma("sp", ntb[:], DA(I["k_s5nt"], 0, [[0, 64], [1, 50]]), writes=["ntb"])
            NN = 128 * 25
            ZR = k.sb("ZR", [64, NN], F32)
            ZI = k.sb("ZI", [64, NN], F32)
            for zi_, Zt, zk in ((0, ZR, "ZR"), (1, ZI, "ZI")):
                for d in range(2):
                    k.op("dve", lambda e, zi_=zi_, Zt=Zt, d=d: e.tensor_tensor(
                        Zt.v(d * 16 * 25, [[32 * 25, 4], [25, 16], [1, 25]]),
                        z[zi_].v(d * 16, [[32, 4], [1, 16], [0, 25]]),
                        ntb.v(d * 25, [[0, 4], [0, 16], [1, 25]]), ALU.mult),
                        reads=[("z", zi_), "ntb"], writes=[zk])
            MAG = k.sb("MAG", [64, NN], F32)
            k.op("act", lambda e: e.activation(MAG[:], ZR[:], AF.Exp), reads=["ZR"], writes=["MAG"])
            t1 = k.sb("t1", [64, NN], F32)
            ti = k.sb("ti", [64, NN], I32)
            TR = [k.sb("TR%d" % i, [64, NN], F32) for i in range(2)]

            def trig(dst, dk, shift):
                k.op("dve", lambda e: e.tensor_scalar(dst[:], ZI[:], shift, None, ALU.add), reads=["ZI"], writes=[dk])
                k.op("dve", lambda e: e.tensor_scalar(t1[:], dst[:], 1.0 / (2 * PI), None, ALU.mult), reads=[dk], writes=["t1"])
                k.op("dve", lambda e: e.tensor_copy(ti[:], t1[:]), reads=["t1"], writes=["ti"])
                k.op("dve", lambda e: e.tensor_copy(t1[:], ti[:]), reads=["ti"], writes=["t1"])
                k.op("dve", lambda e: e.scalar_tensor_tensor(dst[:], t1[:], -2 * PI, dst[:], ALU.mult, ALU.add),
                     reads=["t1", dk], writes=[dk])
                k.op("dve", lambda e: e.tensor_scalar(dst[:], dst[:], -3.1415925, 3.1415925, ALU.max, ALU.min),
                     reads=[dk], writes=[dk])
                k.op("act", lambda e: e.activation(dst[:], dst[:], AF.Sin), reads=[dk], writes=[dk])
            trig(TR[0], "TR0", PI / 2)
            trig(TR[1], "TR1", 0.0)
            for i in range(2):
                k.op("dve", lambda e, i=i: e.tensor_tensor(TR[i][:], TR[i][:], MAG[:], ALU.mult),
                     reads=[("TR%d" % i), "MAG"], writes=[("TR%d" % i)])
                k.dma("sp", DA(PWD, i * 64 * NN, [[NN, 64], [1, NN]]), TR[i][:], reads=["TR%d" % i], writes=["PWD"])
            def a_v(i):
                return TR[i].v(24, [[25, 128]])
            nr = k.sb("nr", [64, 128], F32)
            den = k.sb("den", [64, 128], F32)
            tq = k.sb("tq", [64, 128], F32)
            cc_ = [k.sb("cc%d" % i, [64, 128], F32) for i in range(2)]
            k.op("dve", lambda e: e.tensor_scalar(nr[:], a_v(0), -1.0, None, ALU.add), reads=["TR0"], writes=["nr"])
            k.op("dve", lambda e: e.tensor_tensor(den[:], lam[0][:], lam[0][:], ALU.mult), reads=[("lam", 0)], writes=["den"])
            k.op("dve", lambda e: e.tensor_tensor(tq[:], lam[1][:], lam[1][:], ALU.mult), reads=[("lam", 1)], writes=["tq"])
            k.op("dve", lambda e: e.tensor_tensor(den[:], den[:], tq[:], ALU.add), reads=["den", "tq"], writes=["den"])
            k.op("dve", lambda e: e.reciprocal(den[:], den[:]), reads=["den"], writes=["den"])
            k.op("dve", lambda e: e.tensor_tensor(cc_[0][:], nr[:], lam[0][:], ALU.mult), reads=["nr", ("lam", 0)], writes=["cc0"])
            k.op("dve", lambda e: e.tensor_tensor(tq[:], a_v(1), lam[1][:], ALU.mult), reads=["TR1", ("lam", 1)], writes=["tq"])
            k.op("dve", lambda e: e.tensor_tensor(cc_[0][:], cc_[0][:], tq[:], ALU.add), reads=["cc0", "tq"], writes=["cc0"])
            k.op("dve", lambda e: e.tensor_tensor(cc_[0][:], cc_[0][:], den[:], ALU.mult), reads=["cc0", "den"], writes=["cc0"])
            k.op("dve", lambda e: e.tensor_tensor(cc_[1][:], a_v(1), lam[0][:], ALU.mult), reads=["TR1", ("lam", 0)], writes=["cc1"])
            k.op("dve", lambda e: e.tensor_tensor(tq[:], nr[:], lam[1][:], ALU.mult), reads=["nr", ("lam", 1)], writes=["tq"])
            k.op("dve", lambda e: e.tensor_tensor(cc_[1][:], cc_[1][:], tq[:], ALU.subtract), reads=["cc1", "tq"], writes=["cc1"])
            k.op("dve", lambda e: e.tensor_tensor(cc_[1][:], cc_[1][:], den[:], ALU.mult), reads=["cc1", "den"], writes=["cc1"])
            BN = 128 * 16
            Bsrc = [k.sb("Bsrc%d" % i, [64, BN], F32) for i in range(2)]
            for i, nm in enumerate(["s5_b_re", "s5_b_im"]):
                for l4 in range(4):
                    k.dma("sp", Bsrc[i].v(l4 * 512, [[16, 32], [1, 16]]),
                          DA(I[nm], l4 * 32 * 1024, [[16, 64], [1024, 32], [1, 16]]), writes=[("Bsrc", i)])
            Bb = [k.sb("Bb%d" % i, [64, BN], F32) for i in range(2)]
            tb = k.sb("tb", [64, BN], F32)

            def cb(i):
                return cc_[i].v(0, [[1, 128], [0, 16]])
            k.op("dve", lambda e: e.tensor_tensor(Bb[0].v(0, [[16, 128], [1, 16]]), Bsrc[0].v(0, [[16, 128], [1, 16]]), cb(0), ALU.mult),
                 reads=[("Bsrc", 0), "cc0"], writes=["Bb0"])
            k.op("dve", lambda e: e.tensor_tensor(tb.v(0, [[16, 128], [1, 16]]), Bsrc[1].v(0, [[16, 128], [1, 16]]), cb(1), ALU.mult),
                 reads=[("Bsrc", 1), "cc1"], writes=["tb"])
            k.op("dve", lambda e: e.tensor_tensor(Bb[0][:], Bb[0][:], tb[:], ALU.subtract), reads=["Bb0", "tb"], writes=["Bb0"])
            k.op("dve", lambda e: e.tensor_tensor(Bb[1].v(0, [[16, 128], [1, 16]]), Bsrc[1].v(0, [[16, 128], [1, 16]]), cb(0), ALU.mult),
                 reads=[("Bsrc", 1), "cc0"], writes=["Bb1"])
            k.op("dve", lambda e: e.tensor_tensor(tb.v(0, [[16, 128], [1, 16]]), Bsrc[0].v(0, [[16, 128], [1, 16]]), cb(1), ALU.mult),
                 reads=[("Bsrc", 0), "cc1"], writes=["tb"])
            k.op("dve", lambda e: e.tensor_tensor(Bb[1][:], Bb[1][:], tb[:], ALU.add), reads=["Bb1", "tb"], writes=["Bb1"])
            for i in range(2):
                k.dma("sp", DA(BBD, i * 64 * BN, [[BN, 64], [1, BN]]), Bb[i][:], reads=["Bb%d" % i], writes=["BBD"])
            cn = k.sb("cn", [128, 16, 64], F32)
            CT_ = k.sb("CT_", [64, BN], F32)
            for i, nm in enumerate(["s5_c_re", "s5_c_im"]):
                k.dma("sp", cn[:], DA(I[nm], 0, [[64, 128], [8192, 16], [1, 64]]), writes=["cn"])
                for a in range(16):
                    ps, pk = k.bank()
                    k.op("pe", lambda e, ps=ps, a=a: e.matmul(ps[0:64, 0:128], cn[:, a, :], ident_f[:], start=True, stop=True),
                         reads=["cn", "ident_f"], writes=[pk])
                    k.op("dve", lambda e, ps=ps, a=a: e.tensor_copy(CT_[:, a * 128:(a + 1) * 128], ps[0:64, 0:128]),
                         reads=[pk], writes=["CT_"])
                k.dma("sp", DA(CD, i * 64 * BN, [[BN, 64], [1, BN]]), CT_[:], reads=["CT_"], writes=["CD"])
            dump("PWD", DA(PWD, 0, [[3200, 128], [1, 3200]]), [128, 3200], F32)
            dump("BBD", DA(BBD, 0, [[2048, 128], [1, 2048]]), [128, 2048], F32)
            dump("CD", DA(CD, 0, [[2048, 128], [1, 2048]]), [128, 2048], F32)
            k.release()

        def s5_branch(l):
            k.mark()
            PW = k.sb("PW", [64, 2, 32 * 25], F32)
            BB = k.sb("BB", [64, 2, 32 * 16], F32)
            CC = k.sb("CC", [64, 2, 32 * 16], F32)
            for i in range(2):
                k.dma("sp", PW[:, i, :], DA(PWD, i * 64 * 3200 + l * 800, [[3200, 64], [1, 800]]), reads=["PWD"], writes=["PW"])
                k.dma("sp", BB[:, i, :], DA(BBD, i * 64 * 2048 + l * 512, [[2048, 64], [1, 512]]), reads=["BBD"], writes=["BB"])
                k.dma("sp", CC[:, i, :], DA(CD, i * 64 * 2048 + l * 512, [[2048, 64], [1, 512]]), reads=["CD"], writes=["CC"])
            tri = k.sb("tri", [128, 2, 128], F32)
            k.dma("sp", tri.v(0, [[128, 2], [1, 128]]), DA(I["k_trimask"], 0, [[128, 128], [128 * 128, 2], [1, 128]]), writes=["tri"])
            KIN = [k.sb("KIN%d" % d, [128, 16, 128], BF16) for d in range(2)]
            BM = [k.sb("BM%d" % d, [128, 16, 2, 64], BF16) for d in range(2)]
            CM = [k.sb("CM%d" % d, [64, 16, 2, 128], BF16) for d in range(2)]
            A8x = k.sb("A8x", [64, 2, 2, 16], F32)
            A8y = k.sb("A8y", [64, 2, 2, 16], F32)
            DSK = k.sb("DSK", [128, 2], F32)
            k.dma("sp", DSK.v(0, [[1, 2], [1, 1]]), DA(I["s5_d"], l * 256, [[1, 128], [128, 2], [1, 1]]), writes=["DSK"], allow_slow_non_contiguous=True)
            GB = k.sb("GB", [128, 2], F32)
            k.dma("sp", GB.v(0, [[1, 2], [1, 1]]), DA(I["s5_glu_b"], l * 256, [[1, 128], [128, 2], [1, 1]]), writes=["GB"], allow_slow_non_contiguous=True)
            GW = k.sb("GW", [128, 2, 256], BF16)
            wload(GW[:], I["s5_glu_w"], l * 65536, 256, 2, 256, "GW")
            uTp = k.sb("uTp", [128, 2, 8, 288], BF16)

            def ev_u(ci, tok0, ntok, ps, pk):
                nkb, k0 = ntok // 8, tok0 // 8
                copy_evac(uTp.v(ci * 8 * 288 + k0, [[1, nkb], [288, 8]]), ps.v(0, [[8, nkb], [1, 8]]), [pk], ["uTp"])
            proj_fm(l, 0, 256, ev_u)
            for cc in range(2):
                k.dma("sp", DA(UDP, cc * 128 * NTOK, [[NTOK, 128], [1, NTOK]]), uTp.v(cc * 2304, [[1, 2304]]), reads=both("uTp"), writes=["UDP"])
            U = k.sb("U", [128, 16, 288], BF16)
            for r in range(8):
                k.dma("sp", U.v(0, [[288, 16], [1, 288]], p0=r * 16, np_=16),
                      DA(UDP, r * 288, [[NTOK, 16], [16 * NTOK, 16], [1, 288]]), reads=["UDP"], writes=["U"])
            k.mark()
            Lr = k.sb("Lr", [64, 16, 128], F32); Li = k.sb("Li", [64, 16, 128], F32)
            Pr = k.sb("Pr", [64, 16, 128], F32); Pi = k.sb("Pi", [64, 16, 128], F32)
            Rr = k.sb("Rr", [64, 16, 128], F32); Ri = k.sb("Ri", [64, 16, 128], F32)
            ta = k.sb("ta", [64, 16, 128], F32); tb2 = k.sb("tb2", [64, 16, 128], F32)
            for d in range(2):
                def cmul(Xr, Xi, xk, nbase, S, neg_im, eng, d=d):
                    pw0 = PW.v(0 * 800 + d * 400 + nbase, [[25, 16], [1, 8], [0, 16]])
                    pw1 = PW.v(1 * 800 + d * 400 + nbase, [[25, 16], [1, 8], [0, 16]])
                    sv0 = S.v(0 * 512 + d * 256, [[16, 16], [0, 8], [1, 16]])
                    sv1 = S.v(1 * 512 + d * 256, [[16, 16], [0, 8], [1, 16]])
                    sk = "BB" if S is BB else "CC"
                    o4 = [[128, 16], [16, 8], [1, 16]]
                    k.op(eng, lambda e: e.tensor_tensor(Xr.v(0, o4), pw0, sv0, ALU.mult), reads=["PW", sk], writes=[xk + "r"])
                    k.op(eng, lambda e: e.tensor_tensor(ta.v(0, o4), pw1, sv1, ALU.mult), reads=["PW", sk], writes=["ta"])
                    k.op(eng, lambda e: e.tensor_tensor(Xr[:], Xr[:], ta[:], ALU.subtract), reads=[xk + "r", "ta"], writes=[xk + "r"])
                    k.op(eng, lambda e: e.tensor_tensor(Xi.v(0, o4), pw0, sv1, ALU.mult), reads=["PW", sk], writes=[xk + "i"])
                    k.op(eng, lambda e: e.tensor_tensor(tb2.v(0, o4), pw1, sv0, ALU.mult), reads=["PW", sk], writes=["tb2"])
                    if neg_im:
                        k.op(eng, lambda e: e.scalar_tensor_tensor(Xi[:], Xi[:], -1.0, tb2[:], ALU.mult, ALU.subtract),
                             reads=[xk + "i", "tb2"], writes=[xk + "i"])
                    else:
                        k.op(eng, lambda e: e.tensor_tensor(Xi[:], Xi[:], tb2[:], ALU.add), reads=[xk + "i", "tb2"], writes=[xk + "i"])
                cmul(Lr, Li, "L", 0, BB, False, "dve")
                cmul(Pr, Pi, "P", 8, BB, False, "dve")
                cmul(Rr, Ri, "R", 16, CC, True, "dve")
                for g in range(16):
                    ps, pk = k.bank()
                    k.op("pe", lambda e, ps=ps, g=g: e.matmul(ps[:, 0:128], Pr[:, g, :], Rr[:, g, :], start=True, stop=False),
                         reads=["Pr", "Rr"], writes=[pk])
                    k.op("pe", lambda e, ps=ps, g=g: e.matmul(ps[:, 0:128], Pi[:, g, :], Ri[:, g, :], start=False, stop=True),
                         reads=["Pi", "Ri"], writes=[pk])
                    k.op("dve", lambda e, ps=ps, g=g, d=d: e.tensor_tensor(KIN[d][:, g, :], ps[:, 0:128], tri[:, d, :], ALU.mult),
                         reads=[pk, "tri"], writes=[("KIN", d)])
                for g0 in range(0, 16, 4):
                    ps, pk = k.bank()
                    for gg in range(4):
                        for c, Lc, lk in ((0, Lr, "Lr"), (1, Li, "Li")):
                            s = gg * 2 + c
                            k.op("pe", lambda e, ps=ps, s=s, Lc=Lc, g=g0 + gg: e.matmul(
                                ps[:, s * 64:(s + 1) * 64], Lc[:, g, :], ident_f[0:64, 0:64], start=True, stop=True),
                                reads=[lk, "ident_f"], writes=[pk])
                    copy_evac(BM[d].v(g0 * 128, [[1, 512]]), ps[:, 0:512], [pk], [("BM", d)])
                k.op("act", lambda e, d=d: e.activation(CM[d].v(0, [[256, 16], [1, 128]]), Rr[:], AF.Copy), reads=["Rr"], writes=[("CM", d)])
                k.op("act", lambda e, d=d: e.activation(CM[d].v(128, [[256, 16], [1, 128]]), Ri[:], AF.Copy), reads=["Ri"], writes=[("CM", d)])
                i8 = 23 if d == 0 else 16
                for c in range(2):
                    k.op("dve", lambda e, d=d, c=c, i8=i8: e.tensor_copy(A8x.v((d * 2 + c) * 16, [[1, 16]]), PW.v(d * 400 + i8, [[25, 16]])),
                         reads=["PW"], writes=["A8"])
                k.op("dve", lambda e, d=d, i8=i8: e.tensor_copy(A8y.v((d * 2 + 1) * 16, [[1, 16]]), PW.v(800 + d * 400 + i8, [[25, 16]])),
                     reads=["PW"], writes=["A8"])
                k.op("dve", lambda e, d=d, i8=i8: e.tensor_scalar(A8y.v((d * 2) * 16, [[1, 16]]), PW.v(800 + d * 400 + i8, [[25, 16]]), -1.0, None, ALU.mult),
                     reads=["PW"], writes=["A8"])
            for d in range(2):
                dump("KIN%d" % d, KIN[d].v(0, [[1, 2048]]), [128, 2048], BF16)
                dump("BM%d" % d, BM[d].v(0, [[1, 2048]]), [128, 2048], BF16)
                dump("CM%d" % d, CM[d].v(0, [[1, 4096]]), [64, 4096], BF16)
            dump("A8x", A8x.v(0, [[1, 64]]), [64, 64], F32)
            dump("A8y", A8y.v(0, [[1, 64]]), [64, 64], F32)
            dump("Uu", U.v(0, [[1, 16 * 288]]), [128, 16 * 288], BF16)
            k.release()
            XE = k.sb("XE", [64, 2, 16, 288], F32)
            EB = k.sb("EB", [64, 2, 16, 288], BF16)
            Ysb = k.sb("Ysb", [128, 16, 288], F32)
            P_ = k.sb("P_", [64, 2, 16], F32)
            Q_ = k.sb("Q_", [64, 2, 16], F32)
            GK = 16 * 288
            for d in range(2):
                for g in range(16):
                    for c in range(2):
                        ps, pk = k.bank()
                        k.op("pe", lambda e, ps=ps, d=d, g=g, c=c: e.matmul(ps[0:64, 0:288], BM[d][:, g, c, :], U[:, g, :], start=True, stop=True),
                             reads=both(("BM", d)) + ["U"], writes=[pk])
                        copy_evac(XE[:, c, g, :], ps[0:64, 0:288], [pk], ["XE"])
                if d == 0:
                    order = list(range(288))
                else:
                    order = list(range(31, -1, -1)) + list(range(287, 31, -1))
                eng = "dve"
                for si in range(1, 288):
                    kk, kp = order[si], order[si - 1]
                    k.op(eng, lambda e, kp=kp, d=d: e.tensor_tensor(
                        P_[:], XE.v(kp, [[GK, 2], [288, 16]]), A8x.v(d * 32, [[16, 2], [1, 16]]), ALU.mult),
                        reads=both("XE") + ["XE", "A8"], writes=["P_"])
                    k.op(eng, lambda e, kp=kp, d=d: e.tensor_tensor(
                        Q_[:], XE.v(GK + kp, [[-GK, 2], [288, 16]]), A8y.v(d * 32, [[16, 2], [1, 16]]), ALU.mult),
                        reads=both("XE") + ["XE", "A8"], writes=["Q_"])
                    k.op(eng, lambda e: e.tensor_tensor(P_[:], P_[:], Q_[:], ALU.add), reads=["P_", "Q_"], writes=["P_"])
                    k.op(eng, lambda e, kk=kk: e.tensor_tensor(XE.v(kk, [[GK, 2], [288, 16]]), XE.v(kk, [[GK, 2], [288, 16]]), P_[:], ALU.add),
                         reads=both("XE") + ["XE", "P_"], writes=["XE"])
                k.op("act", lambda e: e.activation(EB[:], XE[:], AF.Copy), reads=both("XE") + ["XE"], writes=["EB"])
                for g in range(16):
                    ps, pk = k.bank()
                    k.op("pe", lambda e, ps=ps, d=d, g=g: e.matmul(ps[:, 0:288], KIN[d][:, g, :], U[:, g, :], start=True, stop=False),
                         reads=[("KIN", d), "U"], writes=[pk])
                    if d == 0:
                        rngs = [(1, 0, 287)]
                    else:
                        rngs = [(0, 1, 31), (32, 33, 255), (287, 0, 1)]
                    nmm = len(rngs) * 2
                    im = 0
                    for (o0, s0, n) in rngs:
                        for c in range(2):
                            im += 1
                            k.op("pe", lambda e, ps=ps, d=d, g=g, c=c, o0=o0, s0=s0, n=n, last=(im == nmm): e.matmul(
                                ps[:, o0:o0 + n], CM[d][:, g, c, :], EB[:, c, g, s0:s0 + n], start=False, stop=last),
                                reads=[("CM", d), "EB"], writes=[pk])
                    if d == 0:
                        copy_evac(Ysb[:, g, :], ps[:, 0:288], [pk], ["Ysb"])
                    else:
                        k.op("dve", lambda e, ps=ps, g=g: e.tensor_tensor(Ysb[:, g, :], ps[:, 0:288], Ysb[:, g, :], ALU.add),
                             reads=[pk] + both("Ysb"), writes=["Ysb"])
            for j in range(8):
                k.dma("sp", DA(YDP, j * 288, [[NTOK, 16], [16 * NTOK, 16], [1, 288]]),
                      Ysb.v(0, [[288, 16], [1, 288]], p0=j * 16, np_=16), reads=both("Ysb") + ["Ysb"], writes=["YDP"])
            dump("YDP%d" % l, DA(YDP, 0, [[NTOK, 256], [1, NTOK]]), [256, NTOK], F32)
            k.release()
            k.mark()
            uTp2 = k.sb("uTp2", [128, 2, 8, 288], BF16)
            for cc in range(2):
                k.dma("sp", uTp2.v(cc * 2304, [[1, 2304]]), DA(UDP, cc * 128 * NTOK, [[NTOK, 128], [1, NTOK]]), reads=["UDP"], writes=["uTp2"])
            yTp = k.sb("yTp", [128, 2, 8, 288], F32)
            for cc in range(2):
                k.dma("sp", yTp.v(cc * 2304, [[1, 2304]]), DA(YDP, cc * 128 * NTOK, [[NTOK, 128], [1, NTOK]]), reads=["YDP"], writes=["yTp"])
            DSK = k.sb("DSK", [128, 2], F32)
            k.dma("sp", DSK.v(0, [[1, 2], [1, 1]]), DA(I["s5_d"], l * 256, [[1, 128], [128, 2], [1, 1]]), writes=["DSK"], allow_slow_non_contiguous=True)
            GB = k.sb("GB", [128, 2], F32)
            k.dma("sp", GB.v(0, [[1, 2], [1, 1]]), DA(I["s5_glu_b"], l * 256, [[1, 128], [128, 2], [1, 1]]), writes=["GB"], allow_slow_non_contiguous=True)
            GW = k.sb("GW", [128, 2, 256], BF16)
            wload(GW[:], I["s5_glu_w"], l * 65536, 256, 2, 256, "GW")
            yl = k.sb("yl", [128, NTOK], F32)
            tt = k.sb("tt", [128, NTOK], F32)
            gT = k.sb("gT", [128, 2, NTOK], BF16)
            YA = k.sb("YA", [128, 2, NTOK], BF16)
            for cc in range(2):
                k.op("dve", lambda e, cc=cc: e.scalar_tensor_tensor(
                    yl.v(0, [[8, 288], [1, 8]]), uTp2.v(cc * 2304, [[1, 288], [288, 8]]), DSK[:, cc:cc + 1],
                    yTp.v(cc * 2304, [[1, 288], [288, 8]]), ALU.mult, ALU.add),
                    reads=["uTp2", "DSK", "yTp"], writes=["yl"])
                k.op("pool", lambda e: e.tensor_tensor(tt[:], yl[:], yl[:], ALU.mult), reads=["yl"], writes=["tt"])
                k.op("dve", lambda e: e.tensor_scalar(tt[:], tt[:], 0.044715, 1.0, ALU.mult, ALU.add), reads=["tt"], writes=["tt"])
                k.op("pool", lambda e: e.tensor_tensor(tt[:], tt[:], yl[:], ALU.mult), reads=["tt", "yl"], writes=["tt"])
                k.op("act", lambda e: e.activation(tt[:], tt[:], AF.Sigmoid, scale=1.5957691216057308), reads=["tt"], writes=["tt"])
                k.op("dve", lambda e, cc=cc: e.tensor_tensor(gT[:, cc, :], yl[:], tt[:], ALU.mult), reads=["yl", "tt"], writes=["gT"])
            sg = [k.sb("sg%d" % i, [128, 512], F32) for i in range(2)]
            n_ = 0
            for co in range(2):
                for (tok0, ntok) in MT:
                    ps, pk = k.bank()
                    for cc in range(2):
                        k.op("pe", lambda e, ps=ps, cc=cc, co=co, tok0=tok0, ntok=ntok: e.matmul(
                            ps[:, 0:ntok], GW[:, cc, co * 128:(co + 1) * 128], gT[:, cc, tok0:tok0 + ntok],
                            start=(cc == 0), stop=(cc == 1)), reads=["GW", "gT"], writes=[pk])
                    s_ = sg[n_ % 2]; sk = ("sg", n_ % 2); n_ += 1
                    k.op("act", lambda e, ps=ps, s_=s_, co=co, ntok=ntok: e.activation(
                        s_[:, 0:ntok], ps[:, 0:ntok], AF.Sigmoid, bias=GB[:, co:co + 1]), reads=[pk, "GB"], writes=[sk])
                    k.op("dve", lambda e, s_=s_, co=co, tok0=tok0, ntok=ntok: e.tensor_tensor(
                        YA[:, co, tok0:tok0 + ntok], gT[:, co, tok0:tok0 + ntok], s_[:, 0:ntok], ALU.mult),
                        reads=[sk, "gT"], writes=["YA"])
            for cc in range(2):
                k.dma("sp", DA(YT, (0 * 256 + cc * 128) * NTOK, [[NTOK, 128], [1, NTOK]]), YA[:, cc, :], reads=["YA"], writes=[("YT", 0)])
            k.release()

        def conv_branch(l):
            k.mark()
            S3 = [k.sb("cv%d" % i, [128, 2, NTOK], BF16) for i in range(3)]

            def ev(ci, tok0, ntok, ps, pk):
                s, cc = ci // 2, ci % 2
                copy_evac(S3[s][:, cc, tok0:tok0 + ntok], ps[:, 0:ntok], [pk], [("cv", s)])
            proj_fm(l, 1024, 768, ev)
            CW = k.sb("CW", [128, 2, 3], F32)
            for cc in range(2):
                k.dma("sp", CW.v(cc * 3, [[1, 3], [1, 1]]), DA(I["conv_w"], l * 768 + cc * 128, [[1, 128], [256, 3], [1, 1]]),
                      writes=["CW"], allow_slow_non_contiguous=True)
            zz = k.sb("zz", [128, NTOK], F32)
            yy = k.sb("yy", [128, NTOK], F32)
            YC = k.sb("YC", [128, 2, NTOK], BF16)
            for cc in range(2):
                k.op("pool", lambda e, cc=cc: e.tensor_tensor(zz[:], S3[1][:, cc, :], S3[2][:, cc, :], ALU.mult),
                     reads=both(("cv", 1)) + both(("cv", 2)), writes=["zz"])
                k.op("dve", lambda e, cc=cc: e.tensor_scalar(yy[:], zz[:], CW[:, cc, 1:2], None, ALU.mult), reads=["zz", "CW"], writes=["yy"])
                for (a, b) in ((0, 256), (256, NTOK)):
                    k.op("dve", lambda e, cc=cc, a=a, b=b: e.scalar_tensor_tensor(
                        yy[:, a + 1:b], zz[:, a:b - 1], CW[:, cc, 0:1], yy[:, a + 1:b], ALU.mult, ALU.add),
                        reads=["zz", "CW", "yy"], writes=["yy"])
                    k.op("dve", lambda e, cc=cc, a=a, b=b: e.scalar_tensor_tensor(
                        yy[:, a:b - 1], zz[:, a + 1:b], CW[:, cc, 2:3], yy[:, a:b - 1], ALU.mult, ALU.add),
                        reads=["zz", "CW", "yy"], writes=["yy"])
                k.op("pool", lambda e, cc=cc: e.tensor_tensor(YC[:, cc, :], S3[0][:, cc, :], yy[:], ALU.mult),
                     reads=both(("cv", 0)) + ["yy"], writes=["YC"])
            for cc in range(2):
                k.dma("sp", DA(YT, (2 * 256 + cc * 128) * NTOK, [[NTOK, 128], [1, NTOK]]), YC[:, cc, :], reads=["YC"], writes=[("YT", 2)])
            k.release()

        def sgu_branch(l):
            k.mark()
            uT = k.sb("sguT", [128, 2, NTOK], BF16)

            def ev(ci, tok0, ntok, ps, pk):
                copy_evac(uT[:, ci, tok0:tok0 + ntok], ps[:, 0:ntok], [pk], ["sguT"])
            proj_fm(l, 1792, 256, ev)
            WV = k.sb("WV", [128, 8, 256], BF16)
            wload(WV[:], I["w_in"], l * 1024 * 6400 + 2048, 6400, 8, 256, "WV")
            LG = k.sb("LG", [128, 256], F32); LB = k.sb("LB", [128, 256], F32)
            k.dma("sp", LG[:], DA(I["sgu_ln_g"], l * 256, [[0, 128], [1, 256]]), writes=["LG"])
            k.dma("sp", LB[:], DA(I["sgu_ln_b"], l * 256, [[0, 128], [1, 256]]), writes=["LB"])
            SGB = k.sb("SGB", [128, 4, 128], F32)
            k.dma("sp", SGB[:], DA(I["sgu_b"], l * 512, [[0, 128], [1, 512]]), writes=["SGB"])
            wsn = k.sb("wsn", [128, 4, 128], F32)
            k.dma("sp", wsn[:], DA(I["sgu_w"], l * 4 * 16384, [[128, 128], [16384, 4], [1, 128]]), writes=["wsn"])
            WST = k.sb("WST", [128, 4, 128], BF16)
            ps, pk = k.bank()
            for g in range(4):
                k.op("pe", lambda e, ps=ps, g=g: e.matmul(ps[:, g * 128:(g + 1) * 128], wsn[:, g, :], ident_f[:], start=True, stop=True),
                     reads=["wsn", "ident_f"], writes=[pk])
            k.op("dve", lambda e, ps=ps: e.tensor_copy(WST[:], ps[:, 0:512]), reads=[pk], writes=["WST"])
            VN = k.sb("VN", [128, NTILE, 256], BF16)
            st_ = k.sb("st_", [128, NTILE, 4], F32)
            junk = k.sb("junk2", [128, 256], F32)
            vt = [k.sb("vt%d" % i, [128, 256], F32) for i in range(2)]
            k.op("dve", lambda e: e.memset(st_[:], 0.0), writes=["st_"])
            for t in range(NTILE):
                ps, pk = k.bank()
                for kc in range(8):
                    k.op("pe", lambda e, ps=ps, kc=kc, t=t: e.matmul(ps[:, 0:256], alT[:, kc, t * 128:(t + 1) * 128], WV[:, kc, :],
                                                                    start=(kc == 0), stop=(kc == 7)),
                         reads=[("alT", t), "WV"], writes=[pk])
                v_ = vt[t % 2]; vk = ("vt", t % 2)
                k.op("act", lambda e, ps=ps, v_=v_, t=t: e.activation(v_[:], ps[:, 0:256], AF.Copy, accum_out=st_[:, t, 0:1]),
                     reads=[pk, "st_"], writes=[vk, "st_"])
                k.op("act", lambda e, v_=v_, t=t: e.activation(junk[:], v_[:], AF.Square, accum_out=st_[:, t, 1:2]),
                     reads=[vk, "st_"], writes=["junk2", "st_"])
                k.op("dve", lambda e, t=t: e.tensor_scalar(st_[:, t, 0:2], st_[:, t, 0:2], 1.0 / 256, None, ALU.mult), reads=["st_"], writes=["st_"])
                k.op("dve", lambda e, t=t: e.tensor_tensor(st_[:, t, 2:3], st_[:, t, 0:1], st_[:, t, 0:1], ALU.mult), reads=["st_"], writes=["st_"])
                k.op("dve", lambda e, t=t: e.tensor_tensor(st_[:, t, 2:3], st_[:, t, 1:2], st_[:, t, 2:3], ALU.subtract), reads=["st_"], writes=["st_"])
                k.op("dve", lambda e, t=t: e.tensor_scalar(st_[:, t, 2:3], st_[:, t, 2:3], 1e-6, None, ALU.add), reads=["st_"], writes=["st_"])
                k.op("act", lambda e, t=t: e.activation(st_[:, t, 2:3], st_[:, t, 2:3], AF.Sqrt), reads=["st_"], writes=["st_"])
                k.op("dve", lambda e, t=t: e.reciprocal(st_[:, t, 2:3], st_[:, t, 2:3]), reads=["st_"], writes=["st_"])
                k.op("dve", lambda e, v_=v_, t=t: e.tensor_scalar(v_[:], v_[:], st_[:, t, 0:1], st_[:, t, 2:3], ALU.subtract, ALU.mult),
                     reads=[vk, "st_"], writes=[vk])
                k.op("pool", lambda e, v_=v_: e.tensor_tensor(v_[:], v_[:], LG[:], ALU.mult), reads=[vk, "LG"], writes=[vk])
                k.op("pool", lambda e, v_=v_, t=t: e.tensor_tensor(VN[:, t, :], v_[:], LB[:], ALU.add), reads=[vk, "LB"], writes=[("VN", t)])
            YD = k.sb("YD", [128, 2, NTOK], BF16)
            zt = [k.sb("zt%d" % i, [128, 128], F32) for i in range(2)]
            n_ = 0
            for t in range(NTILE):
                for cc in range(2):
                    ps, pk = k.bank()
                    for gl in range(2):
                        g = 2 * cc + gl
                        k.op("pe", lambda e, ps=ps, gl=gl, g=g, t=t, cc=cc: e.matmul(
                            ps[:, gl * 128:(gl + 1) * 128], VN[:, t, cc * 128:(cc + 1) * 128], WST[:, g, :], start=True, stop=True),
                            reads=[("VN", t), "WST"], writes=[pk])
                    z_ = zt[n_ % 2]; zk = ("zt", n_ % 2); n_ += 1
                    for gl in range(2):
                        g = 2 * cc + gl
                        p0 = 64 * gl
                        k.op("dve", lambda e, ps=ps, z_=z_, gl=gl, g=g, p0=p0: e.tensor_tensor(
                            z_[p0:p0 + 64, :], ps[p0:p0 + 64, gl * 128:(gl + 1) * 128], SGB[p0:p0 + 64, g, :], ALU.add),
                            reads=[pk, "SGB"], writes=[zk])
                    k.op("pool", lambda e, z_=z_, t=t, cc=cc: e.tensor_tensor(
                        YD[:, cc, t * 128:(t + 1) * 128], z_[:], uT[:, cc, t * 128:(t + 1) * 128], ALU.mult),
                        reads=[zk] + both("sguT"), writes=["YD"])
            for cc in range(2):
                k.dma("sp", DA(YT, (3 * 256 + cc * 128) * NTOK, [[NTOK, 128], [1, NTOK]]), YD[:, cc, :], reads=["YD"], writes=[("YT", 3)])
            k.release()

        def rp_setup():
            k.mark()
            negt = k.sb("negt", [121, 160], F32)
            k.op("dve", lambda e: e.memset(negt[:], NEG), writes=["negt"])
            for hlf in range(2):
                k.dma("sp", DA(RP, hlf * 120 * 160, [[160, 121], [1, 160]]), negt[:], reads=["negt"], writes=["RP"])
            k.dma("sp", DA(RP, 64, [[160, 240], [1, 31]]), DA(I["na_rpb"], 0, [[31, 240], [1, 31]]), reads=["RP"], writes=["RP"])
            k.release()

        def attn_branch(l):
            k.mark()
            qT = k.sb("qT", [128, 2, NTOK], BF16)
            kT = k.sb("kT", [128, 2, NTOK], BF16)

            def ev(ci, tok0, ntok, ps, pk):
                if ci < 2:
                    copy_evac(qT[:, ci, tok0:tok0 + ntok], ps[:, 0:ntok], [pk], ["qT"], scale=0.125)
                else:
                    copy_evac(kT[:, ci - 2, tok0:tok0 + ntok], ps[:, 0:ntok], [pk], ["kT"])
            proj_fm(l, 256, 512, ev)
            if CUT == 21:
                k.release(); return
            WV = k.sb("WVa", [128, 8, 256], BF16)
            wload(WV[:], I["w_in"], l * 1024 * 6400 + 768, 6400, 8, 256, "WVa")
            NVT = 18 + 15
            VP = k.sb("VP", [128, NVT, 4, 128], BF16)
            k.op("pool", lambda e: e.memset(VP[:], 0.0), writes=both("VP"))
            if CUT == 25:
                k.release(); return
            starts = [t * 128 for t in range(18)] + [320 + 128 * m for m in range(15)]
            if CUT == 26:
                starts = starts[:18]
            if CUT == 27:
                starts = starts[:1]
            for vi, s0 in enumerate(starts):
                ps, pk = k.bank()
                for kc in range(8):
                    k.op("pe", lambda e, ps=ps, kc=kc, s0=s0: e.matmul(ps[:, 0:256], alT[:, kc, s0:s0 + 128], WV[:, kc, :],
                                                                      start=(kc == 0), stop=(kc == 7)),
                         reads=tkeys("alT", s0, 128) + ["WVa"], writes=[pk])
                copy_evac(VP.v(vi * 512, [[256, 2], [1, 64]]), ps.v(0, [[128, 2], [1, 64]]), [pk], ["VP"])
                copy_evac(VP.v(vi * 512 + 128 + 64, [[256, 2], [1, 64]]), ps.v(64, [[128, 2], [1, 64]]), [pk], ["VP"])
            if CUT == 22:
                k.release(); return
            ONP = k.sb("ONP", [128, 2, 128], BF16)
            k.op("pool", lambda e: e.memset(ONP[:], 0.0), writes=["ONP"])
            k.op("pool", lambda e: e.memset(ONP[:, 0, 0:64], 1.0), writes=["ONP"])
            k.op("pool", lambda e: e.memset(ONP[:, 1, 64:128], 1.0), writes=["ONP"])
            Wt = k.sb("Wt", [128, 60, 64], F32)
            k.dma("sp", Wt.v(0, [[64, 60], [1, 64]], p0=0, np_=64), DA(RP, l * 60 * 160 + 16, [[1, 64], [160, 60], [1, 64]]), reads=["RP"], writes=["Wt"])
            k.dma("sp", Wt.v(0, [[64, 60], [1, 64]], p0=64, np_=64), DA(RP, l * 60 * 160 + 160 + 16, [[1, 64], [160, 60], [1, 64]]), reads=["RP"], writes=["Wt"])
            cmk = k.sb("cmk", [128, 64], F32)
            k.dma("sp", cmk[:], I["k_colmask"].ap(), writes=["cmk"])
            RPT = k.sb("RPT", [128, 60, 64], F32)
            k.op("dve", lambda e: e.tensor_tensor(RPT.v(0, [[64, 60], [1, 64]]), Wt.v(63, [[64, 60], [-1, 64]]), cmk.v(0, [[0, 60], [1, 64]]), ALU.add),
                 reads=["Wt", "cmk"], writes=["RPT"])
            if CUT == 23:
                k.release(); return
            YB = k.sb("YB", [128, 2, NTOK], BF16)
            tmpS = [k.sb("tmpS%d" % i, [128, 4, 64], F32) for i in range(2)]
            Pm = [k.sb("Pm%d" % i, [128, 6, 64], BF16) for i in range(2)]
            rec = [k.sb("rec%d" % i, [128, 2, 64], F32) for i in range(2)]
            n_ = 0
            for r in range(32):
                q0 = 256 + 64 * r
                rs_ = min(max(r - 4, 0), 24)
                dr0 = rs_ - r + 7
                psO, pkO = k.bank()
                psD, pkD = k.bank()
                for h in range(4):
                    cc, ph = h // 2, 64 * (h % 2)
                    psS, pkS = k.bank()
                    ktoks = [256 + 64 * (rs_ + 2 * c) for c in range(4)] + [0, 128]
                    for c in range(6):
                        k.op("pe", lambda e, psS=psS, c=c, cc=cc, ph=ph, kt=ktoks[c], q0=q0: e.matmul(
                            psS[:, c * 64:(c + 1) * 64], kT[ph:ph + 64, cc, kt:kt + 128], qT[ph:ph + 64, cc, q0:q0 + 64],
                            start=True, stop=True), reads=both("kT") + both("qT"), writes=[pkS])
                    b = n_ % 2; n_ += 1
                    k.op("dve", lambda e, psS=psS, b=b, h=h, dr0=dr0: e.tensor_tensor(
                        tmpS[b].v(0, [[64, 4], [1, 64]]), psS.v(0, [[64, 4], [1, 64]]),
                        RPT.v((h * 15 + dr0) * 64, [[128, 4], [1, 64]]), ALU.add),
                        reads=[pkS, "RPT"], writes=[("tmpS", b)])
                    k.op("act", lambda e, b=b: e.activation(Pm[b].v(0, [[1, 256]]), tmpS[b].v(0, [[1, 256]]), AF.Exp),
                         reads=[("tmpS", b)], writes=[("Pm", b)])
                    k.op("act", lambda e, b=b, psS=psS: e.activation(Pm[b].v(256, [[1, 128]]), psS[:, 256:384], AF.Exp),
                         reads=[pkS], writes=[("Pm", b)])
                    for c in range(6):
                        kt = ktoks[c]
                        if kt < 256:
                            vi = kt // 128
                        elif (kt - 256) % 128 == 0:
                            vi = kt // 128
                        else:
                            vi = 18 + (kt - 320) // 128
                        first = (h % 2 == 0 and c == 0)
                        last = (h % 2 == 1 and c == 5)
                        k.op("pe", lambda e, psO=psO, b=b, c=c, vi=vi, h=h, cc=cc, first=first, last=last: e.matmul(
                            psO[:, cc * 64:cc * 64 + 64], VP[:, vi, h, :], Pm[b][:, c, :], start=first, stop=last),
                            reads=both("VP") + [("Pm", b)], writes=[pkO])
                        k.op("pe", lambda e, psD=psD, b=b, c=c, h=h, cc=cc, first=first, last=last: e.matmul(
                            psD[:, cc * 64:cc * 64 + 64], ONP[:, h % 2, :], Pm[b][:, c, :], start=first, stop=last),
                            reads=["ONP", ("Pm", b)], writes=[pkD])
                rb = r % 2
                k.op("dve", lambda e, psD=psD, rb=rb: e.reciprocal(rec[rb].v(0, [[1, 128]]), psD[:, 0:128]),
                     reads=[pkD], writes=[("rec", rb)])
                k.op("dve", lambda e, psO=psO, rb=rb, q0=q0: e.tensor_tensor(
                    YB.v(q0, [[NTOK, 2], [1, 64]]), psO.v(0, [[64, 2], [1, 64]]), rec[rb].v(0, [[64, 2], [1, 64]]), ALU.mult),
                    reads=[pkO, ("rec", rb)], writes=["YB"])
            if CUT == 24:
                k.release(); return
            Pc = [k.sb("Pc%d" % i, [128, 2, 256], BF16) for i in range(2)]
            recc = k.sb("recc", [128, 256], F32)
            for cc in range(2):
                psO, pkO = k.bank()
                psD, pkD = k.bank()
                for hh in range(2):
                    h = cc * 2 + hh
                    ph = 64 * hh
                    psS, pkS = k.bank()
                    for c in range(2):
                        k.op("pe", lambda e, psS=psS, c=c, cc=cc, ph=ph: e.matmul(
                            psS[:, c * 256:(c + 1) * 256], kT[ph:ph + 64, cc, c * 128:(c + 1) * 128], qT[ph:ph + 64, cc, 0:256],
                            start=True, stop=True), reads=both("kT") + both("qT"), writes=[pkS])
                    k.op("act", lambda e, hh=hh, psS=psS: e.activation(Pc[hh].v(0, [[1, 512]]), psS[:, 0:512], AF.Exp),
                         reads=[pkS], writes=[("Pc", hh)])
                    for c in range(2):
                        first = (hh == 0 and c == 0)
                        last = (hh == 1 and c == 1)
                        k.op("pe", lambda e, psO=psO, hh=hh, c=c, h=h, first=first, last=last: e.matmul(
                            psO[:, 0:256], VP[:, c, h, :], Pc[hh][:, c, :], start=first, stop=last),
                            reads=both("VP") + [("Pc", hh)], writes=[pkO])
                        k.op("pe", lambda e, psD=psD, hh=hh, c=c, first=first, last=last: e.matmul(
                            psD[:, 0:256], ONP[:, hh, :], Pc[hh][:, c, :], start=first, stop=last),
                            reads=["ONP", ("Pc", hh)], writes=[pkD])
                k.op("dve", lambda e, psD=psD: e.reciprocal(recc[:], psD[:, 0:256]), reads=[pkD], writes=["recc"])
                k.op("dve", lambda e, psO=psO, cc=cc: e.tensor_tensor(YB[:, cc, 0:256], psO[:, 0:256], recc[:], ALU.mult),
                     reads=[pkO, "recc"], writes=["YB"])
            for cc in range(2):
                k.dma("sp", DA(YT, (1 * 256 + cc * 128) * NTOK, [[NTOK, 128], [1, NTOK]]), YB[:, cc, :], reads=["YB"], writes=[("YT", 1)])
            k.release()

        def merge_phase(l):
            k.mark()
            YS = k.sb("YS", [128, 8, NTOK], BF16)
            for br in range(4):
                for cc in range(2):
                    k.dma("sp", YS[:, br * 2 + cc, :], DA(YT, (br * 256 + cc * 128) * NTOK, [[NTOK, 128], [1, NTOK]]),
                          reads=[("YT", br)], writes=["YS"])
            mT = k.sb("mT", [128, 8, NTOK], BF16)
            Wg = [k.sb("Wg%d" % i, [128, 8, 4, 128], BF16) for i in range(2)]
            Wb = [k.sb("Wb%d" % i, [128, 4, 2, 128], BF16) for i in range(2)]
            sgt = [k.sb("sgt%d" % i, [128, 512], F32) for i in range(2)]
            acc = [k.sb("acc%d" % i, [128, 512], F32) for i in range(2)]
            n_ = 0
            for dc in range(8):
                i = dc % 2
                for br in range(4):
                    k.dma("pool", Wg[i].v(br * 128, [[512, 8], [1, 128]]),
                          DA(I["w_in"], l * 1024 * 6400 + 2304 + br * 1024 + dc * 128, [[6400, 128], [128 * 6400, 8], [1, 128]]),
                          writes=[("Wg", i)])
                k.dma("pool", Wb[i].v(0, [[128, 8], [1, 128]]),
                      DA(I["w_branch"], l * 4 * 256 * 1024 + dc * 128, [[1024, 128], [128 * 1024, 8], [1, 128]]), writes=[("Wb", i)])
                for mi, (tok0, ntok) in enumerate(MT):
                    a_ = acc[mi % 2]; ak = ("acc", mi % 2)
                    for br in range(4):
                        psA, pkA = k.bank()
                        for kc in range(8):
                            k.op("pe", lambda e, psA=psA, i=i, kc=kc, br=br, tok0=tok0, ntok=ntok: e.matmul(
                                psA[:, 0:ntok], Wg[i][:, kc, br, :], alT[:, kc, tok0:tok0 + ntok], start=(kc == 0), stop=(kc == 7)),
                                reads=[("Wg", i)] + tkeys("alT", tok0, ntok), writes=[pkA])
                        psB, pkB = k.bank()
                        for cc in range(2):
                            k.op("pe", lambda e, psB=psB, i=i, cc=cc, br=br, tok0=tok0, ntok=ntok: e.matmul(
                                psB[:, 0:ntok], Wb[i][:, br, cc, :], YS[:, br * 2 + cc, tok0:tok0 + ntok], start=(cc == 0), stop=(cc == 1)),
                                reads=[("Wb", i), "YS"], writes=[pkB])
                        s_ = sgt[n_ % 2]; sk = ("sgt", n_ % 2); n_ += 1
                        k.op("act", lambda e, psA=psA, s_=s_, ntok=ntok: e.activation(s_[:, 0:ntok], psA[:, 0:ntok], AF.Sigmoid),
                             reads=[pkA], writes=[sk])
                        if br == 0:
                            k.op("dve", lambda e, psB=psB, s_=s_, a_=a_, ntok=ntok: e.tensor_tensor(a_[:, 0:ntok], psB[:, 0:ntok], s_[:, 0:ntok], ALU.mult),
                                 reads=[pkB, sk], writes=[ak])
                        else:
                            k.op("dve", lambda e, psB=psB, s_=s_, ntok=ntok: e.tensor_tensor(s_[:, 0:ntok], psB[:, 0:ntok], s_[:, 0:ntok], ALU.mult),
                                 reads=[pkB, sk], writes=[sk])
                            if br < 3:
                                k.op("pool", lambda e, s_=s_, a_=a_, ntok=ntok: e.tensor_tensor(a_[:, 0:ntok], a_[:, 0:ntok], s_[:, 0:ntok], ALU.add),
                                     reads=[ak, sk], writes=[ak])
                            else:
                                k.op("pool", lambda e, s_=s_, a_=a_, dc=dc, tok0=tok0, ntok=ntok: e.tensor_tensor(
                                    mT[:, dc, tok0:tok0 + ntok], a_[:, 0:ntok], s_[:, 0:ntok], ALU.add),
                                    reads=[ak, sk], writes=tkeys("mT", tok0, ntok))
            WO = k.sb("WO", [128, 8, 1024], BF16)
            for hh in range(2):
                wload(WO.v(hh * 512, [[1024, 8], [1, 512]]), I["w_out"], l * 1024 * 1024 + hh * 512, 1024, 8, 512, "WO")
            GATE = k.sb("GATE", [128, 2, 1024], F32)
            for w in range(2):
                k.dma("sp", GATE[:, w, :], DA(MODROW, (l * 2 + w) * 6144 + 2 * 1024, [[0, 128], [1, 1024]]), reads=[("MODROW", l)], writes=["GATE"])
            HT = [k.sb("HT%d" % i, [128, 1024], F32) for i in range(2)]
            tm = [k.sb("tm%d" % i, [128, 512], F32) for i in range(2)]
            n_ = 0
            for t in range(NTILE):
                b = t % 2
                w = 1 if t < 2 else 0
                k.dma("sp", HT[b][:], DA(H, t * 128 * 1024, [[1024, 128], [1, 1024]]), reads=[("H", t)], writes=[("HT", b)])
                for hh in range(2):
                    ps, pk = k.bank()
                    for dc in range(8):
                        k.op("pe", lambda e, ps=ps, dc=dc, t=t, hh=hh: e.matmul(
                            ps[:, 0:512], mT[:, dc, t * 128:(t + 1) * 128], WO[:, dc, hh * 512:(hh + 1) * 512], start=(dc == 0), stop=(dc == 7)),
                            reads=[("mT", t), "WO"], writes=[pk])
                    tq = tm[n_ % 2]; tk = ("tm", n_ % 2); n_ += 1
                    k.op("dve", lambda e, ps=ps, tq=tq, w=w, hh=hh: e.tensor_tensor(tq[:], ps[:, 0:512], GATE[:, w, hh * 512:(hh + 1) * 512], ALU.mult),
                         reads=[pk, "GATE"], writes=[tk])
                    k.op("pool", lambda e, tq=tq, b=b, hh=hh: e.tensor_tensor(HT[b][:, hh * 512:(hh + 1) * 512], HT[b][:, hh * 512:(hh + 1) * 512], tq[:], ALU.add),
                         reads=[tk, ("HT", b)], writes=[("HT", b)])
                k.dma("sp", DA(H, t * 128 * 1024, [[1024, 128], [1, 1024]]), HT[b][:], reads=[("HT", b)], writes=[("H", t)])
            k.release()

        def ffn_phase(l):
            k.mark()
            GATE = k.sb("GATE5", [128, 2, 1024], F32)
            for w in range(2):
                k.dma("sp", GATE[:, w, :], DA(MODROW, (l * 2 + w) * 6144 + 5 * 1024, [[0, 128], [1, 1024]]), reads=[("MODROW", l)], writes=["GATE5"])
            W1 = k.sb("W1", [128, 8, 2048], BF16)
            W2 = k.sb("W2", [128, 16, 1024], BF16)
            hT = k.sb("hT", [128, 16, 512], BF16)
            rl = [k.sb("rl%d" % i, [128, 512], F32) for i in range(2)]
            HT = [k.sb("HTf%d" % i, [128, 1024], F32) for i in range(2)]
            tm = [k.sb("tmf%d" % i, [128, 512], F32) for i in range(2)]
            n_ = 0
            n2 = 0
            for fh in range(2):
                for q in range(4):
                    wload(W1.v(q * 512, [[2048, 8], [1, 512]]), I["w_ff1"], l * 1024 * 4096 + fh * 2048 + q * 512, 4096, 8, 512, "W1")
                for q in range(2):
                    wload(W2.v(q * 512, [[1024, 16], [1, 512]]), I["w_ff2"], l * 4096 * 1024 + fh * 2048 * 1024 + q * 512, 1024, 16, 512, "W2")
                for (tok0, ntok) in MT:
                    for fc in range(16):
                        ps, pk = k.bank()
                        for kc in range(8):
                            k.op("pe", lambda e, ps=ps, kc=kc, fc=fc, tok0=tok0, ntok=ntok: e.matmul(
                                ps[:, 0:ntok], W1[:, kc, fc * 128:(fc + 1) * 128], alT[:, kc, tok0:tok0 + ntok], start=(kc == 0), stop=(kc == 7)),
                                reads=["W1"] + tkeys("blT", tok0, ntok), writes=[pk])
                        r_ = rl[n_ % 2]; rk = ("rl", n_ % 2); n_ += 1
                        k.op("act", lambda e, ps=ps, r_=r_, ntok=ntok: e.activation(r_[:, 0:ntok], ps[:, 0:ntok], AF.Relu), reads=[pk], writes=[rk])
                        k.op("pool", lambda e, r_=r_, fc=fc, ntok=ntok: e.tensor_tensor(hT[:, fc, 0:ntok], r_[:, 0:ntok], r_[:, 0:ntok], ALU.mult),
                             reads=[rk], writes=["hT"])
                    for tt_ in range(ntok // 128):
                        t = tok0 // 128 + tt_
                        b = t % 2
                        w = 1 if t < 2 else 0
                        k.dma("sp", HT[b][:], DA(H, t * 128 * 1024, [[1024, 128], [1, 1024]]), reads=[("H", t)], writes=[("HTf", b)])
                        for hh in range(2):
                            ps, pk = k.bank()
                            for fc in range(16):
                                k.op("pe", lambda e, ps=ps, fc=fc, tt_=tt_, hh=hh: e.matmul(
                                    ps[:, 0:512], hT[:, fc, tt_ * 128:(tt_ + 1) * 128], W2[:, fc, hh * 512:(hh + 1) * 512],
                                    start=(fc == 0), stop=(fc == 15)), reads=["hT", "W2"], writes=[pk])
                            tq = tm[n2 % 2]; tk = ("tmf", n2 % 2); n2 += 1
                            k.op("dve", lambda e, ps=ps, tq=tq, w=w, hh=hh: e.tensor_tensor(tq[:], ps[:, 0:512], GATE[:, w, hh * 512:(hh + 1) * 512], ALU.mult),
                                 reads=[pk, "GATE5"], writes=[tk])
                            k.op("pool", lambda e, tq=tq, b=b, hh=hh: e.tensor_tensor(HT[b][:, hh * 512:(hh + 1) * 512], HT[b][:, hh * 512:(hh + 1) * 512], tq[:], ALU.add),
                                 reads=[tk, ("HTf", b)], writes=[("HTf", b)])
                        k.dma("sp", DA(H, t * 128 * 1024, [[1024, 128], [1, 1024]]), HT[b][:], reads=[("HTf", b)], writes=[("H", t)])
            k.release()

        def final_phase():
            k.mark()
            FG = k.sb("FG", [128, 1024], F32)
            k.dma("sp", FG[:], DA(I["final_g"], 0, [[0, 128], [1, 1024]]), writes=["FG"])
            XT = [k.sb("XTf%d" % i, [128, 1024], F32) for i in range(2)]
            junk = k.sb("junkf", [128, 1024], BF16)
            ss = k.sb("ssf", [128, NTILE], F32)
            k.op("dve", lambda e: e.memset(ss[:], 0.0), writes=["ssf"])
            for t in range(2, NTILE):
                b = t % 2
                xt = XT[b]
                k.dma("sp", xt[:], DA(H, t * 128 * 1024, [[1024, 128], [1, 1024]]), reads=[("H", t)], writes=[("xtf", b)])
                k.op("act", lambda e, xt=xt, t=t: e.activation(junk[:], xt[:], AF.Square, accum_out=ss[:, t:t + 1]),
                     reads=[("xtf", b), "ssf"], writes=["junkf", "ssf"])
                k.op("dve", lambda e, t=t: e.tensor_scalar(ss[:, t:t + 1], ss[:, t:t + 1], 1.0 / 1024, 1e-6, ALU.mult, ALU.add), reads=["ssf"], writes=["ssf"])
                k.op("act", lambda e, t=t: e.activation(ss[:, t:t + 1], ss[:, t:t + 1], AF.Sqrt), reads=["ssf"], writes=["ssf"])
                k.op("dve", lambda e, t=t: e.reciprocal(ss[:, t:t + 1], ss[:, t:t + 1]), reads=["ssf"], writes=["ssf"])
                k.op("dve", lambda e, xt=xt, t=t: e.scalar_tensor_tensor(xt[:], xt[:], ss[:, t:t + 1], FG[:], ALU.mult, ALU.mult),
                     reads=[("xtf", b), "ssf", "FG"], writes=[("xtf", b)])
                k.dma("sp", DA(OUT, (t - 2) * 128 * 1024, [[1024, 128], [1, 1024]]), xt[:], reads=[("xtf", b)], writes=[("OUT", t)])
            k.release()

        def program():
            stages = [("M", phase_M), ("s5setup", s5_setup), ("rpsetup", rp_setup)]
            for l in range(nl):
                stages += [
                    ("norm1_%d" % l, lambda l=l: (norm_phase(l, 0, alT, "alT"), dump("alT%d" % l, alT.v(0, [[1, 8 * NTOK]]), [128, 8 * NTOK], BF16))),
                    ("s5_%d" % l, lambda l=l: (s5_branch(l), dump("YT%d_s5" % l, DA(YT, 0, [[NTOK, 1024], [1, NTOK]]), [1024, NTOK], BF16))),
                    ("attn_%d" % l, lambda l=l: (attn_branch(l), dump("YT%d_attn" % l, DA(YT, 0, [[NTOK, 1024], [1, NTOK]]), [1024, NTOK], BF16))),
                    ("conv_%d" % l, lambda l=l: (conv_branch(l), dump("YT%d_conv" % l, DA(YT, 0, [[NTOK, 1024], [1, NTOK]]), [1024, NTOK], BF16))),
                    ("sgu_%d" % l, lambda l=l: (sgu_branch(l), dump("YT%d_sgu" % l, DA(YT, 0, [[NTOK, 1024], [1, NTOK]]), [1024, NTOK], BF16))),
                    ("merge_%d" % l, lambda l=l: (merge_phase(l), dump("hmid%d" % l, DA(H, 0, [[1024, NTOK], [1, 1024]]), [NTOK, 1024], F32))),
                    ("norm2_%d" % l, lambda l=l: norm_phase(l, 1, alT, "blT")),
                    ("ffn_%d" % l, lambda l=l: (ffn_phase(l), dump("hend%d" % l, DA(H, 0, [[1024, NTOK], [1, 1024]]), [NTOK, 1024], F32))),
                ]
            stages.append(("final", final_phase))
            for name, fn in stages:
                if skip and name.split("_")[0] in skip:
                    continue
                fn()
                if name == stop:
                    break
        program()
        k.emit()
    return nc, dbg_out


_CACHE = {}


def kernel(**inputs):
    n = 8
    consts = host_constants()
    if "nc" not in _CACHE:
        _CACHE["nc"] = build()[0]
    nc = _CACHE["nc"]
    shared = {}
    for name in INPUT_SHAPES:
        if name in ("x", "c", "ctx"):
            continue
        src = consts[name] if name in consts else inputs[name]
        shared[name] = np.ascontiguousarray(np.asarray(src, dtype=np.float32))
    in_maps = []
    for b in range(n):
        m = dict(shared)
        m["x"] = np.ascontiguousarray(np.asarray(inputs["x"][b], dtype=np.float32))
        m["c"] = np.ascontiguousarray(np.asarray(inputs["c"][b], dtype=np.float32))
        m["ctx"] = np.ascontiguousarray(np.asarray(inputs["ctx"][b], dtype=np.float32))
        in_maps.append(m)
    res = run_bass_kernel_spmd(nc, in_maps, core_ids=list(range(n)))
    return np.stack([np.asarray(r["out"], dtype=np.float32) for r in res.results], axis=0)
```

```python
import os
import numpy as np
from contextlib import ExitStack
CUT = int(os.environ.get('CUT', '0'))
ACT_EVAC = os.environ.get('ACT_EVAC', '0') == '1'
import concourse.bass as bass
import concourse.mybir as mybir
from concourse.bass_utils import run_bass_kernel_spmd

F32 = mybir.dt.float32
BF16 = mybir.dt.bfloat16
I32 = mybir.dt.int32
AF = mybir.ActivationFunctionType
ALU = mybir.AluOpType

NDSEM = 8
SAME_ENGINE_SYNC = True
PI = float(np.pi)

NTOK = 2304
NTILE = 18
MT = [(0, 512), (512, 512), (1024, 512), (1536, 512), (2048, 256)]
NEG = -30000.0


class Buf:
    __slots__ = ("w", "r")

    def __init__(self):
        self.w = None
        self.r = {}


class T:
    def __init__(self, h, shape):
        self.h = h
        self.shape = list(shape)
        self.F = int(np.prod(shape[1:]))

    def __getitem__(self, idx):
        return self.h[idx]

    def v(self, off, dims, p0=0, np_=None):
        if np_ is None:
            np_ = self.shape[0] - p0
        return bass.AP(self.h, p0 * self.F + off, [[self.F, np_]] + [list(d) for d in dims])


def DA(h, off, dims):
    return bass.AP(h, off, [list(d) for d in dims])


class K:
    ENGS = ("pe", "act", "dve", "pool", "sp")
    DMAQ = ("sp", "act", "pool")

    def __init__(self, nc, stack):
        self.nc = nc
        self.stack = stack
        self.prog = {e: [] for e in self.ENGS}
        self.sem = {e: stack.enter_context(nc.semaphore("s_" + e)) for e in self.ENGS}
        self.cnt = {e: 0 for e in self.ENGS}
        self.seen = {e: {f: 0 for f in self.ENGS} for e in self.ENGS}
        self.dsem = {q: [stack.enter_context(nc.semaphore("d_%s%d" % (q, j))) for j in range(NDSEM)]
                     for q in self.DMAQ}
        self.dtarget = {q: [0] * NDSEM for q in self.DMAQ}
        self.dnext = {q: 0 for q in self.DMAQ}
        self.dseen = {e: {} for e in self.ENGS}
        self.bufs = {}
        self.sb_off = 16640
        self.sb_marks = []
        self.uid = 0
        self.nbank = 0
        self.banks = []

    def sb(self, name, shape, dtype, parts=None):
        esz = 2 if dtype == BF16 else 4
        nbytes = int(np.prod(shape[1:])) * esz
        off = (self.sb_off + 63) // 64 * 64
        self.uid += 1
        h = self.nc.alloc_sbuf_tensor_at("%s_%d" % (name, self.uid), list(shape), dtype, offset=off)
        self.sb_off = off + nbytes
        assert self.sb_off <= 229376, ("SBUF overflow", name, self.sb_off)
        return T(h, shape)

    def mark(self):
        self.sb_marks.append(self.sb_off)

    def release(self):
        self.barrier()
        self.sb_off = self.sb_marks.pop()

    def bank(self):
        i = self.nbank % len(self.banks)
        self.nbank += 1
        return self.banks[i], ("ps", i)

    def _buf(self, k):
        b = self.bufs.get(k)
        if b is None:
            b = self.bufs[k] = Buf()
        return b

    def _wait(self, e, tok):
        if tok[0] == "eng":
            _, f, n = tok
            if f == e and (e == "pe" or not SAME_ENGINE_SYNC):
                return
            if self.seen[e][f] >= n:
                return
            self.seen[e][f] = n
            sem = self.sem[f]
            self.prog[e].append(lambda eng, sem=sem, n=n: eng.wait_ge(sem, n))
        else:
            _, q, j, tgt = tok
            if self.dseen[e].get((q, j), 0) >= tgt:
                return
            self.dseen[e][(q, j)] = tgt
            sem = self.dsem[q][j]
            self.prog[e].append(lambda eng, sem=sem, tgt=tgt: eng.wait_ge(sem, tgt))

    def _deps(self, e, reads, writes):
        toks = []
        for k in reads:
            b = self._buf(k)
            if b.w is not None:
                toks.append(b.w)
        for k in writes:
            b = self._buf(k)
            if b.w is not None:
                toks.append(b.w)
            toks.extend(b.r.values())
        for t in toks:
            self._wait(e, t)

    def _commit(self, tok, reads, writes):
        for k in reads:
            b = self._buf(k)
            if tok[0] == "eng":
                b.r[("eng", tok[1])] = tok
            else:
                b.r[("dma", tok[1], tok[2])] = tok
        for k in writes:
            b = self._buf(k)
            b.w = tok
            b.r = {}

    def op(self, e, fn, reads=(), writes=(), count=True):
        psr = [r for r in reads if isinstance(r, tuple) and r[0] in ("ps", "psT")]
        if psr:
            reads = [r for r in reads if r not in psr]
            writes = list(writes) + psr
        self._deps(e, reads, writes)
        if not count:
            self.prog[e].append(lambda eng, fn=fn: fn(eng))
            self._commit(("eng", e, self.cnt[e] + 1), reads, writes)
            return
        self.cnt[e] += 1
        n = self.cnt[e]
        sem = self.sem[e]
        self.prog[e].append(lambda eng, fn=fn, sem=sem: fn(eng).then_inc(sem, 1))
        self._commit(("eng", e, n), reads, writes)

    def dma(self, q, out, in_, reads=(), writes=(), **kw):
        self._deps(q, reads, writes)
        j = self.dnext[q]
        self.dnext[q] = (j + 1) % NDSEM
        prev = self.dtarget[q][j]
        if prev > 0:
            self._wait(q, ("dma", q, j, prev))
        self.dtarget[q][j] = prev + 16
        sem = self.dsem[q][j]
        self.prog[q].append(lambda eng, out=out, in_=in_, sem=sem, kw=kw:
                            eng.dma_start(out=out, in_=in_, **kw).then_inc(sem, 16))
        self._commit(("dma", q, j, prev + 16), reads, writes)

    def barrier(self):
        for e in self.ENGS:
            for f in self.ENGS:
                if f != e and self.cnt[f] > 0:
                    self._wait(e, ("eng", f, self.cnt[f]))
            for q in self.DMAQ:
                for j in range(NDSEM):
                    if self.dtarget[q][j] > 0:
                        self._wait(e, ("dma", q, j, self.dtarget[q][j]))
        self.bufs = {}

    def emit(self):
        self.barrier()
        nc = self.nc
        prog = self.prog
        with nc.Block() as block:
            @block.tensor
            def _(eng):
                for f in prog["pe"]:
                    f(eng)

            @block.scalar
            def _(eng):
                for f in prog["act"]:
                    f(eng)

            @block.vector
            def _(eng):
                for f in prog["dve"]:
                    f(eng)

            @block.gpsimd
            def _(eng):
                for f in prog["pool"]:
                    f(eng)

            @block.sync
            def _(eng):
                for f in prog["sp"]:
                    f(eng)


INPUT_SHAPES = {
    "x": [2048, 1024], "c": [1024], "ctx": [256, 1024], "c_ctx": [1024],
    "ada_w": [4, 1024, 6144], "ada_b": [4, 6144], "norm_g": [4, 2, 1024], "final_g": [1024],
    "w_in": [4, 1024, 6400], "w_branch": [4, 4, 256, 1024], "w_out": [4, 1024, 1024],
    "s5_lam_re": [4, 2, 16, 64], "s5_lam_im": [4, 2, 16, 64], "s5_log_dt": [4, 2, 16],
    "s5_b_re": [4, 2, 16, 64, 16], "s5_b_im": [4, 2, 16, 64, 16],
    "s5_c_re": [4, 2, 16, 16, 64], "s5_c_im": [4, 2, 16, 16, 64],
    "s5_d": [4, 256], "s5_glu_w": [4, 256, 256], "s5_glu_b": [4, 256],
    "na_rpb": [4, 4, 15, 31], "conv_w": [4, 3, 256], "sgu_ln_g": [4, 256], "sgu_ln_b": [4, 256],
    "sgu_w": [4, 4, 128, 128], "sgu_b": [4, 4, 128], "w_ff1": [4, 1024, 4096], "w_ff2": [4, 4096, 1024],
    "k_ident": [128, 128], "k_colmask": [128, 64], "k_trimask": [2, 128, 128], "k_s5nt": [2, 25],
}


def host_constants():
    ident = np.eye(128, dtype=np.float32)
    j = np.arange(64)
    col_start = np.clip(j - 8, 0, 48)
    col_ok = (j[None, :] >= col_start[:, None]) & (j[None, :] < col_start[:, None] + 16)
    cm = np.where(col_ok.T, 0.0, NEG).astype(np.float32)
    colmask = np.concatenate([cm, cm], 0)
    r = np.arange(128) // 16
    tri = np.stack([(r[None, :] >= r[:, None]), (r[:, None] >= r[None, :])]).astype(np.float32)
    nt = np.zeros((2, 25), np.float32)
    rr = np.arange(8)
    nt[0, 0:8] = 7 - rr; nt[0, 8:16] = -1 - rr; nt[0, 16:24] = rr + 1; nt[0, 24] = 1
    nt[1, 0:8] = rr; nt[1, 8:16] = rr - 8; nt[1, 16:24] = 8 - rr; nt[1, 24] = 1
    return {"k_ident": ident, "k_colmask": colmask, "k_trimask": tri, "k_s5nt": nt}


def build(nl=4, dbg=(), stop=None, skip=()):
    nc = bass.Bass("TRN2", target_bir_lowering=False)
    I = {n: nc.dram_tensor(n, s, F32, kind="ExternalInput") for n, s in INPUT_SHAPES.items()}
    OUT = nc.dram_tensor("out", [2048, 1024], F32, kind="ExternalOutput")
    dbg_out = {}

    def scratch(name, shape, dt=F32):
        return nc.dram_tensor(name, list(shape), dt)

    H = scratch("H", [NTOK, 1024])
    MODROW = scratch("MODROW", [4, 2, 6144])
    PWD = scratch("PWD", [2, 64, 128 * 25])
    BBD = scratch("BBD", [2, 64, 128 * 16])
    CD = scratch("CD", [2, 64, 128 * 16])
    UDP = scratch("UDP", [256, NTOK], BF16)
    YDP = scratch("YDP", [256, NTOK])
    YT = scratch("YT", [4, 256, NTOK], BF16)
    RP = scratch("RP", [4 * 4 * 15 + 1, 160])

    with ExitStack() as st:
        k = K(nc, st)
        for i in range(6):
            k.banks.append(T(st.enter_context(nc.psum_tensor("psb%d" % i, [128, 512], F32)), [128, 512]))
        psT = [T(st.enter_context(nc.psum_tensor("psT%d" % i, [128, 1024], BF16)), [128, 1024]) for i in range(2)]

        def dump(name, src_ap, shape, dt=F32, reads=()):
            if name not in dbg:
                return
            t = nc.dram_tensor("dbg_" + name, list(shape), dt, kind="ExternalOutput")
            dbg_out[name] = t
            k.barrier()
            k.dma("sp", t.ap(), src_ap, reads=reads)
            k.barrier()

        ident_f = k.sb("ident_f", [128, 128], F32)
        ident_b = k.sb("ident_b", [128, 128], BF16)
        k.dma("sp", ident_f[:], I["k_ident"].ap(), writes=["ident_f"])
        k.dma("pool", ident_b[:], I["k_ident"].ap(), writes=["ident_b"])
        MODC = k.sb("MODC", [128, 4, 4, 8, 2], F32)
        GS = k.sb("GS", [128, 4, 2, 8, 2], F32)
        WB = [k.sb("WB%d" % i, [128, 8, 256], BF16) for i in range(2)]
        alT = k.sb("alT", [128, 8, NTOK], BF16)

        def phase_M():
            k.dma("sp", DA(H, 0, [[1024, 256], [1, 1024]]), I["ctx"].ap(), writes=[("H", t) for t in range(2)])
            k.dma("sp", DA(H, 256 * 1024, [[1024, 2048], [1, 1024]]), I["x"].ap(), writes=[("H", t) for t in range(2, 18)])

            if CUT == 1:
                return
            k.mark()
            scraw = k.sb("scraw", [128, 8, 2], F32)
            SC = k.sb("SC", [128, 8, 2], F32)
            k.dma("sp", scraw.v(0, [[2, 8], [1, 1]]), DA(I["c"], 0, [[1, 128], [128, 8], [1, 1]]), writes=["scraw"], allow_slow_non_contiguous=True)
            k.dma("sp", scraw.v(1, [[2, 8], [1, 1]]), DA(I["c_ctx"], 0, [[1, 128], [128, 8], [1, 1]]), writes=["scraw"], allow_slow_non_contiguous=True)
            k.op("act", lambda e: e.activation(SC[:], scraw[:], AF.Silu), reads=["scraw"], writes=["SC"])
            NG = k.sb("NG", [128, 8, 8], F32)
            for lj in range(8):
                k.dma("sp", NG.v(lj * 8, [[1, 8], [1, 1]]), DA(I["norm_g"], lj * 1024, [[1, 128], [128, 8], [1, 1]]),
                      writes=["NG"], allow_slow_non_contiguous=True)
            if CUT == 2:
                k.release(); return
            modrow = k.sb("modrow", [2, 6144], F32)
            adab = k.sb("adab", [2, 6144], F32)
            adaw = [k.sb("adaw%d" % i, [128, 8, 512], F32) for i in range(4)]
            for l in range(nl):
                k.dma("sp", adab[:], DA(I["ada_b"], l * 6144, [[0, 2], [1, 6144]]), writes=["adab"])
                for n in range(12):
                    wt = adaw[n % 4]
                    wk = ("adaw", n % 4)
                    k.dma("sp", wt[:], DA(I["ada_w"], l * 1024 * 6144 + n * 512, [[6144, 128], [128 * 6144, 8], [1, 512]]),
                          writes=[wk])
                    ps, pk = k.bank()
                    for kc in range(8):
                        k.op("pe", lambda e, ps=ps, wt=wt, kc=kc: e.matmul(ps[0:2, 0:512], SC[:, kc, :], wt[:, kc, :],
                                                                          start=(kc == 0), stop=(kc == 7)),
                             reads=[wk, "SC"], writes=[pk], count=(kc == 7))
                    k.op("dve", lambda e, ps=ps, n=n: e.tensor_tensor(modrow[0:2, n * 512:(n + 1) * 512], ps[0:2, 0:512],
                                                                     adab[0:2, n * 512:(n + 1) * 512], ALU.add),
                         reads=[pk, "adab"], writes=["modrow"])
                if CUT == 3:
                    k.release(); return
                k.dma("sp", DA(MODROW, l * 2 * 6144, [[6144, 2], [1, 6144]]), modrow[:], reads=["modrow"],
                      writes=[("MODROW", l)])
                if CUT == 4:
                    k.release(); return
                ps, pk = k.bank()
                for mi, m in enumerate([0, 1, 3, 4]):
                    for kc in range(8):
                        k.op("pe", lambda e, ps=ps, mi=mi, m=m, kc=kc: e.matmul(
                            ps[:, (mi * 8 + kc) * 2:(mi * 8 + kc) * 2 + 2],
                            modrow[0:2, m * 1024 + kc * 128:m * 1024 + kc * 128 + 128], ident_f[0:2, 0:2],
                            start=True, stop=True), reads=["modrow", "ident_f"], writes=[pk], count=(mi == 3 and kc == 7))
                k.op("dve", lambda e, ps=ps, l=l: e.tensor_copy(MODC.v(l * 64, [[1, 64]]), ps[:, 0:64]),
                     reads=[pk], writes=["MODC"])
                if CUT == 5:
                    k.release(); return
                for j in range(2):
                    k.op("dve", lambda e, l=l, j=j: e.scalar_tensor_tensor(
                        GS.v((l * 2 + j) * 16, [[2, 8], [1, 2]]), MODC.v(l * 64 + (2 * j + 1) * 16, [[2, 8], [1, 2]]), 1.0,
                        NG.v((l * 2 + j) * 8, [[1, 8], [0, 2]]), ALU.add, ALU.mult),
                        reads=["MODC", "NG"], writes=["GS"])
            k.release()


        def SHv(l, j, kc, w):
            return MODC.v(l * 64 + (2 * j) * 16 + kc * 2 + w, [[1, 1]])

        def GSv(l, j, kc, w):
            return GS.v((l * 2 + j) * 16 + kc * 2 + w, [[1, 1]])

        def norm_phase(l, j, dst, dkey):
            k.mark()
            NB = 9
            XA = k.sb("XA", [128, NB, 1024], F32)
            xn = [k.sb("xn%d" % i, [128, 1024], BF16) for i in range(2)]
            junk = k.sb("junk", [128, 1024], BF16)
            ss = k.sb("ss", [128, NTILE], F32)
            rs = k.sb("rs", [128, NTILE], F32)
            k.op("dve", lambda e: e.memset(ss[:], 0.0), writes=["ss"])
            for t0 in range(0, NTILE, NB):
                for t in range(t0, t0 + NB):
                    k.dma("sp" if t % 2 == 0 else "act", XA[:, t - t0, :], DA(H, t * 128 * 1024, [[1024, 128], [1, 1024]]),
                          reads=[("H", t)], writes=[("xa", t - t0)])
                for t in range(t0, t0 + NB):
                    k.op("act", lambda e, t=t, t0=t0: e.activation(junk[:], XA[:, t - t0, :], AF.Square, accum_out=ss[:, t:t + 1]),
                         reads=[("xa", t - t0), "ss"], writes=["junk", "ss"])
                k.op("dve", lambda e, t0=t0: e.tensor_scalar(rs[:, t0:t0 + NB], ss[:, t0:t0 + NB], 1.0 / 1024, 1e-6, ALU.mult, ALU.add), reads=["ss"], writes=["rs"])
                k.op("act", lambda e, t0=t0: e.activation(rs[:, t0:t0 + NB], rs[:, t0:t0 + NB], AF.Sqrt), reads=["rs"], writes=["rs"])
                k.op("dve", lambda e, t0=t0: e.reciprocal(rs[:, t0:t0 + NB], rs[:, t0:t0 + NB]), reads=["rs"], writes=["rs"])
                for t in range(t0, t0 + NB):
                    b = t % 2
                    k.op("dve", lambda e, b=b, t=t, t0=t0: e.tensor_scalar(xn[b][:], XA[:, t - t0, :], rs[:, t:t + 1], None, ALU.mult),
                         reads=[("xa", t - t0), "rs"], writes=[("xn", b)])
                    for kc in range(8):
                        k.op("pe", lambda e, b=b, kc=kc: e.transpose(psT[b][:, kc * 128:(kc + 1) * 128],
                                                                     xn[b][:, kc * 128:(kc + 1) * 128], ident_b[:]),
                             reads=[("xn", b), "ident_b"], writes=[("psT", b)], count=(kc == 7))
                    w = 1 if t < 2 else 0
                    for kc in range(8):
                        o = dst[:, kc, t * 128:(t + 1) * 128]
                        i_ = psT[b][:, kc * 128:(kc + 1) * 128]
                        if b == 0 or not ACT_EVAC:
                            k.op("dve", lambda e, o=o, i_=i_, kc=kc, w=w: e.tensor_scalar(
                                o, i_, GSv(l, j, kc, w), SHv(l, j, kc, w), ALU.mult, ALU.add),
                                reads=[("psT", b), "GS", "MODC"], writes=[(dkey, t, 0)])
                        else:
                            k.op("act", lambda e, o=o, i_=i_, kc=kc, w=w: e.activation(
                                o, i_, AF.Identity, bias=SHv(l, j, kc, w), scale=GSv(l, j, kc, w)),
                                reads=[("psT", b), "GS", "MODC"], writes=[(dkey, t, 0)])
            k.release()

        def tkeys(key, tok0, ntok):
            return [(key, t, p) for t in range(tok0 // 128, (tok0 + ntok + 127) // 128) for p in range(2)]

        def both(key):
            return [(key, 0), (key, 1)]

        def wload(wb_ap, handle, off, row_stride, nk, ncols, key, kstride=None):
            if kstride is None:
                kstride = 128 * row_stride
            k.dma("pool", wb_ap, DA(handle, off, [[row_stride, 128], [kstride, nk], [1, ncols]]), writes=[key])

        wbi = [0]

        def proj_fm(l, col0, ncols, evac, srcT=alT, skey="alT"):
            cbs = list(range(0, ncols, 256))
            slots = {}

            def pf(cb):
                nb = min(256, ncols - cb)
                i = wbi[0] % 2
                wbi[0] += 1
                wload(WB[i].v(0, [[256, 8], [1, nb]]), I["w_in"], l * 1024 * 6400 + col0 + cb, 6400, 8, nb, ("WB", i))
                slots[cb] = i
            pf(cbs[0])
            for ci_, cb in enumerate(cbs):
                nb = min(256, ncols - cb)
                if ci_ + 1 < len(cbs):
                    pf(cbs[ci_ + 1])
                i = slots[cb]
                wb = WB[i]
                for sub in range(0, nb, 128):
                    ci = (cb + sub) // 128
                    for (tok0, ntok) in MT:
                        ps, pk = k.bank()
                        for kc in range(8):
                            k.op("pe", lambda e, ps=ps, wb=wb, kc=kc, sub=sub, tok0=tok0, ntok=ntok: e.matmul(
                                ps[:, 0:ntok], wb[:, kc, sub:sub + 128], srcT[:, kc, tok0:tok0 + ntok],
                                start=(kc == 0), stop=(kc == 7)),
                                reads=[("WB", i)] + tkeys(skey, tok0, ntok), writes=[pk], count=(kc == 7))
                        evac(ci, tok0, ntok, ps, pk)

        evi = [0]

        def copy_evac(out_ap, in_ap, reads, writes, scale=None):
            evi[0] += 1
            writes = [(w_, evi[0] % 2) for w_ in writes]
            fe = os.environ.get("EVAC", "")
            if (evi[0] % 2 == 0 or fe == "dve") and fe != "act":
                if scale is None:
                    k.op("dve", lambda e: e.tensor_copy(out_ap, in_ap), reads=reads, writes=writes)
                else:
                    k.op("dve", lambda e: e.tensor_scalar(out_ap, in_ap, scale, None, ALU.mult), reads=reads, writes=writes)
            else:
                if scale is None:
                    k.op("act", lambda e: e.activation(out_ap, in_ap, AF.Copy), reads=reads, writes=writes)
                else:
                    k.op("act", lambda e: e.activation(out_ap, in_ap, AF.Copy, scale=scale), reads=reads, writes=writes)

        def s5_setup():
            k.mark()
            nat = k.sb("nat", [128, 64], F32)
            lam = [k.sb("lam%d" % i, [64, 128], F32) for i in range(2)]
            for i, nm in enumerate(["s5_lam_re", "s5_lam_im"]):
                k.dma("sp", nat[:], DA(I[nm], 0, [[64, 128], [1, 64]]), writes=["nat"])
                ps, pk = k.bank()
                k.op("pe", lambda e, ps=ps: e.matmul(ps[0:64, 0:128], nat[:], ident_f[:], start=True, stop=True),
                     reads=["nat", "ident_f"], writes=[pk])
                k.op("dve", lambda e, ps=ps, i=i: e.tensor_copy(lam[i][:], ps[0:64, 0:128]), reads=[pk], writes=[("lam", i)])
            dt = k.sb("dt", [64, 128], F32)
            k.dma("sp", dt[:], DA(I["s5_log_dt"], 0, [[0, 64], [1, 128]]), writes=["dt"])
            k.op("act", lambda e: e.activation(dt[:], dt[:], AF.Exp), reads=["dt"], writes=["dt"])
            z = [k.sb("z%d" % i, [64, 128], F32) for i in range(2)]
            for i in range(2):
                k.op("dve", lambda e, i=i: e.tensor_tensor(z[i][:], lam[i][:], dt[:], ALU.mult),
                     reads=[("lam", i), "dt"], writes=[("z", i)])
            ntb = k.sb("ntb", [64, 50], F32)
            k.dma("sp", ntb[:], DA(I["k_s5nt"], 0, [[0, 64], [1, 50]]), writes=["ntb"])
            NN = 128 * 25
            ZR = k.sb("ZR", [64, NN], F32)
            ZI = k.sb("ZI", [64, NN], F32)
            for zi_, Zt, zk in ((0, ZR, "ZR"), (1, ZI, "ZI")):
                for d in range(2):
                    k.op("dve", lambda e, zi_=zi_, Zt=Zt, d=d: e.tensor_tensor(
                        Zt.v(d * 16 * 25, [[32 * 25, 4], [25, 16], [1, 25]]),
                        z[zi_].v(d * 16, [[32, 4], [1, 16], [0, 25]]),
                        ntb.v(d * 25, [[0, 4], [0, 16], [1, 25]]), ALU.mult),
                        reads=[("z", zi_), "ntb"], writes=[zk])
            MAG = k.sb("MAG", [64, NN], F32)
            k.op("act", lambda e: e.activation(MAG[:], ZR[:], AF.Exp), reads=["ZR"], writes=["MAG"])
            t1 = k.sb("t1", [64, NN], F32)
            ti = k.sb("ti", [64, NN], I32)
            TR = [k.sb("TR%d" % i, [64, NN], F32) for i in range(2)]

            def trig(dst, dk, shift):
                k.op("dve", lambda e: e.tensor_scalar(dst[:], ZI[:], shift, None, ALU.add), reads=["ZI"], writes=[dk])
                k.op("dve", lambda e: e.tensor_scalar(t1[:], dst[:], 1.0 / (2 * PI), None, ALU.mult), reads=[dk], writes=["t1"])
                k.op("dve", lambda e: e.tensor_copy(ti[:], t1[:]), reads=["t1"], writes=["ti"])
                k.op("dve", lambda e: e.tensor_copy(t1[:], ti[:]), reads=["ti"], writes=["t1"])
                k.op("dve", lambda e: e.scalar_tensor_tensor(dst[:], t1[:], -2 * PI, dst[:], ALU.mult, ALU.add),
                     reads=["t1", dk], writes=[dk])
                k.op("dve", lambda e: e.tensor_scalar(dst[:], dst[:], -3.1415925, 3.1415925, ALU.max, ALU.min),
                     reads=[dk], writes=[dk])
                k.op("act", lambda e: e.activation(dst[:], dst[:], AF.Sin), reads=[dk], writes=[dk])
            trig(TR[0], "TR0", PI / 2)
            trig(TR[1], "TR1", 0.0)
            for i in range(2):
                k.op("dve", lambda e, i=i: e.tensor_tensor(TR[i][:], TR[i][:], MAG[:], ALU.mult),
                     reads=[("TR%d" % i), "MAG"], writes=[("TR%d" % i)])
                k.dma("sp", DA(PWD, i * 64 * NN, [[NN, 64], [1, NN]]), TR[i][:], reads=["TR%d" % i], writes=["PWD"])
            def a_v(i):
                return TR[i].v(24, [[25, 128]])
            nr = k.sb("nr", [64, 128], F32)
            den = k.sb("den", [64, 128], F32)
            tq = k.sb("tq", [64, 128], F32)
            cc_ = [k.sb("cc%d" % i, [64, 128], F32) for i in range(2)]
            k.op("dve", lambda e: e.tensor_scalar(nr[:], a_v(0), -1.0, None, ALU.add), reads=["TR0"], writes=["nr"])
            k.op("dve", lambda e: e.tensor_tensor(den[:], lam[0][:], lam[0][:], ALU.mult), reads=[("lam", 0)], writes=["den"])
            k.op("dve", lambda e: e.tensor_tensor(tq[:], lam[1][:], lam[1][:], ALU.mult), reads=[("lam", 1)], writes=["tq"])
            k.op("dve", lambda e: e.tensor_tensor(den[:], den[:], tq[:], ALU.add), reads=["den", "tq"], writes=["den"])
            k.op("dve", lambda e: e.reciprocal(den[:], den[:]), reads=["den"], writes=["den"])
            k.op("dve", lambda e: e.tensor_tensor(cc_[0][:], nr[:], lam[0][:], ALU.mult), reads=["nr", ("lam", 0)], writes=["cc0"])
            k.op("dve", lambda e: e.tensor_tensor(tq[:], a_v(1), lam[1][:], ALU.mult), reads=["TR1", ("lam", 1)], writes=["tq"])
            k.op("dve", lambda e: e.tensor_tensor(cc_[0][:], cc_[0][:], tq[:], ALU.add), reads=["cc0", "tq"], writes=["cc0"])
            k.op("dve", lambda e: e.tensor_tensor(cc_[0][:], cc_[0][:], den[:], ALU.mult), reads=["cc0", "den"], writes=["cc0"])
            k.op("dve", lambda e: e.tensor_tensor(cc_[1][:], a_v(1), lam[0][:], ALU.mult), reads=["TR1", ("lam", 0)], writes=["cc1"])
            k.op("dve", lambda e: e.tensor_tensor(tq[:], nr[:], lam[1][:], ALU.mult), reads=["nr", ("lam", 1)], writes=["tq"])
            k.op("dve", lambda e: e.tensor_tensor(cc_[1][:], cc_[1][:], tq[:], ALU.subtract), reads=["cc1", "tq"], writes=["cc1"])
            k.op("dve", lambda e: e.tensor_tensor(cc_[1][:], cc_[1][:], den[:], ALU.mult), reads=["cc1", "den"], writes=["cc1"])
            BN = 128 * 16
            Bsrc = [k.sb("Bsrc%d" % i, [64, BN], F32) for i in range(2)]
            for i, nm in enumerate(["s5_b_re", "s5_b_im"]):
                for l4 in range(4):
                    k.dma("sp", Bsrc[i].v(l4 * 512, [[16, 32], [1, 16]]),
                          DA(I[nm], l4 * 32 * 1024, [[16, 64], [1024, 32], [1, 16]]), writes=[("Bsrc", i)])
            Bb = [k.sb("Bb%d" % i, [64, BN], F32) for i in range(2)]
            tb = k.sb("tb", [64, BN], F32)

            def cb(i):
                return cc_[i].v(0, [[1, 128], [0, 16]])
            k.op("dve", lambda e: e.tensor_tensor(Bb[0].v(0, [[16, 128], [1, 16]]), Bsrc[0].v(0, [[16, 128], [1, 16]]), cb(0), ALU.mult),
                 reads=[("Bsrc", 0), "cc0"], writes=["Bb0"])
            k.op("dve", lambda e: e.tensor_tensor(tb.v(0, [[16, 128], [1, 16]]), Bsrc[1].v(0, [[16, 128], [1, 16]]), cb(1), ALU.mult),
                 reads=[("Bsrc", 1), "cc1"], writes=["tb"])
            k.op("dve", lambda e: e.tensor_tensor(Bb[0][:], Bb[0][:], tb[:], ALU.subtract), reads=["Bb0", "tb"], writes=["Bb0"])
            k.op("dve", lambda e: e.tensor_tensor(Bb[1].v(0, [[16, 128], [1, 16]]), Bsrc[1].v(0, [[16, 128], [1, 16]]), cb(0), ALU.mult),
                 reads=[("Bsrc", 1), "cc0"], writes=["Bb1"])
            k.op("dve", lambda e: e.tensor_tensor(tb.v(0, [[16, 128], [1, 16]]), Bsrc[0].v(0, [[16, 128], [1, 16]]), cb(1), ALU.mult),
                 reads=[("Bsrc", 0), "cc1"], writes=["tb"])
            k.op("dve", lambda e: e.tensor_tensor(Bb[1][:], Bb[1][:], tb[:], ALU.add), reads=["Bb1", "tb"], writes=["Bb1"])
            for i in range(2):
                k.dma("sp", DA(BBD, i * 64 * BN, [[BN, 64], [1, BN]]), Bb[i][:], reads=["Bb%d" % i], writes=["BBD"])
            cn = k.sb("cn", [128, 16, 64], F32)
            CT_ = k.sb("CT_", [64, BN], F32)
            for i, nm in enumerate(["s5_c_re", "s5_c_im"]):
                k.dma("sp", cn[:], DA(I[nm], 0, [[64, 128], [8192, 16], [1, 64]]), writes=["cn"])
                for a in range(16):
                    ps, pk = k.bank()
                    k.op("pe", lambda e, ps=ps, a=a: e.matmul(ps[0:64, 0:128], cn[:, a, :], ident_f[:], start=True, stop=True),
                         reads=["cn", "ident_f"], writes=[pk])
                    k.op("dve", lambda e, ps=ps, a=a: e.tensor_copy(CT_[:, a * 128:(a + 1) * 128], ps[0:64, 0:128]),
                         reads=[pk], writes=["CT_"])
                k.dma("sp", DA(CD, i * 64 * BN, [[BN, 64], [1, BN]]), CT_[:], reads=["CT_"], writes=["CD"])
            dump("PWD", DA(PWD, 0, [[3200, 128], [1, 3200]]), [128, 3200], F32)
            dump("BBD", DA(BBD, 0, [[2048, 128], [1, 2048]]), [128, 2048], F32)
            dump("CD", DA(CD, 0, [[2048, 128], [1, 2048]]), [128, 2048], F32)
            k.release()

        def s5_branch(l):
            k.mark()
            KIN = [k.sb("KIN%d" % d, [128, 16, 128], BF16) for d in range(2)]
            BM = [k.sb("BM%d" % d, [128, 16, 2, 64], BF16) for d in range(2)]
            CM = [k.sb("CM%d" % d, [64, 16, 2, 128], BF16) for d in range(2)]
            A8x = k.sb("A8x", [64, 2, 2, 16], F32)
            A8y = k.sb("A8y", [64, 2, 2, 16], F32)
            DSK = k.sb("DSK", [128, 2], F32)
            k.dma("sp", DSK.v(0, [[1, 2], [1, 1]]), DA(I["s5_d"], l * 256, [[1, 128], [128, 2], [1, 1]]), writes=["DSK"], allow_slow_non_contiguous=True)
            GB = k.sb("GB", [128, 2], F32)
            k.dma("sp", GB.v(0, [[1, 2], [1, 1]]), DA(I["s5_glu_b"], l * 256, [[1, 128], [128, 2], [1, 1]]), writes=["GB"], allow_slow_non_contiguous=True)
            GW = k.sb("GW", [128, 2, 256], BF16)
            wload(GW[:], I["s5_glu_w"], l * 65536, 256, 2, 256, "GW")
            uTp = k.sb("uTp", [128, 2, 8, 288], BF16)

            def ev_u(ci, tok0, ntok, ps, pk):
                nkb, k0 = ntok // 8, tok0 // 8
                copy_evac(uTp.v(ci * 8 * 288 + k0, [[1, nkb], [288, 8]]), ps.v(0, [[8, nkb], [1, 8]]), [pk], ["uTp"])
            proj_fm(l, 0, 256, ev_u)
            for cc in range(2):
                k.dma("sp", DA(UDP, cc * 128 * NTOK, [[NTOK, 128], [1, NTOK]]), uTp.v(cc * 2304, [[1, 2304]]), reads=both("uTp"), writes=["UDP"])
            U = k.sb("U", [128, 16, 288], BF16)
            for r in range(8):
                k.dma("sp", U.v(0, [[288, 16], [1, 288]], p0=r * 16, np_=16),
                      DA(UDP, r * 288, [[NTOK, 16], [16 * NTOK, 16], [1, 288]]), reads=["UDP"], writes=["U"])
            k.mark()
            PW = k.sb("PW", [64, 2, 32 * 25], F32)
            BB = k.sb("BB", [64, 2, 32 * 16], F32)
            CC = k.sb("CC", [64, 2, 32 * 16], F32)
            for i in range(2):
                k.dma("sp", PW[:, i, :], DA(PWD, i * 64 * 3200 + l * 800, [[3200, 64], [1, 800]]), reads=["PWD"], writes=["PW"])
                k.dma("sp", BB[:, i, :], DA(BBD, i * 64 * 2048 + l * 512, [[2048, 64], [1, 512]]), reads=["BBD"], writes=["BB"])
                k.dma("sp", CC[:, i, :], DA(CD, i * 64 * 2048 + l * 512, [[2048, 64], [1, 512]]), reads=["CD"], writes=["CC"])
            tri = k.sb("tri", [128, 2, 128], F32)
            k.dma("sp", tri.v(0, [[128, 2], [1, 128]]), DA(I["k_trimask"], 0, [[128, 128], [128 * 128, 2], [1, 128]]), writes=["tri"])
            Lr = k.sb("Lr", [64, 16, 128], F32); Li = k.sb("Li", [64, 16, 128], F32)
            Pr = k.sb("Pr", [64, 16, 128], F32); Pi = k.sb("Pi", [64, 16, 128], F32)
            Rr = k.sb("Rr", [64, 16, 128], F32); Ri = k.sb("Ri", [64, 16, 128], F32)
            ta = k.sb("ta", [64, 16, 128], F32); tb2 = k.sb("tb2", [64, 16, 128], F32)
            for d in range(2):
                def cmul(Xr, Xi, xk, nbase, S, neg_im, eng, d=d):
                    pw0 = PW.v(0 * 800 + d * 400 + nbase, [[25, 16], [1, 8], [0, 16]])
                    pw1 = PW.v(1 * 800 + d * 400 + nbase, [[25, 16], [1, 8], [0, 16]])
                    sv0 = S.v(0 * 512 + d * 256, [[16, 16], [0, 8], [1, 16]])
                    sv1 = S.v(1 * 512 + d * 256, [[16, 16], [0, 8], [1, 16]])
                    sk = "BB" if S is BB else "CC"
                    o4 = [[128, 16], [16, 8], [1, 16]]
                    k.op(eng, lambda e: e.tensor_tensor(Xr.v(0, o4), pw0, sv0, ALU.mult), reads=["PW", sk], writes=[xk + "r"])
                    k.op(eng, lambda e: e.tensor_tensor(ta.v(0, o4), pw1, sv1, ALU.mult), reads=["PW", sk], writes=["ta"])
                    k.op(eng, lambda e: e.tensor_tensor(Xr[:], Xr[:], ta[:], ALU.subtract), reads=[xk + "r", "ta"], writes=[xk + "r"])
                    k.op(eng, lambda e: e.tensor_tensor(Xi.v(0, o4), pw0, sv1, ALU.mult), reads=["PW", sk], writes=[xk + "i"])
                    k.op(eng, lambda e: e.tensor_tensor(tb2.v(0, o4), pw1, sv0, ALU.mult), reads=["PW", sk], writes=["tb2"])
                    if neg_im:
                        k.op(eng, lambda e: e.scalar_tensor_tensor(Xi[:], Xi[:], -1.0, tb2[:], ALU.mult, ALU.subtract),
                             reads=[xk + "i", "tb2"], writes=[xk + "i"])
                    else:
                        k.op(eng, lambda e: e.tensor_tensor(Xi[:], Xi[:], tb2[:], ALU.add), reads=[xk + "i", "tb2"], writes=[xk + "i"])
                cmul(Lr, Li, "L", 0, BB, False, "dve")
                cmul(Pr, Pi, "P", 8, BB, False, "dve")
                cmul(Rr, Ri, "R", 16, CC, True, "dve")
                for g in range(16):
                    ps, pk = k.bank()
                    k.op("pe", lambda e, ps=ps, g=g: e.matmul(ps[:, 0:128], Pr[:, g, :], Rr[:, g, :], start=True, stop=False),
                         reads=["Pr", "Rr"], writes=[pk], count=False)
                    k.op("pe", lambda e, ps=ps, g=g: e.matmul(ps[:, 0:128], Pi[:, g, :], Ri[:, g, :], start=False, stop=True),
                         reads=["Pi", "Ri"], writes=[pk])
                    k.op("dve", lambda e, ps=ps, g=g, d=d: e.tensor_tensor(KIN[d][:, g, :], ps[:, 0:128], tri[:, d, :], ALU.mult),
                         reads=[pk, "tri"], writes=[("KIN", d)])
                for g0 in range(0, 16, 4):
                    ps, pk = k.bank()
                    for gg in range(4):
                        for c, Lc, lk in ((0, Lr, "Lr"), (1, Li, "Li")):
                            s = gg * 2 + c
                            k.op("pe", lambda e, ps=ps, s=s, Lc=Lc, g=g0 + gg: e.matmul(
                                ps[:, s * 64:(s + 1) * 64], Lc[:, g, :], ident_f[0:64, 0:64], start=True, stop=True),
                                reads=[lk, "ident_f"], writes=[pk], count=(s == 7))
                    copy_evac(BM[d].v(g0 * 128, [[1, 512]]), ps[:, 0:512], [pk], [("BM", d)])
                k.op("act", lambda e, d=d: e.activation(CM[d].v(0, [[256, 16], [1, 128]]), Rr[:], AF.Copy), reads=["Rr"], writes=[("CM", d)])
                k.op("act", lambda e, d=d: e.activation(CM[d].v(128, [[256, 16], [1, 128]]), Ri[:], AF.Copy), reads=["Ri"], writes=[("CM", d)])
                i8 = 23 if d == 0 else 16
                for c in range(2):
                    k.op("dve", lambda e, d=d, c=c, i8=i8: e.tensor_copy(A8x.v((d * 2 + c) * 16, [[1, 16]]), PW.v(d * 400 + i8, [[25, 16]])),
                         reads=["PW"], writes=["A8"])
                k.op("dve", lambda e, d=d, i8=i8: e.tensor_copy(A8y.v((d * 2 + 1) * 16, [[1, 16]]), PW.v(800 + d * 400 + i8, [[25, 16]])),
                     reads=["PW"], writes=["A8"])
                k.op("dve", lambda e, d=d, i8=i8: e.tensor_scalar(A8y.v((d * 2) * 16, [[1, 16]]), PW.v(800 + d * 400 + i8, [[25, 16]]), -1.0, None, ALU.mult),
                     reads=["PW"], writes=["A8"])
            for d in range(2):
                dump("KIN%d" % d, KIN[d].v(0, [[1, 2048]]), [128, 2048], BF16)
                dump("BM%d" % d, BM[d].v(0, [[1, 2048]]), [128, 2048], BF16)
                dump("CM%d" % d, CM[d].v(0, [[1, 4096]]), [64, 4096], BF16)
            dump("A8x", A8x.v(0, [[1, 64]]), [64, 64], F32)
            dump("A8y", A8y.v(0, [[1, 64]]), [64, 64], F32)
            dump("Uu", U.v(0, [[1, 16 * 288]]), [128, 16 * 288], BF16)
            k.release()
            KP = 320
            XE = k.sb("XE", [64, 2, 16, KP], F32)
            EB = k.sb("EB", [64, 2, 16, 288], BF16)
            Ysb = k.sb("Ysb", [128, 16, 288], F32)
            P_ = k.sb("P_", [64, 2, 16, 10], F32)
            Q_ = k.sb("Q_", [64, 2, 16, 10], F32)
            W31x = k.sb("W31x", [64, 2, 16], F32)
            W31y = k.sb("W31y", [64, 2, 16], F32)
            TT = k.sb("TT", [64, 2, 16, 9], F32)
            P1 = k.sb("P1", [64, 2, 16], F32)
            Q1 = k.sb("Q1", [64, 2, 16], F32)
            MM = [k.sb("MM%d" % i, [64, 16, 64], F32) for i in range(4)]
            GK = 16 * KP
            RX = both("XE") + ["XE"]
            for d in range(2):
                for g in range(16):
                    for c in range(2):
                        ps, pk = k.bank()
                        k.op("pe", lambda e, ps=ps, d=d, g=g, c=c: e.matmul(ps[0:64, 0:288], BM[d][:, g, c, :], U[:, g, :], start=True, stop=True),
                             reads=both(("BM", d)) + ["U"], writes=[pk])
                        copy_evac(XE[:, c, g, 0:288], ps[0:64, 0:288], [pk], ["XE"])
                ppos = 288 if d == 0 else 319
                k.op("dve", lambda e: e.memset(XE.v(288, [[GK, 2], [KP, 16], [1, 32]]), 0.0), reads=RX, writes=["XE"])
                k.op("dve", lambda e, d=d, ppos=ppos: e.tensor_copy(XE.v(ppos, [[KP, 16]]), A8x.v(d * 32, [[1, 16]])), reads=RX + ["A8"], writes=["XE"])
                k.op("dve", lambda e, d=d, ppos=ppos: e.tensor_copy(XE.v(GK + ppos, [[KP, 16]]), A8y.v(d * 32 + 16, [[1, 16]])), reads=RX + ["A8"], writes=["XE"])
                sst = 32 if d == 0 else -32

                def pos(j, d=d):
                    return j if d == 0 else 319 - j
                for j in range(1, 32):
                    cur, prv = pos(j), pos(j - 1)
                    k.op("dve", lambda e, prv=prv, d=d, sst=sst: e.tensor_tensor(
                        P_[:], XE.v(prv, [[GK, 2], [KP, 16], [sst, 10]]), A8x.v(d * 32, [[16, 2], [1, 16], [0, 10]]), ALU.mult),
                        reads=RX + ["A8"], writes=["P_"])
                    k.op("dve", lambda e, prv=prv, d=d, sst=sst: e.tensor_tensor(
                        Q_[:], XE.v(GK + prv, [[-GK, 2], [KP, 16], [sst, 10]]), A8y.v(d * 32, [[16, 2], [1, 16], [0, 10]]), ALU.mult),
                        reads=RX + ["A8"], writes=["Q_"])
                    k.op("dve", lambda e: e.tensor_tensor(P_[:], P_[:], Q_[:], ALU.add), reads=["P_", "Q_"], writes=["P_"])
                    k.op("dve", lambda e, cur=cur, sst=sst: e.tensor_tensor(
                        XE.v(cur, [[GK, 2], [KP, 16], [sst, 10]]), XE.v(cur, [[GK, 2], [KP, 16], [sst, 10]]), P_[:], ALU.add),
                        reads=RX + ["P_"], writes=["XE"])
                w31 = 319 if d == 0 else 288
                k.op("dve", lambda e, w31=w31: e.tensor_copy(W31x.v(0, [[16, 2], [1, 16]]), XE.v(w31, [[0, 2], [KP, 16]])), reads=RX, writes=["W31"])
                k.op("dve", lambda e, w31=w31: e.tensor_copy(W31y.v(16, [[1, 16]]), XE.v(GK + w31, [[KP, 16]])), reads=RX, writes=["W31"])
                k.op("dve", lambda e, w31=w31: e.tensor_scalar(W31y.v(0, [[1, 16]]), XE.v(GK + w31, [[KP, 16]]), -1.0, None, ALU.mult), reads=RX, writes=["W31"])

                def kend(sg, d=d):
                    if d == 0:
                        return 32 * sg + 31
                    return 0 if sg == 0 else 288 - 32 * sg
                k.op("dve", lambda e, ke=kend(0): e.tensor_copy(TT.v(0, [[144, 2], [9, 16]]), XE.v(ke, [[GK, 2], [KP, 16]])), reads=RX, writes=["TT"])
                for sg in range(1, 9):
                    k.op("dve", lambda e, sg=sg: e.tensor_tensor(P1[:], TT.v(sg - 1, [[144, 2], [9, 16]]), W31x[:], ALU.mult), reads=["TT", "W31"], writes=["P1"])
                    k.op("dve", lambda e, sg=sg: e.tensor_tensor(Q1[:], TT.v(144 + sg - 1, [[-144, 2], [9, 16]]), W31y[:], ALU.mult), reads=["TT", "W31"], writes=["Q1"])
                    k.op("dve", lambda e: e.tensor_tensor(P1[:], P1[:], Q1[:], ALU.add), reads=["P1", "Q1"], writes=["P1"])
                    k.op("dve", lambda e, sg=sg, ke=kend(sg): e.tensor_tensor(TT.v(sg, [[144, 2], [9, 16]]), XE.v(ke, [[GK, 2], [KP, 16]]), P1[:], ALU.add),
                         reads=RX + ["P1"], writes=["TT"])
                for s0 in range(1, 9, 2):
                    if d == 0:
                        def xo(c, s0=s0):
                            return XE.v(c * GK + 32 * s0, [[KP, 16], [32, 2], [1, 32]])

                        def wv(c):
                            return XE.v(c * GK + 288, [[KP, 16], [0, 2], [1, 32]])
                    else:
                        def xo(c, s0=s0):
                            return XE.v(c * GK + 319 - 32 * s0, [[KP, 16], [-32, 2], [-1, 32]])

                        def wv(c):
                            return XE.v(c * GK + 319, [[KP, 16], [0, 2], [-1, 32]])

                    def tv(c, s0=s0):
                        return TT.v(c * 144 + s0 - 1, [[9, 16], [1, 2], [0, 32]])
                    m4 = [[64, 16], [32, 2], [1, 32]]
                    for eng, (ma, mb), co, (ca, cb), op2 in (("dve", (MM[0], MM[1]), 0, (0, 1), ALU.subtract),
                                                             ("dve", (MM[2], MM[3]), 1, (1, 0), ALU.add)):
                        mk = "MM%d" % co
                        wre, wim, ta_, tb_, xo_ = wv(0), wv(1), tv(ca), tv(cb), xo(co)
                        k.op(eng, lambda e, ma=ma, wre=wre, ta_=ta_: e.tensor_tensor(ma.v(0, m4), wre, ta_, ALU.mult), reads=RX + ["TT"], writes=[mk + "a"])
                        k.op(eng, lambda e, mb=mb, wim=wim, tb_=tb_: e.tensor_tensor(mb.v(0, m4), wim, tb_, ALU.mult), reads=RX + ["TT"], writes=[mk + "b"])
                        k.op(eng, lambda e, ma=ma, mb=mb, op2=op2: e.tensor_tensor(ma[:], ma[:], mb[:], op2), reads=[mk + "a", mk + "b"], writes=[mk + "a"])
                        k.op(eng, lambda e, ma=ma, xo_=xo_: e.tensor_tensor(xo_, xo_, ma.v(0, m4), ALU.add), reads=RX + [mk + "a"], writes=["XEc%d" % co])
                k.op("act", lambda e: e.activation(EB.v(0, [[288, 32], [1, 288]]), XE.v(0, [[KP, 32], [1, 288]]), AF.Copy), reads=RX + ["XEc0", "XEc1"], writes=["EB"])
                for g in range(16):
                    ps, pk = k.bank()
                    k.op("pe", lambda e, ps=ps, d=d, g=g: e.matmul(ps[:, 0:288], KIN[d][:, g, :], U[:, g, :], start=True, stop=False),
                         reads=[("KIN", d), "U"], writes=[pk], count=False)
                    if d == 0:
                        rngs = [(1, 0, 287)]
                    else:
                        rngs = [(0, 1, 31), (32, 33, 255), (287, 0, 1)]
                    nmm = len(rngs) * 2
                    im = 0
                    for (o0, s0, n) in rngs:
                        for c in range(2):
                            im += 1
                            k.op("pe", lambda e, ps=ps, d=d, g=g, c=c, o0=o0, s0=s0, n=n, last=(im == nmm): e.matmul(
                                ps[:, o0:o0 + n], CM[d][:, g, c, :], EB[:, c, g, s0:s0 + n], start=False, stop=last),
                                reads=[("CM", d), "EB"], writes=[pk], count=(im == nmm))
                    if d == 0:
                        copy_evac(Ysb[:, g, :], ps[:, 0:288], [pk], ["Ysb"])
                    else:
                        k.op("dve", lambda e, ps=ps, g=g: e.tensor_tensor(Ysb[:, g, :], ps[:, 0:288], Ysb[:, g, :], ALU.add),
                             reads=[pk] + both("Ysb"), writes=["Ysb"])
            for j in range(8):
                k.dma("sp", DA(YDP, j * 288, [[NTOK, 16], [16 * NTOK, 16], [1, 288]]),
                      Ysb.v(0, [[288, 16], [1, 288]], p0=j * 16, np_=16), reads=both("Ysb") + ["Ysb"], writes=["YDP"])
            dump("YDP%d" % l, DA(YDP, 0, [[NTOK, 256], [1, NTOK]]), [256, NTOK], F32)
            k.release()
            k.mark()
            uTp2 = k.sb("uTp2", [128, 2, 8, 288], BF16)
            for cc in range(2):
                k.dma("sp", uTp2.v(cc * 2304, [[1, 2304]]), DA(UDP, cc * 128 * NTOK, [[NTOK, 128], [1, NTOK]]), reads=["UDP"], writes=["uTp2"])
            yTp = k.sb("yTp", [128, 2, 8, 288], F32)
            for cc in range(2):
                k.dma("sp", yTp.v(cc * 2304, [[1, 2304]]), DA(YDP, cc * 128 * NTOK, [[NTOK, 128], [1, NTOK]]), reads=["YDP"], writes=["yTp"])
            DSK = k.sb("DSK", [128, 2], F32)
            k.dma("sp", DSK.v(0, [[1, 2], [1, 1]]), DA(I["s5_d"], l * 256, [[1, 128], [128, 2], [1, 1]]), writes=["DSK"], allow_slow_non_contiguous=True)
            GB = k.sb("GB", [128, 2], F32)
            k.dma("sp", GB.v(0, [[1, 2], [1, 1]]), DA(I["s5_glu_b"], l * 256, [[1, 128], [128, 2], [1, 1]]), writes=["GB"], allow_slow_non_contiguous=True)
            GW = k.sb("GW", [128, 2, 256], BF16)
            wload(GW[:], I["s5_glu_w"], l * 65536, 256, 2, 256, "GW")
            yl = k.sb("yl", [128, NTOK], F32)
            tt = k.sb("tt", [128, NTOK], F32)
            gT = k.sb("gT", [128, 2, NTOK], BF16)
            YA = k.sb("YA", [128, 2, NTOK], BF16)
            for cc in range(2):
                k.op("dve", lambda e, cc=cc: e.scalar_tensor_tensor(
                    yl.v(0, [[8, 288], [1, 8]]), uTp2.v(cc * 2304, [[1, 288], [288, 8]]), DSK[:, cc:cc + 1],
                    yTp.v(cc * 2304, [[1, 288], [288, 8]]), ALU.mult, ALU.add),
                    reads=["uTp2", "DSK", "yTp"], writes=["yl"])
                k.op("dve", lambda e: e.tensor_tensor(tt[:], yl[:], yl[:], ALU.mult), reads=["yl"], writes=["tt"])
                k.op("dve", lambda e: e.tensor_scalar(tt[:], tt[:], 0.044715, 1.0, ALU.mult, ALU.add), reads=["tt"], writes=["tt"])
                k.op("dve", lambda e: e.tensor_tensor(tt[:], tt[:], yl[:], ALU.mult), reads=["tt", "yl"], writes=["tt"])
                k.op("act", lambda e: e.activation(tt[:], tt[:], AF.Sigmoid, scale=1.5957691216057308), reads=["tt"], writes=["tt"])
                k.op("dve", lambda e, cc=cc: e.tensor_tensor(gT[:, cc, :], yl[:], tt[:], ALU.mult), reads=["yl", "tt"], writes=["gT"])
            sg = [k.sb("sg%d" % i, [128, 512], F32) for i in range(2)]
            n_ = 0
            for co in range(2):
                for (tok0, ntok) in MT:
                    ps, pk = k.bank()
                    for cc in range(2):
                        k.op("pe", lambda e, ps=ps, cc=cc, co=co, tok0=tok0, ntok=ntok: e.matmul(
                            ps[:, 0:ntok], GW[:, cc, co * 128:(co + 1) * 128], gT[:, cc, tok0:tok0 + ntok],
                            start=(cc == 0), stop=(cc == 1)), reads=["GW", "gT"], writes=[pk], count=(cc == 1))
                    s_ = sg[n_ % 2]; sk = ("sg", n_ % 2); n_ += 1
                    k.op("act", lambda e, ps=ps, s_=s_, co=co, ntok=ntok: e.activation(
                        s_[:, 0:ntok], ps[:, 0:ntok], AF.Sigmoid, bias=GB[:, co:co + 1]), reads=[pk, "GB"], writes=[sk])
                    k.op("dve", lambda e, s_=s_, co=co, tok0=tok0, ntok=ntok: e.tensor_tensor(
                        YA[:, co, tok0:tok0 + ntok], gT[:, co, tok0:tok0 + ntok], s_[:, 0:ntok], ALU.mult),
                        reads=[sk, "gT"], writes=["YA"])
            for cc in range(2):
                k.dma("sp", DA(YT, (0 * 256 + cc * 128) * NTOK, [[NTOK, 128], [1, NTOK]]), YA[:, cc, :], reads=["YA"], writes=[("YT", 0)])
            k.release()

        def conv_branch(l):
            k.mark()
            S3 = [k.sb("cv%d" % i, [128, 2, NTOK], BF16) for i in range(3)]

            def ev(ci, tok0, ntok, ps, pk):
                s, cc = ci // 2, ci % 2
                copy_evac(S3[s][:, cc, tok0:tok0 + ntok], ps[:, 0:ntok], [pk], [("cv", s)])
            proj_fm(l, 1024, 768, ev)
            CW = k.sb("CW", [128, 2, 3], F32)
            for cc in range(2):
                k.dma("sp", CW.v(cc * 3, [[1, 3], [1, 1]]), DA(I["conv_w"], l * 768 + cc * 128, [[1, 128], [256, 3], [1, 1]]),
                      writes=["CW"], allow_slow_non_contiguous=True)
            zz = k.sb("zz", [128, NTOK], F32)
            yy = k.sb("yy", [128, NTOK], F32)
            YC = k.sb("YC", [128, 2, NTOK], BF16)
            for cc in range(2):
                k.op("dve", lambda e, cc=cc: e.tensor_tensor(zz[:], S3[1][:, cc, :], S3[2][:, cc, :], ALU.mult),
                     reads=both(("cv", 1)) + both(("cv", 2)), writes=["zz"])
                k.op("dve", lambda e, cc=cc: e.tensor_scalar(yy[:], zz[:], CW[:, cc, 1:2], None, ALU.mult), reads=["zz", "CW"], writes=["yy"])
                for (a, b) in ((0, 256), (256, NTOK)):
                    k.op("dve", lambda e, cc=cc, a=a, b=b: e.scalar_tensor_tensor(
                        yy[:, a + 1:b], zz[:, a:b - 1], CW[:, cc, 0:1], yy[:, a + 1:b], ALU.mult, ALU.add),
                        reads=["zz", "CW", "yy"], writes=["yy"])
                    k.op("dve", lambda e, cc=cc, a=a, b=b: e.scalar_tensor_tensor(
                        yy[:, a:b - 1], zz[:, a + 1:b], CW[:, cc, 2:3], yy[:, a:b - 1], ALU.mult, ALU.add),
                        reads=["zz", "CW", "yy"], writes=["yy"])
                k.op("dve", lambda e, cc=cc: e.tensor_tensor(YC[:, cc, :], S3[0][:, cc, :], yy[:], ALU.mult),
                     reads=both(("cv", 0)) + ["yy"], writes=["YC"])
            for cc in range(2):
                k.dma("sp", DA(YT, (2 * 256 + cc * 128) * NTOK, [[NTOK, 128], [1, NTOK]]), YC[:, cc, :], reads=["YC"], writes=[("YT", 2)])
            k.release()

        def sgu_branch(l):
            k.mark()
            uT = k.sb("sguT", [128, 2, NTOK], BF16)

            def ev(ci, tok0, ntok, ps, pk):
                copy_evac(uT[:, ci, tok0:tok0 + ntok], ps[:, 0:ntok], [pk], ["sguT"])
            proj_fm(l, 1792, 256, ev)
            WV = k.sb("WV", [128, 8, 256], BF16)
            wload(WV[:], I["w_in"], l * 1024 * 6400 + 2048, 6400, 8, 256, "WV")
            LG = k.sb("LG", [128, 256], F32); LB = k.sb("LB", [128, 256], F32)
            k.dma("sp", LG[:], DA(I["sgu_ln_g"], l * 256, [[0, 128], [1, 256]]), writes=["LG"])
            k.dma("sp", LB[:], DA(I["sgu_ln_b"], l * 256, [[0, 128], [1, 256]]), writes=["LB"])
            SGB = k.sb("SGB", [128, 4, 128], F32)
            k.dma("sp", SGB[:], DA(I["sgu_b"], l * 512, [[0, 128], [1, 512]]), writes=["SGB"])
            wsn = k.sb("wsn", [128, 4, 128], F32)
            k.dma("sp", wsn[:], DA(I["sgu_w"], l * 4 * 16384, [[128, 128], [16384, 4], [1, 128]]), writes=["wsn"])
            WST = k.sb("WST", [128, 4, 128], BF16)
            ps, pk = k.bank()
            for g in range(4):
                k.op("pe", lambda e, ps=ps, g=g: e.matmul(ps[:, g * 128:(g + 1) * 128], wsn[:, g, :], ident_f[:], start=True, stop=True),
                     reads=["wsn", "ident_f"], writes=[pk])
            k.op("dve", lambda e, ps=ps: e.tensor_copy(WST[:], ps[:, 0:512]), reads=[pk], writes=["WST"])
            VN = k.sb("VN", [128, NTILE, 256], BF16)
            st_ = k.sb("st_", [128, NTILE, 4], F32)
            junk = k.sb("junk2", [128, 256], F32)
            VT = k.sb("VT", [128, NTILE, 256], F32)
            k.op("dve", lambda e: e.memset(st_[:], 0.0), writes=["st_"])
            for t in range(NTILE):
                ps, pk = k.bank()
                for kc in range(8):
                    k.op("pe", lambda e, ps=ps, kc=kc, t=t: e.matmul(ps[:, 0:256], alT[:, kc, t * 128:(t + 1) * 128], WV[:, kc, :],
                                                                    start=(kc == 0), stop=(kc == 7)),
                         reads=tkeys("alT", t * 128, 128) + ["WV"], writes=[pk], count=(kc == 7))
                k.op("act", lambda e, ps=ps, t=t: e.activation(VT[:, t, :], ps[:, 0:256], AF.Copy, accum_out=st_[:, t, 0:1]),
                     reads=[pk, "st_"], writes=[("VT", t), "st_"])
                k.op("act", lambda e, t=t: e.activation(junk[:], VT[:, t, :], AF.Square, accum_out=st_[:, t, 1:2]),
                     reads=[("VT", t), "st_"], writes=["junk2", "st_"])
            def sv_(c0, n=1):
                return st_.v(c0, [[4, NTILE], [1, n]])
            k.op("dve", lambda e: e.tensor_scalar(sv_(0, 2), sv_(0, 2), 1.0 / 256, None, ALU.mult), reads=["st_"], writes=["st_"])
            k.op("dve", lambda e: e.tensor_tensor(sv_(2), sv_(0), sv_(0), ALU.mult), reads=["st_"], writes=["st_"])
            k.op("dve", lambda e: e.tensor_tensor(sv_(2), sv_(1), sv_(2), ALU.subtract), reads=["st_"], writes=["st_"])
            k.op("dve", lambda e: e.tensor_scalar(sv_(2), sv_(2), 1e-6, None, ALU.add), reads=["st_"], writes=["st_"])
            k.op("act", lambda e: e.activation(sv_(2), sv_(2), AF.Sqrt), reads=["st_"], writes=["st_"])
            k.op("dve", lambda e: e.reciprocal(sv_(2), sv_(2)), reads=["st_"], writes=["st_"])
            for t in range(NTILE):
                k.op("dve", lambda e, t=t: e.tensor_scalar(VT[:, t, :], VT[:, t, :], st_[:, t, 0:1], st_[:, t, 2:3], ALU.subtract, ALU.mult),
                     reads=[("VT", t), "st_"], writes=[("VT", t)])
                k.op("dve", lambda e, t=t: e.tensor_tensor(VT[:, t, :], VT[:, t, :], LG[:], ALU.mult), reads=[("VT", t), "LG"], writes=[("VT", t)])
                k.op("dve", lambda e, t=t: e.tensor_tensor(VN[:, t, :], VT[:, t, :], LB[:], ALU.add), reads=[("VT", t), "LB"], writes=[("VN", t)])
            YD = k.sb("YD", [128, 2, NTOK], BF16)
            zt = [k.sb("zt%d" % i, [128, 128], F32) for i in range(2)]
            n_ = 0
            for t in range(NTILE):
                for cc in range(2):
                    ps, pk = k.bank()
                    for gl in range(2):
                        g = 2 * cc + gl
                        k.op("pe", lambda e, ps=ps, gl=gl, g=g, t=t, cc=cc: e.matmul(
                            ps[:, gl * 128:(gl + 1) * 128], VN[:, t, cc * 128:(cc + 1) * 128], WST[:, g, :], start=True, stop=True),
                            reads=[("VN", t), "WST"], writes=[pk])
                    z_ = zt[n_ % 2]; zk = ("zt", n_ % 2); n_ += 1
                    for gl in range(2):
                        g = 2 * cc + gl
                        p0 = 64 * gl
                        k.op("dve", lambda e, ps=ps, z_=z_, gl=gl, g=g, p0=p0: e.tensor_tensor(
                            z_[p0:p0 + 64, :], ps[p0:p0 + 64, gl * 128:(gl + 1) * 128], SGB[p0:p0 + 64, g, :], ALU.add),
                            reads=[pk, "SGB"], writes=[zk])
                    k.op("dve", lambda e, z_=z_, t=t, cc=cc: e.tensor_tensor(
                        YD[:, cc, t * 128:(t + 1) * 128], z_[:], uT[:, cc, t * 128:(t + 1) * 128], ALU.mult),
                        reads=[zk] + both("sguT"), writes=["YD"])
            for cc in range(2):
                k.dma("sp", DA(YT, (3 * 256 + cc * 128) * NTOK, [[NTOK, 128], [1, NTOK]]), YD[:, cc, :], reads=["YD"], writes=[("YT", 3)])
            k.release()

        def rp_setup():
            k.mark()
            negt = k.sb("negt", [121, 160], F32)
            k.op("dve", lambda e: e.memset(negt[:], NEG), writes=["negt"])
            for hlf in range(2):
                k.dma("sp", DA(RP, hlf * 120 * 160, [[160, 121], [1, 160]]), negt[:], reads=["negt"], writes=["RP"])
            k.dma("sp", DA(RP, 64, [[160, 240], [1, 31]]), DA(I["na_rpb"], 0, [[31, 240], [1, 31]]), reads=["RP"], writes=["RP"])
            k.release()

        def attn_branch(l):
            k.mark()
            qT = k.sb("qT", [128, 2, NTOK], BF16)
            kT = k.sb("kT", [128, 2, NTOK], BF16)

            def ev(ci, tok0, ntok, ps, pk):
                if ci < 2:
                    copy_evac(qT[:, ci, tok0:tok0 + ntok], ps[:, 0:ntok], [pk], ["qT"], scale=0.125)
                else:
                    copy_evac(kT[:, ci - 2, tok0:tok0 + ntok], ps[:, 0:ntok], [pk], ["kT"])
            proj_fm(l, 256, 512, ev)
            if CUT == 21:
                k.release(); return
            WV = k.sb("WVa", [128, 8, 256], BF16)
            wload(WV[:], I["w_in"], l * 1024 * 6400 + 768, 6400, 8, 256, "WVa")
            NVT = 18 + 15
            VP = k.sb("VP", [128, NVT, 4, 128], BF16)
            k.op("dve", lambda e: e.memset(VP[:], 0.0), writes=both("VP"))
            if CUT == 25:
                k.release(); return
            starts = [t * 128 for t in range(18)] + [320 + 128 * m for m in range(15)]
            if CUT == 26:
                starts = starts[:18]
            if CUT == 27:
                starts = starts[:1]
            for vi, s0 in enumerate(starts):
                ps, pk = k.bank()
                for kc in range(8):
                    k.op("pe", lambda e, ps=ps, kc=kc, s0=s0: e.matmul(ps[:, 0:256], alT[:, kc, s0:s0 + 128], WV[:, kc, :],
                                                                      start=(kc == 0), stop=(kc == 7)),
                         reads=tkeys("alT", s0, 128) + ["WVa"], writes=[pk], count=(kc == 7))
                copy_evac(VP.v(vi * 512, [[256, 2], [1, 64]]), ps.v(0, [[128, 2], [1, 64]]), [pk], ["VP"])
                copy_evac(VP.v(vi * 512 + 128 + 64, [[256, 2], [1, 64]]), ps.v(64, [[128, 2], [1, 64]]), [pk], ["VP"])
            if CUT == 22:
                k.release(); return
            ONP = k.sb("ONP", [128, 2, 128], BF16)
            k.op("dve", lambda e: e.memset(ONP[:], 0.0), writes=["ONP"])
            k.op("dve", lambda e: e.memset(ONP[:, 0, 0:64], 1.0), writes=["ONP"])
            k.op("dve", lambda e: e.memset(ONP[:, 1, 64:128], 1.0), writes=["ONP"])
            Wt = k.sb("Wt", [128, 60, 64], F32)
            k.dma("sp", Wt.v(0, [[64, 60], [1, 64]], p0=0, np_=64), DA(RP, l * 60 * 160 + 16, [[1, 64], [160, 60], [1, 64]]), reads=["RP"], writes=["Wt"])
            k.dma("sp", Wt.v(0, [[64, 60], [1, 64]], p0=64, np_=64), DA(RP, l * 60 * 160 + 160 + 16, [[1, 64], [160, 60], [1, 64]]), reads=["RP"], writes=["Wt"])
            cmk = k.sb("cmk", [128, 64], F32)
            k.dma("sp", cmk[:], I["k_colmask"].ap(), writes=["cmk"])
            RPT = k.sb("RPT", [128, 60, 64], F32)
            k.op("dve", lambda e: e.tensor_tensor(RPT.v(0, [[64, 60], [1, 64]]), Wt.v(63, [[64, 60], [-1, 64]]), cmk.v(0, [[0, 60], [1, 64]]), ALU.add),
                 reads=["Wt", "cmk"], writes=["RPT"])
            if CUT == 23:
                k.release(); return
            YB = k.sb("YB", [128, 2, NTOK], BF16)
            tmpS = [k.sb("tmpS%d" % i, [128, 4, 64], F32) for i in range(2)]
            Pm = [k.sb("Pm%d" % i, [128, 6, 64], BF16) for i in range(2)]
            rec = [k.sb("rec%d" % i, [128, 2, 64], F32) for i in range(2)]
            n_ = 0

            def a_scores(r, h):
                q0 = 256 + 64 * r
                rs_ = min(max(r - 4, 0), 24)
                cc, ph = h // 2, 64 * (h % 2)
                psS, pkS = k.bank()
                ktoks = [256 + 64 * (rs_ + 2 * c) for c in range(4)] + [0, 128]
                for c in range(6):
                    k.op("pe", lambda e, psS=psS, c=c, cc=cc, ph=ph, kt=ktoks[c], q0=q0: e.matmul(
                        psS[:, c * 64:(c + 1) * 64], kT[ph:ph + 64, cc, kt:kt + 128], qT[ph:ph + 64, cc, q0:q0 + 64],
                        start=True, stop=True), reads=both("kT") + both("qT"), writes=[pkS], count=(c == 5))
                return (r, h, psS, pkS, ktoks)

            rowst = {}

            def a_rest(st, b):
                r, h, psS, pkS, ktoks = st
                q0 = 256 + 64 * r
                rs_ = min(max(r - 4, 0), 24)
                dr0 = rs_ - r + 7
                cc = h // 2
                if h == 0:
                    rowst[r] = (k.bank(), k.bank())
                (psO, pkO), (psD, pkD) = rowst[r]
                k.op("dve", lambda e: e.tensor_tensor(
                    tmpS[b].v(0, [[64, 4], [1, 64]]), psS.v(0, [[64, 4], [1, 64]]),
                    RPT.v((h * 15 + dr0) * 64, [[128, 4], [1, 64]]), ALU.add),
                    reads=[pkS, "RPT"], writes=[("tmpS", b)])
                k.op("act", lambda e: e.activation(Pm[b].v(0, [[1, 256]]), tmpS[b].v(0, [[1, 256]]), AF.Exp),
                     reads=[("tmpS", b)], writes=[("Pm", b)])
                k.op("act", lambda e: e.activation(Pm[b].v(256, [[1, 128]]), psS[:, 256:384], AF.Exp),
                     reads=[pkS], writes=[("Pm", b)])
                for c in range(6):
                    kt = ktoks[c]
                    if kt < 256:
                        vi = kt // 128
                    elif (kt - 256) % 128 == 0:
                        vi = kt // 128
                    else:
                        vi = 18 + (kt - 320) // 128
                    first = (h % 2 == 0 and c == 0)
                    last = (h % 2 == 1 and c == 5)
                    k.op("pe", lambda e, c=c, vi=vi, first=first, last=last: e.matmul(
                        psO[:, cc * 64:cc * 64 + 64], VP[:, vi, h, :], Pm[b][:, c, :], start=first, stop=last),
                        reads=both("VP") + [("Pm", b)], writes=[pkO], count=last)
                    k.op("pe", lambda e, c=c, first=first, last=last: e.matmul(
                        psD[:, cc * 64:cc * 64 + 64], ONP[:, h % 2, :], Pm[b][:, c, :], start=first, stop=last),
                        reads=["ONP", ("Pm", b)], writes=[pkD], count=last)
                if h == 3:
                    rb = r % 2
                    k.op("dve", lambda e: e.reciprocal(rec[rb].v(0, [[1, 128]]), psD[:, 0:128]),
                         reads=[pkD], writes=[("rec", rb)])
                    k.op("dve", lambda e: e.tensor_tensor(
                        YB.v(q0, [[NTOK, 2], [1, 64]]), psO.v(0, [[64, 2], [1, 64]]), rec[rb].v(0, [[64, 2], [1, 64]]), ALU.mult),
                        reads=[pkO, ("rec", rb)], writes=["YB"])

            items = [(r, h) for r in range(32) for h in range(4)]
            st = a_scores(*items[0])
            for i in range(len(items)):
                nxt = a_scores(*items[i + 1]) if i + 1 < len(items) else None
                a_rest(st, i % 2)
                st = nxt
            if CUT == 24:
                k.release(); return
            q0 = None
            Pc = [k.sb("Pc%d" % i, [128, 2, 256], BF16) for i in range(2)]
            recc = k.sb("recc", [128, 256], F32)
            for cc in range(2):
                psO, pkO = k.bank()
                psD, pkD = k.bank()
                for hh in range(2):
                    h = cc * 2 + hh
                    ph = 64 * hh
                    psS, pkS = k.bank()
                    for c in range(2):
                        k.op("pe", lambda e, psS=psS, c=c, cc=cc, ph=ph: e.matmul(
                            psS[:, c * 256:(c + 1) * 256], kT[ph:ph + 64, cc, c * 128:(c + 1) * 128], qT[ph:ph + 64, cc, 0:256],
                            start=True, stop=True), reads=both("kT") + both("qT"), writes=[pkS], count=(c == 1))
                    k.op("act", lambda e, hh=hh, psS=psS: e.activation(Pc[hh].v(0, [[1, 512]]), psS[:, 0:512], AF.Exp),
                         reads=[pkS], writes=[("Pc", hh)])
                    for c in range(2):
                        first = (hh == 0 and c == 0)
                        last = (hh == 1 and c == 1)
                        k.op("pe", lambda e, psO=psO, hh=hh, c=c, h=h, first=first, last=last: e.matmul(
                            psO[:, 0:256], VP[:, c, h, :], Pc[hh][:, c, :], start=first, stop=last),
                            reads=both("VP") + [("Pc", hh)], writes=[pkO], count=last)
                        k.op("pe", lambda e, psD=psD, hh=hh, c=c, first=first, last=last: e.matmul(
                            psD[:, 0:256], ONP[:, hh, :], Pc[hh][:, c, :], start=first, stop=last),
                            reads=["ONP", ("Pc", hh)], writes=[pkD], count=last)
                k.op("dve", lambda e, psD=psD: e.reciprocal(recc[:], psD[:, 0:256]), reads=[pkD], writes=["recc"])
                k.op("dve", lambda e, psO=psO, cc=cc: e.tensor_tensor(YB[:, cc, 0:256], psO[:, 0:256], recc[:], ALU.mult),
                     reads=[pkO, "recc"], writes=["YB"])
            for cc in range(2):
                k.dma("sp", DA(YT, (1 * 256 + cc * 128) * NTOK, [[NTOK, 128], [1, NTOK]]), YB[:, cc, :], reads=["YB"], writes=[("YT", 1)])
            k.release()

        def merge_phase(l):
            k.mark()
            YS = k.sb("YS", [128, 8, NTOK], BF16)
            for br in range(4):
                for cc in range(2):
                    k.dma("sp", YS[:, br * 2 + cc, :], DA(YT, (br * 256 + cc * 128) * NTOK, [[NTOK, 128], [1, NTOK]]),
                          reads=[("YT", br)], writes=["YS"])
            mT = k.sb("mT", [128, 8, NTOK], BF16)
            Wg = [k.sb("Wg%d" % i, [128, 8, 4, 128], BF16) for i in range(2)]
            Wb = [k.sb("Wb%d" % i, [128, 4, 2, 128], BF16) for i in range(2)]
            sgt = [k.sb("sgt%d" % i, [128, 512], F32) for i in range(2)]
            acc = [k.sb("acc%d" % i, [128, 512], F32) for i in range(2)]
            n_ = 0
            def load_dc(dc):
                i = dc % 2
                for br in range(4):
                    k.dma("pool", Wg[i].v(br * 128, [[512, 8], [1, 128]]),
                          DA(I["w_in"], l * 1024 * 6400 + 2304 + br * 1024 + dc * 128, [[6400, 128], [128 * 6400, 8], [1, 128]]),
                          writes=[("Wg", i)])
                k.dma("pool", Wb[i].v(0, [[128, 8], [1, 128]]),
                      DA(I["w_branch"], l * 4 * 256 * 1024 + dc * 128, [[1024, 128], [128 * 1024, 8], [1, 128]]), writes=[("Wb", i)])
            load_dc(0)
            for dc in range(8):
                i = dc % 2
                if dc + 1 < 8:
                    load_dc(dc + 1)
                for mi, (tok0, ntok) in enumerate(MT):
                    a_ = acc[mi % 2]; ak = ("acc", mi % 2)
                    for br in range(4):
                        psA, pkA = k.bank()
                        for kc in range(8):
                            k.op("pe", lambda e, psA=psA, i=i, kc=kc, br=br, tok0=tok0, ntok=ntok: e.matmul(
                                psA[:, 0:ntok], Wg[i][:, kc, br, :], alT[:, kc, tok0:tok0 + ntok], start=(kc == 0), stop=(kc == 7)),
                                reads=[("Wg", i)] + tkeys("alT", tok0, ntok), writes=[pkA], count=(kc == 7))
                        psB, pkB = k.bank()
                        for cc in range(2):
                            k.op("pe", lambda e, psB=psB, i=i, cc=cc, br=br, tok0=tok0, ntok=ntok: e.matmul(
                                psB[:, 0:ntok], Wb[i][:, br, cc, :], YS[:, br * 2 + cc, tok0:tok0 + ntok], start=(cc == 0), stop=(cc == 1)),
                                reads=[("Wb", i), "YS"], writes=[pkB], count=(cc == 1))
                        s_ = sgt[n_ % 2]; sk = ("sgt", n_ % 2); n_ += 1
                        k.op("act", lambda e, psA=psA, s_=s_, ntok=ntok: e.activation(s_[:, 0:ntok], psA[:, 0:ntok], AF.Sigmoid),
                             reads=[pkA], writes=[sk])
                        if br == 0:
                            k.op("dve", lambda e, psB=psB, s_=s_, a_=a_, ntok=ntok: e.tensor_tensor(a_[:, 0:ntok], psB[:, 0:ntok], s_[:, 0:ntok], ALU.mult),
                                 reads=[pkB, sk], writes=[ak])
                        else:
                            k.op("dve", lambda e, psB=psB, s_=s_, ntok=ntok: e.tensor_tensor(s_[:, 0:ntok], psB[:, 0:ntok], s_[:, 0:ntok], ALU.mult),
                                 reads=[pkB, sk], writes=[sk])
                            if br < 3:
                                k.op("dve", lambda e, s_=s_, a_=a_, ntok=ntok: e.tensor_tensor(a_[:, 0:ntok], a_[:, 0:ntok], s_[:, 0:ntok], ALU.add),
                                     reads=[ak, sk], writes=[ak])
                            else:
                                k.op("dve", lambda e, s_=s_, a_=a_, dc=dc, tok0=tok0, ntok=ntok: e.tensor_tensor(
                                    mT[:, dc, tok0:tok0 + ntok], a_[:, 0:ntok], s_[:, 0:ntok], ALU.add),
                                    reads=[ak, sk], writes=tkeys("mT", tok0, ntok))
            WO = k.sb("WO", [128, 8, 1024], BF16)
            for hh in range(2):
                wload(WO.v(hh * 512, [[1024, 8], [1, 512]]), I["w_out"], l * 1024 * 1024 + hh * 512, 1024, 8, 512, "WO")
            GATE = k.sb("GATE", [128, 2, 1024], F32)
            for w in range(2):
                k.dma("sp", GATE[:, w, :], DA(MODROW, (l * 2 + w) * 6144 + 2 * 1024, [[0, 128], [1, 1024]]), reads=[("MODROW", l)], writes=["GATE"])
            HT = [k.sb("HT%d" % i, [128, 1024], F32) for i in range(2)]
            tm = [k.sb("tm%d" % i, [128, 512], F32) for i in range(2)]
            n_ = 0
            for t in range(NTILE):
                b = t % 2
                w = 1 if t < 2 else 0
                k.dma("sp", HT[b][:], DA(H, t * 128 * 1024, [[1024, 128], [1, 1024]]), reads=[("H", t)], writes=[("HT", b)])
                for hh in range(2):
                    ps, pk = k.bank()
                    for dc in range(8):
                        k.op("pe", lambda e, ps=ps, dc=dc, t=t, hh=hh: e.matmul(
                            ps[:, 0:512], mT[:, dc, t * 128:(t + 1) * 128], WO[:, dc, hh * 512:(hh + 1) * 512], start=(dc == 0), stop=(dc == 7)),
                            reads=tkeys("mT", t * 128, 128) + ["WO"], writes=[pk], count=(dc == 7))
                    tq = tm[n_ % 2]; tk = ("tm", n_ % 2); n_ += 1
                    k.op("dve", lambda e, ps=ps, tq=tq, w=w, hh=hh: e.tensor_tensor(tq[:], ps[:, 0:512], GATE[:, w, hh * 512:(hh + 1) * 512], ALU.mult),
                         reads=[pk, "GATE"], writes=[tk])
                    k.op("dve", lambda e, tq=tq, b=b, hh=hh: e.tensor_tensor(HT[b][:, hh * 512:(hh + 1) * 512], HT[b][:, hh * 512:(hh + 1) * 512], tq[:], ALU.add),
                         reads=[tk, ("HT", b)], writes=[("HT", b)])
                k.dma("sp", DA(H, t * 128 * 1024, [[1024, 128], [1, 1024]]), HT[b][:], reads=[("HT", b)], writes=[("H", t)])
            k.release()

        def ffn_phase(l):
            k.mark()
            W1 = [k.sb("W1_%d" % i, [128, 8, 2048], BF16) for i in range(2)]
            W2 = k.sb("W2", [128, 16, 1024], BF16)

            def load_w1(fh, bi):
                for q in range(4):
                    wload(W1[bi].v(q * 512, [[2048, 8], [1, 512]]), I["w_ff1"], l * 1024 * 4096 + fh * 2048 + q * 512, 4096, 8, 512, ("W1", bi))

            def load_w2(fh):
                for q in range(2):
                    wload(W2.v(q * 512, [[1024, 16], [1, 512]]), I["w_ff2"], l * 4096 * 1024 + fh * 2048 * 1024 + q * 512, 1024, 16, 512, "W2")
            load_w1(0, 0)
            load_w2(0)
            norm_phase(l, 1, alT, "blT")
            GATE = k.sb("GATE5", [128, 2, 1024], F32)
            for w in range(2):
                k.dma("sp", GATE[:, w, :], DA(MODROW, (l * 2 + w) * 6144 + 5 * 1024, [[0, 128], [1, 1024]]), reads=[("MODROW", l)], writes=["GATE5"])
            hT = k.sb("hT", [128, 16, 512], BF16)
            rl = [k.sb("rl%d" % i, [128, 512], F32) for i in range(2)]
            rd = [k.sb("rd%d" % i, [128, 512], BF16) for i in range(2)]
            HT = [k.sb("HTf%d" % i, [128, 1024], F32) for i in range(2)]
            tm = [k.sb("tmf%d" % i, [128, 512], F32) for i in range(2)]
            n_ = 0
            n2 = 0
            for fh in range(2):
                W1c = W1[fh]
                w1k = ("W1", fh)
                if fh == 0:
                    load_w1(1, 1)
                else:
                    load_w2(1)
                for (tok0, ntok) in MT:
                    for fc in range(16):
                        ps, pk = k.bank()
                        for kc in range(8):
                            k.op("pe", lambda e, ps=ps, kc=kc, fc=fc, tok0=tok0, ntok=ntok, W1c=W1c: e.matmul(
                                ps[:, 0:ntok], W1c[:, kc, fc * 128:(fc + 1) * 128], alT[:, kc, tok0:tok0 + ntok], start=(kc == 0), stop=(kc == 7)),
                                reads=[w1k] + tkeys("blT", tok0, ntok), writes=[pk], count=(kc == 7))
                        if fc % 2 == 0:
                            rd_ = rd[(fc // 2) % 2]; rdk = ("rd", (fc // 2) % 2)
                            k.op("dve", lambda e, ps=ps, rd_=rd_, ntok=ntok: e.tensor_scalar(rd_[:, 0:ntok], ps[:, 0:ntok], 0.0, None, ALU.max), reads=[pk], writes=[rdk])
                            k.op("dve", lambda e, rd_=rd_, fc=fc, ntok=ntok: e.tensor_tensor(hT[:, fc, 0:ntok], rd_[:, 0:ntok], rd_[:, 0:ntok], ALU.mult),
                                 reads=[rdk], writes=[("hT", 0)])
                        else:
                            r_ = rl[n_ % 2]; rk = ("rl", n_ % 2); n_ += 1
                            k.op("act", lambda e, ps=ps, r_=r_, ntok=ntok: e.activation(r_[:, 0:ntok], ps[:, 0:ntok], AF.Relu), reads=[pk], writes=[rk])
                            k.op("act", lambda e, r_=r_, fc=fc, ntok=ntok: e.activation(hT[:, fc, 0:ntok], r_[:, 0:ntok], AF.Square),
                                 reads=[rk], writes=[("hT", 1)])
                    for tt_ in range(ntok // 128):
                        t = tok0 // 128 + tt_
                        b = t % 2
                        w = 1 if t < 2 else 0
                        k.dma("sp", HT[b][:], DA(H, t * 128 * 1024, [[1024, 128], [1, 1024]]), reads=[("H", t)], writes=[("HTf", b)])
                        for hh in range(2):
                            ps, pk = k.bank()
                            for fc in range(16):
                                k.op("pe", lambda e, ps=ps, fc=fc, tt_=tt_, hh=hh: e.matmul(
                                    ps[:, 0:512], hT[:, fc, tt_ * 128:(tt_ + 1) * 128], W2[:, fc, hh * 512:(hh + 1) * 512],
                                    start=(fc == 0), stop=(fc == 15)), reads=both("hT") + ["W2"], writes=[pk], count=(fc == 15))
                            tq = tm[n2 % 2]; tk = ("tmf", n2 % 2); n2 += 1
                            k.op("dve", lambda e, ps=ps, tq=tq, w=w, hh=hh: e.tensor_tensor(tq[:], ps[:, 0:512], GATE[:, w, hh * 512:(hh + 1) * 512], ALU.mult),
                                 reads=[pk, "GATE5"], writes=[tk])
                            k.op("dve", lambda e, tq=tq, b=b, hh=hh: e.tensor_tensor(HT[b][:, hh * 512:(hh + 1) * 512], HT[b][:, hh * 512:(hh + 1) * 512], tq[:], ALU.add),
                                 reads=[tk, ("HTf", b)], writes=[("HTf", b)])
                        k.dma("sp", DA(H, t * 128 * 1024, [[1024, 128], [1, 1024]]), HT[b][:], reads=[("HTf", b)], writes=[("H", t)])
            k.release()

        def final_phase():
            k.mark()
            FG = k.sb("FG", [128, 1024], F32)
            k.dma("sp", FG[:], DA(I["final_g"], 0, [[0, 128], [1, 1024]]), writes=["FG"])
            XT = [k.sb("XTf%d" % i, [128, 1024], F32) for i in range(2)]
            junk = k.sb("junkf", [128, 1024], BF16)
            ss = k.sb("ssf", [128, NTILE], F32)
            k.op("dve", lambda e: e.memset(ss[:], 0.0), writes=["ssf"])
            for t in range(2, NTILE):
                b = t % 2
                xt = XT[b]
                k.dma("sp", xt[:], DA(H, t * 128 * 1024, [[1024, 128], [1, 1024]]), reads=[("H", t)], writes=[("xtf", b)])
                k.op("act", lambda e, xt=xt, t=t: e.activation(junk[:], xt[:], AF.Square, accum_out=ss[:, t:t + 1]),
                     reads=[("xtf", b), "ssf"], writes=["junkf", "ssf"])
                k.op("dve", lambda e, t=t: e.tensor_scalar(ss[:, t:t + 1], ss[:, t:t + 1], 1.0 / 1024, 1e-6, ALU.mult, ALU.add), reads=["ssf"], writes=["ssf"])
                k.op("act", lambda e, t=t: e.activation(ss[:, t:t + 1], ss[:, t:t + 1], AF.Sqrt), reads=["ssf"], writes=["ssf"])
                k.op("dve", lambda e, t=t: e.reciprocal(ss[:, t:t + 1], ss[:, t:t + 1]), reads=["ssf"], writes=["ssf"])
                k.op("dve", lambda e, xt=xt, t=t: e.scalar_tensor_tensor(xt[:], xt[:], ss[:, t:t + 1], FG[:], ALU.mult, ALU.mult),
                     reads=[("xtf", b), "ssf", "FG"], writes=[("xtf", b)])
                k.dma("sp", DA(OUT, (t - 2) * 128 * 1024, [[1024, 128], [1, 1024]]), xt[:], reads=[("xtf", b)], writes=[("OUT", t)])
            k.release()

        def program():
            stages = [("M", phase_M), ("s5setup", s5_setup), ("rpsetup", rp_setup)]
            for l in range(nl):
                stages += [
                    ("norm1_%d" % l, lambda l=l: (norm_phase(l, 0, alT, "alT"), dump("alT%d" % l, alT.v(0, [[1, 8 * NTOK]]), [128, 8 * NTOK], BF16))),
                    ("s5_%d" % l, lambda l=l: (s5_branch(l), dump("YT%d_s5" % l, DA(YT, 0, [[NTOK, 1024], [1, NTOK]]), [1024, NTOK], BF16))),
                    ("attn_%d" % l, lambda l=l: (attn_branch(l), dump("YT%d_attn" % l, DA(YT, 0, [[NTOK, 1024], [1, NTOK]]), [1024, NTOK], BF16))),
                    ("conv_%d" % l, lambda l=l: (conv_branch(l), dump("YT%d_conv" % l, DA(YT, 0, [[NTOK, 1024], [1, NTOK]]), [1024, NTOK], BF16))),
                    ("sgu_%d" % l, lambda l=l: (sgu_branch(l), dump("YT%d_sgu" % l, DA(YT, 0, [[NTOK, 1024], [1, NTOK]]), [1024, NTOK], BF16))),
                    ("merge_%d" % l, lambda l=l: (merge_phase(l), dump("hmid%d" % l, DA(H, 0, [[1024, NTOK], [1, 1024]]), [NTOK, 1024], F32))),
                    ("ffn_%d" % l, lambda l=l: (ffn_phase(l), dump("hend%d" % l, DA(H, 0, [[1024, NTOK], [1, 1024]]), [NTOK, 1024], F32))),
                ]
            stages.append(("final", final_phase))
            for name, fn in stages:
                if skip and name.split("_")[0] in skip:
                    continue
                fn()
                if name == stop:
                    break
        program()
        k.emit()
    return nc, dbg_out


_CACHE = {}


def kernel(**inputs):
    n = 8
    consts = host_constants()
    if "nc" not in _CACHE:
        _CACHE["nc"] = build()[0]
    nc = _CACHE["nc"]
    shared = {}
    for name in INPUT_SHAPES:
        if name in ("x", "c", "ctx"):
            continue
        src = consts[name] if name in consts else inputs[name]
        shared[name] = np.ascontiguousarray(np.asarray(src, dtype=np.float32))
    in_maps = []
    for b in range(n):
        m = dict(shared)
        m["x"] = np.ascontiguousarray(np.asarray(inputs["x"][b], dtype=np.float32))
        m["c"] = np.ascontiguousarray(np.asarray(inputs["c"][b], dtype=np.float32))
        m["ctx"] = np.ascontiguousarray(np.asarray(inputs["ctx"][b], dtype=np.float32))
        in_maps.append(m)
    res = run_bass_kernel_spmd(nc, in_maps, core_ids=list(range(n)))
    return np.stack([np.asarray(r["out"], dtype=np.float32) for r in res.results], axis=0)
```

```python
import os
import numpy as np
from contextlib import ExitStack
CUT = int(os.environ.get('CUT', '0'))
ACT_EVAC = os.environ.get('ACT_EVAC', '1') == '1'
import concourse.bass as bass
import concourse.mybir as mybir
from concourse.bass_utils import run_bass_kernel_spmd

F32 = mybir.dt.float32
BF16 = mybir.dt.bfloat16
I32 = mybir.dt.int32
AF = mybir.ActivationFunctionType
ALU = mybir.AluOpType

NDSEM = 8
SAME_ENGINE_SYNC = True
PI = float(np.pi)

NTOK = 2304
NTILE = 18
MT = [(0, 512), (512, 512), (1024, 512), (1536, 512), (2048, 256)]
NEG = -30000.0


class Buf:
    __slots__ = ("w", "r")

    def __init__(self):
        self.w = None
        self.r = {}


class T:
    def __init__(self, h, shape):
        self.h = h
        self.shape = list(shape)
        self.F = int(np.prod(shape[1:]))

    def __getitem__(self, idx):
        return self.h[idx]

    def v(self, off, dims, p0=0, np_=None):
        if np_ is None:
            np_ = self.shape[0] - p0
        return bass.AP(self.h, p0 * self.F + off, [[self.F, np_]] + [list(d) for d in dims])


def DA(h, off, dims):
    return bass.AP(h, off, [list(d) for d in dims])


class K:
    ENGS = ("pe", "act", "dve", "pool", "sp")
    DMAQ = ("sp", "act", "pool")

    def __init__(self, nc, stack):
        self.nc = nc
        self.stack = stack
        self.prog = {e: [] for e in self.ENGS}
        self.sem = {e: stack.enter_context(nc.semaphore("s_" + e)) for e in self.ENGS}
        self.cnt = {e: 0 for e in self.ENGS}
        self.seen = {e: {f: 0 for f in self.ENGS} for e in self.ENGS}
        self.dsem = {q: [stack.enter_context(nc.semaphore("d_%s%d" % (q, j))) for j in range(NDSEM)]
                     for q in self.DMAQ}
        self.dtarget = {q: [0] * NDSEM for q in self.DMAQ}
        self.dnext = {q: 0 for q in self.DMAQ}
        self.dseen = {e: {} for e in self.ENGS}
        self.bufs = {}
        self.sb_off = 16640
        self.sb_marks = []
        self.uid = 0
        self.nbank = 0
        self.banks = []

    def sb(self, name, shape, dtype, parts=None):
        esz = 2 if dtype == BF16 else 4
        nbytes = int(np.prod(shape[1:])) * esz
        off = (self.sb_off + 63) // 64 * 64
        self.uid += 1
        h = self.nc.alloc_sbuf_tensor_at("%s_%d" % (name, self.uid), list(shape), dtype, offset=off)
        self.sb_off = off + nbytes
        assert self.sb_off <= 229376, ("SBUF overflow", name, self.sb_off)
        return T(h, shape)

    def mark(self):
        self.sb_marks.append(self.sb_off)

    def release(self):
        self.barrier()
        self.sb_off = self.sb_marks.pop()

    def bank(self):
        i = self.nbank % len(self.banks)
        self.nbank += 1
        return self.banks[i], ("ps", i)

    def _buf(self, k):
        b = self.bufs.get(k)
        if b is None:
            b = self.bufs[k] = Buf()
        return b

    def _wait(self, e, tok):
        if tok[0] == "eng":
            _, f, n = tok
            if f == e and (e == "pe" or not SAME_ENGINE_SYNC):
                return
            if self.seen[e][f] >= n:
                return
            self.seen[e][f] = n
            sem = self.sem[f]
            self.prog[e].append(lambda eng, sem=sem, n=n: eng.wait_ge(sem, n))
        else:
            _, q, j, tgt = tok
            if self.dseen[e].get((q, j), 0) >= tgt:
                return
            self.dseen[e][(q, j)] = tgt
            sem = self.dsem[q][j]
            self.prog[e].append(lambda eng, sem=sem, tgt=tgt: eng.wait_ge(sem, tgt))

    def _deps(self, e, reads, writes):
        toks = []
        for k in reads:
            b = self._buf(k)
            if b.w is not None:
                toks.append(b.w)
        for k in writes:
            b = self._buf(k)
            if b.w is not None:
                toks.append(b.w)
            toks.extend(b.r.values())
        for t in toks:
            self._wait(e, t)

    def _commit(self, tok, reads, writes):
        for k in reads:
            b = self._buf(k)
            if tok[0] == "eng":
                b.r[("eng", tok[1])] = tok
            else:
                b.r[("dma", tok[1], tok[2])] = tok
        for k in writes:
            b = self._buf(k)
            b.w = tok
            b.r = {}

    def op(self, e, fn, reads=(), writes=(), count=True):
        psr = [r for r in reads if isinstance(r, tuple) and r[0] in ("ps", "psT")]
        if psr:
            reads = [r for r in reads if r not in psr]
            writes = list(writes) + psr
        self._deps(e, reads, writes)
        if not count:
            self.prog[e].append(lambda eng, fn=fn: fn(eng))
            self._commit(("eng", e, self.cnt[e] + 1), reads, writes)
            return
        self.cnt[e] += 1
        n = self.cnt[e]
        sem = self.sem[e]
        self.prog[e].append(lambda eng, fn=fn, sem=sem: fn(eng).then_inc(sem, 1))
        self._commit(("eng", e, n), reads, writes)

    def dma(self, q, out, in_, reads=(), writes=(), **kw):
        self._deps(q, reads, writes)
        j = self.dnext[q]
        self.dnext[q] = (j + 1) % NDSEM
        prev = self.dtarget[q][j]
        if prev > 0:
            self._wait(q, ("dma", q, j, prev))
        self.dtarget[q][j] = prev + 16
        sem = self.dsem[q][j]
        self.prog[q].append(lambda eng, out=out, in_=in_, sem=sem, kw=kw:
                            eng.dma_start(out=out, in_=in_, **kw).then_inc(sem, 16))
        self._commit(("dma", q, j, prev + 16), reads, writes)

    def barrier(self):
        for e in self.ENGS:
            for f in self.ENGS:
                if f != e and self.cnt[f] > 0:
                    self._wait(e, ("eng", f, self.cnt[f]))
            for q in self.DMAQ:
                for j in range(NDSEM):
                    if self.dtarget[q][j] > 0:
                        self._wait(e, ("dma", q, j, self.dtarget[q][j]))
        self.bufs = {}

    def emit(self):
        self.barrier()
        nc = self.nc
        prog = self.prog
        with nc.Block() as block:
            @block.tensor
            def _(eng):
                for f in prog["pe"]:
                    f(eng)

            @block.scalar
            def _(eng):
                for f in prog["act"]:
                    f(eng)

            @block.vector
            def _(eng):
                for f in prog["dve"]:
                    f(eng)

            @block.gpsimd
            def _(eng):
                for f in prog["pool"]:
                    f(eng)

            @block.sync
            def _(eng):
                for f in prog["sp"]:
                    f(eng)


INPUT_SHAPES = {
    "x": [2048, 1024], "c": [1024], "ctx": [256, 1024], "c_ctx": [1024],
    "ada_w": [4, 1024, 6144], "ada_b": [4, 6144], "norm_g": [4, 2, 1024], "final_g": [1024],
    "w_in": [4, 1024, 6400], "w_branch": [4, 4, 256, 1024], "w_out": [4, 1024, 1024],
    "s5_lam_re": [4, 2, 16, 64], "s5_lam_im": [4, 2, 16, 64], "s5_log_dt": [4, 2, 16],
    "s5_b_re": [4, 2, 16, 64, 16], "s5_b_im": [4, 2, 16, 64, 16],
    "s5_c_re": [4, 2, 16, 16, 64], "s5_c_im": [4, 2, 16, 16, 64],
    "s5_d": [4, 256], "s5_glu_w": [4, 256, 256], "s5_glu_b": [4, 256],
    "na_rpb": [4, 4, 15, 31], "conv_w": [4, 3, 256], "sgu_ln_g": [4, 256], "sgu_ln_b": [4, 256],
    "sgu_w": [4, 4, 128, 128], "sgu_b": [4, 4, 128], "w_ff1": [4, 1024, 4096], "w_ff2": [4, 4096, 1024],
    "k_ident": [128, 128], "k_colmask": [128, 64], "k_trimask": [2, 128, 128], "k_s5nt": [2, 25],
}


def host_constants():
    ident = np.eye(128, dtype=np.float32)
    j = np.arange(64)
    col_start = np.clip(j - 8, 0, 48)
    col_ok = (j[None, :] >= col_start[:, None]) & (j[None, :] < col_start[:, None] + 16)
    cm = np.where(col_ok.T, 0.0, NEG).astype(np.float32)
    colmask = np.concatenate([cm, cm], 0)
    r = np.arange(128) // 16
    tri = np.stack([(r[None, :] >= r[:, None]), (r[:, None] >= r[None, :])]).astype(np.float32)
    nt = np.zeros((2, 25), np.float32)
    rr = np.arange(8)
    nt[0, 0:8] = 7 - rr; nt[0, 8:16] = -1 - rr; nt[0, 16:24] = rr + 1; nt[0, 24] = 1
    nt[1, 0:8] = rr; nt[1, 8:16] = rr - 8; nt[1, 16:24] = 8 - rr; nt[1, 24] = 1
    return {"k_ident": ident, "k_colmask": colmask, "k_trimask": tri, "k_s5nt": nt}


def build(nl=4, dbg=(), stop=None, skip=()):
    nc = bass.Bass("TRN2", target_bir_lowering=False)
    I = {n: nc.dram_tensor(n, s, F32, kind="ExternalInput") for n, s in INPUT_SHAPES.items()}
    OUT = nc.dram_tensor("out", [2048, 1024], F32, kind="ExternalOutput")
    dbg_out = {}

    def scratch(name, shape, dt=F32):
        return nc.dram_tensor(name, list(shape), dt)

    H = scratch("H", [NTOK, 1024])
    MODROW = scratch("MODROW", [4, 2, 6144])
    PWD = scratch("PWD", [2, 64, 128 * 25])
    BBD = scratch("BBD", [2, 64, 128 * 16])
    CD = scratch("CD", [2, 64, 128 * 16])
    UDP = scratch("UDP", [256, NTOK], BF16)
    YDP = scratch("YDP", [256, NTOK])
    YT = scratch("YT", [4, 256, NTOK], BF16)
    RP = scratch("RP", [4 * 4 * 15 + 1, 160])

    with ExitStack() as st:
        k = K(nc, st)
        for i in range(6):
            k.banks.append(T(st.enter_context(nc.psum_tensor("psb%d" % i, [128, 512], F32)), [128, 512]))
        psT = [T(st.enter_context(nc.psum_tensor("psT%d" % i, [128, 1024], BF16)), [128, 1024]) for i in range(2)]

        def dump(name, src_ap, shape, dt=F32, reads=()):
            if name not in dbg:
                return
            t = nc.dram_tensor("dbg_" + name, list(shape), dt, kind="ExternalOutput")
            dbg_out[name] = t
            k.barrier()
            k.dma("sp", t.ap(), src_ap, reads=reads)
            k.barrier()

        ident_f = k.sb("ident_f", [128, 128], F32)
        ident_b = k.sb("ident_b", [128, 128], BF16)
        k.dma("sp", ident_f[:], I["k_ident"].ap(), writes=["ident_f"])
        k.dma("pool", ident_b[:], I["k_ident"].ap(), writes=["ident_b"])
        MODC = k.sb("MODC", [128, 4, 4, 8, 2], F32)
        GS = k.sb("GS", [128, 4, 2, 8, 2], F32)
        alT = k.sb("alT", [128, 8, NTOK], BF16)

        def phase_M():
            k.dma("sp", DA(H, 0, [[1024, 256], [1, 1024]]), I["ctx"].ap(), writes=[("H", t) for t in range(2)])
            k.dma("sp", DA(H, 256 * 1024, [[1024, 2048], [1, 1024]]), I["x"].ap(), writes=[("H", t) for t in range(2, 18)])

            if CUT == 1:
                return
            k.mark()
            scraw = k.sb("scraw", [128, 8, 2], F32)
            SC = k.sb("SC", [128, 8, 2], F32)
            k.dma("sp", scraw.v(0, [[2, 8], [1, 1]]), DA(I["c"], 0, [[1, 128], [128, 8], [1, 1]]), writes=["scraw"], allow_slow_non_contiguous=True)
            k.dma("sp", scraw.v(1, [[2, 8], [1, 1]]), DA(I["c_ctx"], 0, [[1, 128], [128, 8], [1, 1]]), writes=["scraw"], allow_slow_non_contiguous=True)
            k.op("act", lambda e: e.activation(SC[:], scraw[:], AF.Silu), reads=["scraw"], writes=["SC"])
            NG = k.sb("NG", [128, 8, 8], F32)
            for lj in range(8):
                k.dma("sp", NG.v(lj * 8, [[1, 8], [1, 1]]), DA(I["norm_g"], lj * 1024, [[1, 128], [128, 8], [1, 1]]),
                      writes=["NG"], allow_slow_non_contiguous=True)
            if CUT == 2:
                k.release(); return
            modrow = k.sb("modrow", [2, 6144], F32)
            adab = k.sb("adab", [2, 6144], F32)
            adaw = [k.sb("adaw%d" % i, [128, 8, 512], F32) for i in range(4)]
            for l in range(nl):
                k.dma("sp", adab[:], DA(I["ada_b"], l * 6144, [[0, 2], [1, 6144]]), writes=["adab"])
                for n in range(12):
                    wt = adaw[n % 4]
                    wk = ("adaw", n % 4)
                    k.dma("sp", wt[:], DA(I["ada_w"], l * 1024 * 6144 + n * 512, [[6144, 128], [128 * 6144, 8], [1, 512]]),
                          writes=[wk])
                    ps, pk = k.bank()
                    for kc in range(8):
                        k.op("pe", lambda e, ps=ps, wt=wt, kc=kc: e.matmul(ps[0:2, 0:512], SC[:, kc, :], wt[:, kc, :],
                                                                          start=(kc == 0), stop=(kc == 7)),
                             reads=[wk, "SC"], writes=[pk], count=(kc == 7))
                    k.op("dve", lambda e, ps=ps, n=n: e.tensor_tensor(modrow[0:2, n * 512:(n + 1) * 512], ps[0:2, 0:512],
                                                                     adab[0:2, n * 512:(n + 1) * 512], ALU.add),
                         reads=[pk, "adab"], writes=["modrow"])
                if CUT == 3:
                    k.release(); return
                k.dma("sp", DA(MODROW, l * 2 * 6144, [[6144, 2], [1, 6144]]), modrow[:], reads=["modrow"],
                      writes=[("MODROW", l)])
                if CUT == 4:
                    k.release(); return
                ps, pk = k.bank()
                for mi, m in enumerate([0, 1, 3, 4]):
                    for kc in range(8):
                        k.op("pe", lambda e, ps=ps, mi=mi, m=m, kc=kc: e.matmul(
                            ps[:, (mi * 8 + kc) * 2:(mi * 8 + kc) * 2 + 2],
                            modrow[0:2, m * 1024 + kc * 128:m * 1024 + kc * 128 + 128], ident_f[0:2, 0:2],
                            start=True, stop=True), reads=["modrow", "ident_f"], writes=[pk], count=(mi == 3 and kc == 7))
                k.op("dve", lambda e, ps=ps, l=l: e.tensor_copy(MODC.v(l * 64, [[1, 64]]), ps[:, 0:64]),
                     reads=[pk], writes=["MODC"])
                if CUT == 5:
                    k.release(); return
                for j in range(2):
                    k.op("dve", lambda e, l=l, j=j: e.scalar_tensor_tensor(
                        GS.v((l * 2 + j) * 16, [[2, 8], [1, 2]]), MODC.v(l * 64 + (2 * j + 1) * 16, [[2, 8], [1, 2]]), 1.0,
                        NG.v((l * 2 + j) * 8, [[1, 8], [0, 2]]), ALU.add, ALU.mult),
                        reads=["MODC", "NG"], writes=["GS"])
            k.release()


        def SHv(l, j, kc, w):
            return MODC.v(l * 64 + (2 * j) * 16 + kc * 2 + w, [[1, 1]])

        def GSv(l, j, kc, w):
            return GS.v((l * 2 + j) * 16 + kc * 2 + w, [[1, 1]])

        def norm_phase(l, j, dst, dkey):
            k.mark()
            NB = 6
            XA = k.sb("XA", [128, NB, 1024], F32)
            xn = [k.sb("xn%d" % i, [128, 1024], BF16) for i in range(2)]
            junk = k.sb("junk", [128, 1024], BF16)
            ss = k.sb("ss", [128, NTILE], F32)
            rs = k.sb("rs", [128, NTILE], F32)
            k.op("dve", lambda e: e.memset(ss[:], 0.0), writes=["ss"])
            for t0 in range(0, NTILE, NB):
                for t in range(t0, t0 + NB):
                    k.dma("sp" if t % 2 == 0 else "act", XA[:, t - t0, :], DA(H, t * 128 * 1024, [[1024, 128], [1, 1024]]),
                          reads=[("H", t)], writes=[("xa", t - t0)])
                for t in range(t0, t0 + NB):
                    k.op("act", lambda e, t=t, t0=t0: e.activation(junk[:], XA[:, t - t0, :], AF.Square, accum_out=ss[:, t:t + 1]),
                         reads=[("xa", t - t0), "ss"], writes=["junk", "ss"])
                k.op("dve", lambda e, t0=t0: e.tensor_scalar(rs[:, t0:t0 + NB], ss[:, t0:t0 + NB], 1.0 / 1024, 1e-6, ALU.mult, ALU.add), reads=["ss"], writes=["rs"])
                k.op("act", lambda e, t0=t0: e.activation(rs[:, t0:t0 + NB], rs[:, t0:t0 + NB], AF.Sqrt), reads=["rs"], writes=["rs"])
                k.op("dve", lambda e, t0=t0: e.reciprocal(rs[:, t0:t0 + NB], rs[:, t0:t0 + NB]), reads=["rs"], writes=["rs"])
                for t in range(t0, t0 + NB):
                    b = t % 2
                    k.op("dve", lambda e, b=b, t=t, t0=t0: e.tensor_scalar(xn[b][:], XA[:, t - t0, :], rs[:, t:t + 1], None, ALU.mult),
                         reads=[("xa", t - t0), "rs"], writes=[("xn", b)])
                    for kc in range(8):
                        k.op("pe", lambda e, b=b, kc=kc: e.transpose(psT[b][:, kc * 128:(kc + 1) * 128],
                                                                     xn[b][:, kc * 128:(kc + 1) * 128], ident_b[:]),
                             reads=[("xn", b), "ident_b"], writes=[("psT", b)], count=(kc == 7))
                    w = 1 if t < 2 else 0
                    for kc in range(8):
                        o = dst[:, kc, t * 128:(t + 1) * 128]
                        i_ = psT[b][:, kc * 128:(kc + 1) * 128]
                        if b == 0 or not ACT_EVAC:
                            k.op("dve", lambda e, o=o, i_=i_, kc=kc, w=w: e.tensor_scalar(
                                o, i_, GSv(l, j, kc, w), SHv(l, j, kc, w), ALU.mult, ALU.add),
                                reads=[("psT", b), "GS", "MODC"], writes=[(dkey, t, 0)])
                        else:
                            k.op("act", lambda e, o=o, i_=i_, kc=kc, w=w: e.activation(
                                o, i_, AF.Identity, bias=SHv(l, j, kc, w), scale=GSv(l, j, kc, w)),
                                reads=[("psT", b), "GS", "MODC"], writes=[(dkey, t, 0)])
            k.release()

        def tkeys(key, tok0, ntok):
            return [(key, t, p) for t in range(tok0 // 128, (tok0 + ntok + 127) // 128) for p in range(2)]

        def both(key):
            return [(key, 0), (key, 1)]

        def wload(wb_ap, handle, off, row_stride, nk, ncols, key, kstride=None):
            if kstride is None:
                kstride = 128 * row_stride
            k.dma("pool", wb_ap, DA(handle, off, [[row_stride, 128], [kstride, nk], [1, ncols]]), writes=[key])

        wbi = [0]

        def proj_fm(l, col0, ncols, evac, srcT=alT, skey="alT"):
            cbs = list(range(0, ncols, 256))
            slots = {}
            WB = [k.sb("WB%d" % i, [128, 8, 256], BF16) for i in range(2)]

            def pf(cb):
                nb = min(256, ncols - cb)
                i = wbi[0] % 2
                wbi[0] += 1
                wload(WB[i].v(0, [[256, 8], [1, nb]]), I["w_in"], l * 1024 * 6400 + col0 + cb, 6400, 8, nb, ("WB", i))
                slots[cb] = i
            pf(cbs[0])
            for ci_, cb in enumerate(cbs):
                nb = min(256, ncols - cb)
                if ci_ + 1 < len(cbs):
                    pf(cbs[ci_ + 1])
                i = slots[cb]
                wb = WB[i]
                for sub in range(0, nb, 128):
                    ci = (cb + sub) // 128
                    for (tok0, ntok) in MT:
                        ps, pk = k.bank()
                        for kc in range(8):
                            k.op("pe", lambda e, ps=ps, wb=wb, kc=kc, sub=sub, tok0=tok0, ntok=ntok: e.matmul(
                                ps[:, 0:ntok], wb[:, kc, sub:sub + 128], srcT[:, kc, tok0:tok0 + ntok],
                                start=(kc == 0), stop=(kc == 7)),
                                reads=[("WB", i)] + tkeys(skey, tok0, ntok), writes=[pk], count=(kc == 7))
                        evac(ci, tok0, ntok, ps, pk)

        evi = [0]

        def copy_evac(out_ap, in_ap, reads, writes, scale=None):
            evi[0] += 1
            writes = [(w_, evi[0] % 2) for w_ in writes]
            fe = os.environ.get("EVAC", "")
            if (evi[0] % 2 == 0 or fe == "dve") and fe != "act":
                if scale is None:
                    k.op("dve", lambda e: e.tensor_copy(out_ap, in_ap), reads=reads, writes=writes)
                else:
                    k.op("dve", lambda e: e.tensor_scalar(out_ap, in_ap, scale, None, ALU.mult), reads=reads, writes=writes)
            else:
                if scale is None:
                    k.op("act", lambda e: e.activation(out_ap, in_ap, AF.Copy), reads=reads, writes=writes)
                else:
                    k.op("act", lambda e: e.activation(out_ap, in_ap, AF.Copy, scale=scale), reads=reads, writes=writes)

        def s5_setup():
            k.mark()
            nat = k.sb("nat", [128, 64], F32)
            lam = [k.sb("lam%d" % i, [64, 128], F32) for i in range(2)]
            for i, nm in enumerate(["s5_lam_re", "s5_lam_im"]):
                k.dma("sp", nat[:], DA(I[nm], 0, [[64, 128], [1, 64]]), writes=["nat"])
                ps, pk = k.bank()
                k.op("pe", lambda e, ps=ps: e.matmul(ps[0:64, 0:128], nat[:], ident_f[:], start=True, stop=True),
                     reads=["nat", "ident_f"], writes=[pk])
                k.op("dve", lambda e, ps=ps, i=i: e.tensor_copy(lam[i][:], ps[0:64, 0:128]), reads=[pk], writes=[("lam", i)])
            dt = k.sb("dt", [64, 128], F32)
            k.dma("sp", dt[:], DA(I["s5_log_dt"], 0, [[0, 64], [1, 128]]), writes=["dt"])
            k.op("act", lambda e: e.activation(dt[:], dt[:], AF.Exp), reads=["dt"], writes=["dt"])
            z = [k.sb("z%d" % i, [64, 128], F32) for i in range(2)]
            for i in range(2):
                k.op("dve", lambda e, i=i: e.tensor_tensor(z[i][:], lam[i][:], dt[:], ALU.mult),
                     reads=[("lam", i), "dt"], writes=[("z", i)])
            ntb = k.sb("ntb", [64, 50], F32)
            k.dma("sp", ntb[:], DA(I["k_s5nt"], 0, [[0, 64], [1, 50]]), writes=["ntb"])
            NN = 128 * 25
            ZR = k.sb("ZR", [64, NN], F32)
            ZI = k.sb("ZI", [64, NN], F32)
            for zi_, Zt, zk in ((0, ZR, "ZR"), (1, ZI, "ZI")):
                for d in range(2):
                    k.op("dve", lambda e, zi_=zi_, Zt=Zt, d=d: e.tensor_tensor(
                        Zt.v(d * 16 * 25, [[32 * 25, 4], [25, 16], [1, 25]]),
                        z[zi_].v(d * 16, [[32, 4], [1, 16], [0, 25]]),
                        ntb.v(d * 25, [[0, 4], [0, 16], [1, 25]]), ALU.mult),
                        reads=[("z", zi_), "ntb"], writes=[zk])
            MAG = k.sb("MAG", [64, NN], F32)
            k.op("act", lambda e: e.activation(MAG[:], ZR[:], AF.Exp), reads=["ZR"], writes=["MAG"])
            t1 = k.sb("t1", [64, NN], F32)
            ti = k.sb("ti", [64, NN], I32)
            TR = [k.sb("TR%d" % i, [64, NN], F32) for i in range(2)]

            def trig(dst, dk, shift):
                k.op("dve", lambda e: e.tensor_scalar(dst[:], ZI[:], shift, None, ALU.add), reads=["ZI"], writes=[dk])
                k.op("dve", lambda e: e.tensor_scalar(t1[:], dst[:], 1.0 / (2 * PI), None, ALU.mult), reads=[dk], writes=["t1"])
                k.op("dve", lambda e: e.tensor_copy(ti[:], t1[:]), reads=["t1"], writes=["ti"])
                k.op("dve", lambda e: e.tensor_copy(t1[:], ti[:]), reads=["ti"], writes=["t1"])
                k.op("dve", lambda e: e.scalar_tensor_tensor(dst[:], t1[:], -2 * PI, dst[:], ALU.mult, ALU.add),
                     reads=["t1", dk], writes=[dk])
                k.op("dve", lambda e: e.tensor_scalar(dst[:], dst[:], -3.1415925, 3.1415925, ALU.max, ALU.min),
                     reads=[dk], writes=[dk])
                k.op("act", lambda e: e.activation(dst[:], dst[:], AF.Sin), reads=[dk], writes=[dk])
            trig(TR[0], "TR0", PI / 2)
            trig(TR[1], "TR1", 0.0)
            for i in range(2):
                k.op("dve", lambda e, i=i: e.tensor_tensor(TR[i][:], TR[i][:], MAG[:], ALU.mult),
                     reads=[("TR%d" % i), "MAG"], writes=[("TR%d" % i)])
                k.dma("sp", DA(PWD, i * 64 * NN, [[NN, 64], [1, NN]]), TR[i][:], reads=["TR%d" % i], writes=["PWD"])
            def a_v(i):
                return TR[i].v(24, [[25, 128]])
            nr = k.sb("nr", [64, 128], F32)
            den = k.sb("den", [64, 128], F32)
            tq = k.sb("tq", [64, 128], F32)
            cc_ = [k.sb("cc%d" % i, [64, 128], F32) for i in range(2)]
            k.op("dve", lambda e: e.tensor_scalar(nr[:], a_v(0), -1.0, None, ALU.add), reads=["TR0"], writes=["nr"])
            k.op("dve", lambda e: e.tensor_tensor(den[:], lam[0][:], lam[0][:], ALU.mult), reads=[("lam", 0)], writes=["den"])
            k.op("dve", lambda e: e.tensor_tensor(tq[:], lam[1][:], lam[1][:], ALU.mult), reads=[("lam", 1)], writes=["tq"])
            k.op("dve", lambda e: e.tensor_tensor(den[:], den[:], tq[:], ALU.add), reads=["den", "tq"], writes=["den"])
            k.op("dve", lambda e: e.reciprocal(den[:], den[:]), reads=["den"], writes=["den"])
            k.op("dve", lambda e: e.tensor_tensor(cc_[0][:], nr[:], lam[0][:], ALU.mult), reads=["nr", ("lam", 0)], writes=["cc0"])
            k.op("dve", lambda e: e.tensor_tensor(tq[:], a_v(1), lam[1][:], ALU.mult), reads=["TR1", ("lam", 1)], writes=["tq"])
            k.op("dve", lambda e: e.tensor_tensor(cc_[0][:], cc_[0][:], tq[:], ALU.add), reads=["cc0", "tq"], writes=["cc0"])
            k.op("dve", lambda e: e.tensor_tensor(cc_[0][:], cc_[0][:], den[:], ALU.mult), reads=["cc0", "den"], writes=["cc0"])
            k.op("dve", lambda e: e.tensor_tensor(cc_[1][:], a_v(1), lam[0][:], ALU.mult), reads=["TR1", ("lam", 0)], writes=["cc1"])
            k.op("dve", lambda e: e.tensor_tensor(tq[:], nr[:], lam[1][:], ALU.mult), reads=["nr", ("lam", 1)], writes=["tq"])
            k.op("dve", lambda e: e.tensor_tensor(cc_[1][:], cc_[1][:], tq[:], ALU.subtract), reads=["cc1", "tq"], writes=["cc1"])
            k.op("dve", lambda e: e.tensor_tensor(cc_[1][:], cc_[1][:], den[:], ALU.mult), reads=["cc1", "den"], writes=["cc1"])
            BN = 128 * 16
            Bsrc = [k.sb("Bsrc%d" % i, [64, BN], F32) for i in range(2)]
            for i, nm in enumerate(["s5_b_re", "s5_b_im"]):
                for l4 in range(4):
                    k.dma("sp", Bsrc[i].v(l4 * 512, [[16, 32], [1, 16]]),
                          DA(I[nm], l4 * 32 * 1024, [[16, 64], [1024, 32], [1, 16]]), writes=[("Bsrc", i)])
            Bb = [k.sb("Bb%d" % i, [64, BN], F32) for i in range(2)]
            tb = k.sb("tb", [64, BN], F32)

            def cb(i):
                return cc_[i].v(0, [[1, 128], [0, 16]])
            k.op("dve", lambda e: e.tensor_tensor(Bb[0].v(0, [[16, 128], [1, 16]]), Bsrc[0].v(0, [[16, 128], [1, 16]]), cb(0), ALU.mult),
                 reads=[("Bsrc", 0), "cc0"], writes=["Bb0"])
            k.op("dve", lambda e: e.tensor_tensor(tb.v(0, [[16, 128], [1, 16]]), Bsrc[1].v(0, [[16, 128], [1, 16]]), cb(1), ALU.mult),
                 reads=[("Bsrc", 1), "cc1"], writes=["tb"])
            k.op("dve", lambda e: e.tensor_tensor(Bb[0][:], Bb[0][:], tb[:], ALU.subtract), reads=["Bb0", "tb"], writes=["Bb0"])
            k.op("dve", lambda e: e.tensor_tensor(Bb[1].v(0, [[16, 128], [1, 16]]), Bsrc[1].v(0, [[16, 128], [1, 16]]), cb(0), ALU.mult),
                 reads=[("Bsrc", 1), "cc0"], writes=["Bb1"])
            k.op("dve", lambda e: e.tensor_tensor(tb.v(0, [[16, 128], [1, 16]]), Bsrc[0].v(0, [[16, 128], [1, 16]]), cb(1), ALU.mult),
                 reads=[("Bsrc", 0), "cc1"], writes=["tb"])
            k.op("dve", lambda e: e.tensor_tensor(Bb[1][:], Bb[1][:], tb[:], ALU.add), reads=["Bb1", "tb"], writes=["Bb1"])
            for i in range(2):
                k.dma("sp", DA(BBD, i * 64 * BN, [[BN, 64], [1, BN]]), Bb[i][:], reads=["Bb%d" % i], writes=["BBD"])
            cn = k.sb("cn", [128, 16, 64], F32)
            CT_ = k.sb("CT_", [64, BN], F32)
            for i, nm in enumerate(["s5_c_re", "s5_c_im"]):
                k.dma("sp", cn[:], DA(I[nm], 0, [[64, 128], [8192, 16], [1, 64]]), writes=["cn"])
                for a in range(16):
                    ps, pk = k.bank()
                    k.op("pe", lambda e, ps=ps, a=a: e.matmul(ps[0:64, 0:128], cn[:, a, :], ident_f[:], start=True, stop=True),
                         reads=["cn", "ident_f"], writes=[pk])
                    k.op("dve", lambda e, ps=ps, a=a: e.tensor_copy(CT_[:, a * 128:(a + 1) * 128], ps[0:64, 0:128]),
                         reads=[pk], writes=["CT_"])
                k.dma("sp", DA(CD, i * 64 * BN, [[BN, 64], [1, BN]]), CT_[:], reads=["CT_"], writes=["CD"])
            dump("PWD", DA(PWD, 0, [[3200, 128], [1, 3200]]), [128, 3200], F32)
            dump("BBD", DA(BBD, 0, [[2048, 128], [1, 2048]]), [128, 2048], F32)
            dump("CD", DA(CD, 0, [[2048, 128], [1, 2048]]), [128, 2048], F32)
            k.release()

        def s5_branch(l):
            k.mark()
            KIN = [k.sb("KIN%d" % d, [128, 16, 128], BF16) for d in range(2)]
            BM = [k.sb("BM%d" % d, [128, 16, 2, 64], BF16) for d in range(2)]
            CM = [k.sb("CM%d" % d, [64, 16, 2, 128], BF16) for d in range(2)]
            A8x = k.sb("A8x", [64, 2, 2, 16], F32)
            A8y = k.sb("A8y", [64, 2, 2, 16], F32)
            DSK = k.sb("DSK", [128, 2], F32)
            k.dma("sp", DSK.v(0, [[1, 2], [1, 1]]), DA(I["s5_d"], l * 256, [[1, 128], [128, 2], [1, 1]]), writes=["DSK"], allow_slow_non_contiguous=True)
            GB = k.sb("GB", [128, 2], F32)
            k.dma("sp", GB.v(0, [[1, 2], [1, 1]]), DA(I["s5_glu_b"], l * 256, [[1, 128], [128, 2], [1, 1]]), writes=["GB"], allow_slow_non_contiguous=True)
            GW = k.sb("GW", [128, 2, 256], BF16)
            wload(GW[:], I["s5_glu_w"], l * 65536, 256, 2, 256, "GW")
            uTp = k.sb("uTp", [128, 2, 8, 288], BF16)

            def ev_u(ci, tok0, ntok, ps, pk):
                nkb, k0 = ntok // 8, tok0 // 8
                copy_evac(uTp.v(ci * 8 * 288 + k0, [[1, nkb], [288, 8]]), ps.v(0, [[8, nkb], [1, 8]]), [pk], ["uTp"])
            proj_fm(l, 0, 256, ev_u)
            for cc in range(2):
                k.dma("sp", DA(UDP, cc * 128 * NTOK, [[NTOK, 128], [1, NTOK]]), uTp.v(cc * 2304, [[1, 2304]]), reads=both("uTp"), writes=["UDP"])
            U = k.sb("U", [128, 16, 288], BF16)
            UK = [("U", r) for r in range(8)]
            for r in range(8):
                k.dma("sp", U.v(0, [[288, 16], [1, 288]], p0=r * 16, np_=16),
                      DA(UDP, r * 288, [[NTOK, 16], [16 * NTOK, 16], [1, 288]]), reads=["UDP"], writes=[("U", r)])
            k.mark()
            PW = k.sb("PW", [64, 2, 32 * 25], F32)
            BB = k.sb("BB", [64, 2, 32 * 16], F32)
            CC = k.sb("CC", [64, 2, 32 * 16], F32)
            for i in range(2):
                k.dma("sp", PW[:, i, :], DA(PWD, i * 64 * 3200 + l * 800, [[3200, 64], [1, 800]]), reads=["PWD"], writes=["PW"])
                k.dma("sp", BB[:, i, :], DA(BBD, i * 64 * 2048 + l * 512, [[2048, 64], [1, 512]]), reads=["BBD"], writes=["BB"])
                k.dma("sp", CC[:, i, :], DA(CD, i * 64 * 2048 + l * 512, [[2048, 64], [1, 512]]), reads=["CD"], writes=["CC"])
            tri = k.sb("tri", [128, 2, 128], F32)
            k.dma("sp", tri.v(0, [[128, 2], [1, 128]]), DA(I["k_trimask"], 0, [[128, 128], [128 * 128, 2], [1, 128]]), writes=["tri"])
            Lr = k.sb("Lr", [64, 16, 128], F32); Li = k.sb("Li", [64, 16, 128], F32)
            Pr = k.sb("Pr", [64, 16, 128], F32); Pi = k.sb("Pi", [64, 16, 128], F32)
            Rr = k.sb("Rr", [64, 16, 128], F32); Ri = k.sb("Ri", [64, 16, 128], F32)
            ta = k.sb("ta", [64, 16, 128], F32); tb2 = k.sb("tb2", [64, 16, 128], F32)
            for d in range(2):
                def cmul(Xr, Xi, xk, nbase, S, neg_im, eng, d=d):
                    pw0 = PW.v(0 * 800 + d * 400 + nbase, [[25, 16], [1, 8], [0, 16]])
                    pw1 = PW.v(1 * 800 + d * 400 + nbase, [[25, 16], [1, 8], [0, 16]])
                    sv0 = S.v(0 * 512 + d * 256, [[16, 16], [0, 8], [1, 16]])
                    sv1 = S.v(1 * 512 + d * 256, [[16, 16], [0, 8], [1, 16]])
                    sk = "BB" if S is BB else "CC"
                    o4 = [[128, 16], [16, 8], [1, 16]]
                    k.op(eng, lambda e: e.tensor_tensor(Xr.v(0, o4), pw0, sv0, ALU.mult), reads=["PW", sk], writes=[xk + "r"])
                    k.op(eng, lambda e: e.tensor_tensor(ta.v(0, o4), pw1, sv1, ALU.mult), reads=["PW", sk], writes=["ta"])
                    k.op(eng, lambda e: e.tensor_tensor(Xr[:], Xr[:], ta[:], ALU.subtract), reads=[xk + "r", "ta"], writes=[xk + "r"])
                    k.op(eng, lambda e: e.tensor_tensor(Xi.v(0, o4), pw0, sv1, ALU.mult), reads=["PW", sk], writes=[xk + "i"])
                    k.op(eng, lambda e: e.tensor_tensor(tb2.v(0, o4), pw1, sv0, ALU.mult), reads=["PW", sk], writes=["tb2"])
                    if neg_im:
                        k.op(eng, lambda e: e.scalar_tensor_tensor(Xi[:], Xi[:], -1.0, tb2[:], ALU.mult, ALU.subtract),
                             reads=[xk + "i", "tb2"], writes=[xk + "i"])
                    else:
                        k.op(eng, lambda e: e.tensor_tensor(Xi[:], Xi[:], tb2[:], ALU.add), reads=[xk + "i", "tb2"], writes=[xk + "i"])
                cmul(Lr, Li, "L", 0, BB, False, "dve")
                cmul(Pr, Pi, "P", 8, BB, False, "dve")
                cmul(Rr, Ri, "R", 16, CC, True, "dve")
                for g in range(16):
                    ps, pk = k.bank()
                    k.op("pe", lambda e, ps=ps, g=g: e.matmul(ps[:, 0:128], Pr[:, g, :], Rr[:, g, :], start=True, stop=False),
                         reads=["Pr", "Rr"], writes=[pk], count=False)
                    k.op("pe", lambda e, ps=ps, g=g: e.matmul(ps[:, 0:128], Pi[:, g, :], Ri[:, g, :], start=False, stop=True),
                         reads=["Pi", "Ri"], writes=[pk])
                    k.op("dve", lambda e, ps=ps, g=g, d=d: e.tensor_tensor(KIN[d][:, g, :], ps[:, 0:128], tri[:, d, :], ALU.mult),
                         reads=[pk, "tri"], writes=[("KIN", d)])
                for g0 in range(0, 16, 4):
                    ps, pk = k.bank()
                    for gg in range(4):
                        for c, Lc, lk in ((0, Lr, "Lr"), (1, Li, "Li")):
                            s = gg * 2 + c
                            k.op("pe", lambda e, ps=ps, s=s, Lc=Lc, g=g0 + gg: e.matmul(
                                ps[:, s * 64:(s + 1) * 64], Lc[:, g, :], ident_f[0:64, 0:64], start=True, stop=True),
                                reads=[lk, "ident_f"], writes=[pk], count=(s == 7))
                    copy_evac(BM[d].v(g0 * 128, [[1, 512]]), ps[:, 0:512], [pk], [("BM", d)])
                k.op("act", lambda e, d=d: e.activation(CM[d].v(0, [[256, 16], [1, 128]]), Rr[:], AF.Copy), reads=["Rr"], writes=[("CM", d)])
                k.op("act", lambda e, d=d: e.activation(CM[d].v(128, [[256, 16], [1, 128]]), Ri[:], AF.Copy), reads=["Ri"], writes=[("CM", d)])
                i8 = 23 if d == 0 else 16
                for c in range(2):
                    k.op("dve", lambda e, d=d, c=c, i8=i8: e.tensor_copy(A8x.v((d * 2 + c) * 16, [[1, 16]]), PW.v(d * 400 + i8, [[25, 16]])),
                         reads=["PW"], writes=["A8"])
                k.op("dve", lambda e, d=d, i8=i8: e.tensor_copy(A8y.v((d * 2 + 1) * 16, [[1, 16]]), PW.v(800 + d * 400 + i8, [[25, 16]])),
                     reads=["PW"], writes=["A8"])
                k.op("dve", lambda e, d=d, i8=i8: e.tensor_scalar(A8y.v((d * 2) * 16, [[1, 16]]), PW.v(800 + d * 400 + i8, [[25, 16]]), -1.0, None, ALU.mult),
                     reads=["PW"], writes=["A8"])
            for d in range(2):
                dump("KIN%d" % d, KIN[d].v(0, [[1, 2048]]), [128, 2048], BF16)
                dump("BM%d" % d, BM[d].v(0, [[1, 2048]]), [128, 2048], BF16)
                dump("CM%d" % d, CM[d].v(0, [[1, 4096]]), [64, 4096], BF16)
            dump("A8x", A8x.v(0, [[1, 64]]), [64, 64], F32)
            dump("A8y", A8y.v(0, [[1, 64]]), [64, 64], F32)
            dump("Uu", U.v(0, [[1, 16 * 288]]), [128, 16 * 288], BF16)
            k.release()
            KP = 320
            XE = k.sb("XE", [64, 2, 16, KP], F32)
            EB = k.sb("EB", [64, 2, 16, 288], BF16)
            Ysb = k.sb("Ysb", [128, 16, 288], F32)
            P_ = k.sb("P_", [64, 2, 16, 10], F32)
            Q_ = k.sb("Q_", [64, 2, 16, 10], F32)
            W31x = k.sb("W31x", [64, 2, 16], F32)
            W31y = k.sb("W31y", [64, 2, 16], F32)
            TT = k.sb("TT", [64, 2, 16, 9], F32)
            P1 = k.sb("P1", [64, 2, 16], F32)
            Q1 = k.sb("Q1", [64, 2, 16], F32)
            MM = [k.sb("MM%d" % i, [64, 16, 64], F32) for i in range(4)]
            GK = 16 * KP
            RX = both("XE") + ["XE"]
            for d in range(2):
                for g in range(16):
                    for c in range(2):
                        ps, pk = k.bank()
                        k.op("pe", lambda e, ps=ps, d=d, g=g, c=c: e.matmul(ps[0:64, 0:288], BM[d][:, g, c, :], U[:, g, :], start=True, stop=True),
                             reads=both(("BM", d)) + UK, writes=[pk])
                        copy_evac(XE[:, c, g, 0:288], ps[0:64, 0:288], [pk], ["XE"])
                ppos = 288 if d == 0 else 319
                k.op("dve", lambda e: e.memset(XE.v(288, [[GK, 2], [KP, 16], [1, 32]]), 0.0), reads=RX, writes=["XE"])
                k.op("dve", lambda e, d=d, ppos=ppos: e.tensor_copy(XE.v(ppos, [[KP, 16]]), A8x.v(d * 32, [[1, 16]])), reads=RX + ["A8"], writes=["XE"])
                k.op("dve", lambda e, d=d, ppos=ppos: e.tensor_copy(XE.v(GK + ppos, [[KP, 16]]), A8y.v(d * 32 + 16, [[1, 16]])), reads=RX + ["A8"], writes=["XE"])
                sst = 32 if d == 0 else -32

                def pos(j, d=d):
                    return j if d == 0 else 319 - j
                for j in range(1, 32):
                    cur, prv = pos(j), pos(j - 1)
                    k.op("dve", lambda e, prv=prv, d=d, sst=sst: e.tensor_tensor(
                        P_[:], XE.v(prv, [[GK, 2], [KP, 16], [sst, 10]]), A8x.v(d * 32, [[16, 2], [1, 16], [0, 10]]), ALU.mult),
                        reads=RX + ["A8"], writes=["P_"])
                    k.op("dve", lambda e, prv=prv, d=d, sst=sst: e.tensor_tensor(
                        Q_[:], XE.v(GK + prv, [[-GK, 2], [KP, 16], [sst, 10]]), A8y.v(d * 32, [[16, 2], [1, 16], [0, 10]]), ALU.mult),
                        reads=RX + ["A8"], writes=["Q_"])
                    k.op("dve", lambda e: e.tensor_tensor(P_[:], P_[:], Q_[:], ALU.add), reads=["P_", "Q_"], writes=["P_"])
                    k.op("dve", lambda e, cur=cur, sst=sst: e.tensor_tensor(
                        XE.v(cur, [[GK, 2], [KP, 16], [sst, 10]]), XE.v(cur, [[GK, 2], [KP, 16], [sst, 10]]), P_[:], ALU.add),
                        reads=RX + ["P_"], writes=["XE"])
                w31 = 319 if d == 0 else 288
                k.op("dve", lambda e, w31=w31: e.tensor_copy(W31x.v(0, [[16, 2], [1, 16]]), XE.v(w31, [[0, 2], [KP, 16]])), reads=RX, writes=["W31"])
                k.op("dve", lambda e, w31=w31: e.tensor_copy(W31y.v(16, [[1, 16]]), XE.v(GK + w31, [[KP, 16]])), reads=RX, writes=["W31"])
                k.op("dve", lambda e, w31=w31: e.tensor_scalar(W31y.v(0, [[1, 16]]), XE.v(GK + w31, [[KP, 16]]), -1.0, None, ALU.mult), reads=RX, writes=["W31"])

                def kend(sg, d=d):
                    if d == 0:
                        return 32 * sg + 31
                    return 0 if sg == 0 else 288 - 32 * sg
                k.op("dve", lambda e, ke=kend(0): e.tensor_copy(TT.v(0, [[144, 2], [9, 16]]), XE.v(ke, [[GK, 2], [KP, 16]])), reads=RX, writes=["TT"])
                for sg in range(1, 9):
                    k.op("dve", lambda e, sg=sg: e.tensor_tensor(P1[:], TT.v(sg - 1, [[144, 2], [9, 16]]), W31x[:], ALU.mult), reads=["TT", "W31"], writes=["P1"])
                    k.op("dve", lambda e, sg=sg: e.tensor_tensor(Q1[:], TT.v(144 + sg - 1, [[-144, 2], [9, 16]]), W31y[:], ALU.mult), reads=["TT", "W31"], writes=["Q1"])
                    k.op("dve", lambda e: e.tensor_tensor(P1[:], P1[:], Q1[:], ALU.add), reads=["P1", "Q1"], writes=["P1"])
                    k.op("dve", lambda e, sg=sg, ke=kend(sg): e.tensor_tensor(TT.v(sg, [[144, 2], [9, 16]]), XE.v(ke, [[GK, 2], [KP, 16]]), P1[:], ALU.add),
                         reads=RX + ["P1"], writes=["TT"])
                for s0 in range(1, 9, 2):
                    if d == 0:
                        def xo(c, s0=s0):
                            return XE.v(c * GK + 32 * s0, [[KP, 16], [32, 2], [1, 32]])

                        def wv(c):
                            return XE.v(c * GK + 288, [[KP, 16], [0, 2], [1, 32]])
                    else:
                        def xo(c, s0=s0):
                            return XE.v(c * GK + 319 - 32 * s0, [[KP, 16], [-32, 2], [-1, 32]])

                        def wv(c):
                            return XE.v(c * GK + 319, [[KP, 16], [0, 2], [-1, 32]])

                    def tv(c, s0=s0):
                        return TT.v(c * 144 + s0 - 1, [[9, 16], [1, 2], [0, 32]])
                    m4 = [[64, 16], [32, 2], [1, 32]]
                    for eng, (ma, mb), co, (ca, cb), op2 in (("dve", (MM[0], MM[1]), 0, (0, 1), ALU.subtract),
                                                             ("dve", (MM[2], MM[3]), 1, (1, 0), ALU.add)):
                        mk = "MM%d" % co
                        wre, wim, ta_, tb_, xo_ = wv(0), wv(1), tv(ca), tv(cb), xo(co)
                        k.op(eng, lambda e, ma=ma, wre=wre, ta_=ta_: e.tensor_tensor(ma.v(0, m4), wre, ta_, ALU.mult), reads=RX + ["TT"], writes=[mk + "a"])
                        k.op(eng, lambda e, mb=mb, wim=wim, tb_=tb_: e.tensor_tensor(mb.v(0, m4), wim, tb_, ALU.mult), reads=RX + ["TT"], writes=[mk + "b"])
                        k.op(eng, lambda e, ma=ma, mb=mb, op2=op2: e.tensor_tensor(ma[:], ma[:], mb[:], op2), reads=[mk + "a", mk + "b"], writes=[mk + "a"])
                        k.op(eng, lambda e, ma=ma, xo_=xo_: e.tensor_tensor(xo_, xo_, ma.v(0, m4), ALU.add), reads=RX + [mk + "a"], writes=["XEc%d" % co])
                k.op("act", lambda e: e.activation(EB.v(0, [[288, 32], [1, 288]]), XE.v(0, [[KP, 32], [1, 288]]), AF.Copy), reads=RX + ["XEc0", "XEc1"], writes=["EB"])
                for g in range(16):
                    ps, pk = k.bank()
                    k.op("pe", lambda e, ps=ps, d=d, g=g: e.matmul(ps[:, 0:288], KIN[d][:, g, :], U[:, g, :], start=True, stop=False),
                         reads=[("KIN", d)] + UK, writes=[pk], count=False)
                    if d == 0:
                        rngs = [(1, 0, 287)]
                    else:
                        rngs = [(0, 1, 31), (32, 33, 255), (287, 0, 1)]
                    nmm = len(rngs) * 2
                    im = 0
                    for (o0, s0, n) in rngs:
                        for c in range(2):
                            im += 1
                            k.op("pe", lambda e, ps=ps, d=d, g=g, c=c, o0=o0, s0=s0, n=n, last=(im == nmm): e.matmul(
                                ps[:, o0:o0 + n], CM[d][:, g, c, :], EB[:, c, g, s0:s0 + n], start=False, stop=last),
                                reads=[("CM", d), "EB"], writes=[pk], count=(im == nmm))
                    if d == 0:
                        copy_evac(Ysb[:, g, :], ps[:, 0:288], [pk], ["Ysb"])
                    else:
                        k.op("dve", lambda e, ps=ps, g=g: e.tensor_tensor(Ysb[:, g, :], ps[:, 0:288], Ysb[:, g, :], ALU.add),
                             reads=[pk] + both("Ysb"), writes=["Ysb"])
            for j in range(8):
                k.dma("sp", DA(YDP, j * 288, [[NTOK, 16], [16 * NTOK, 16], [1, 288]]),
                      Ysb.v(0, [[288, 16], [1, 288]], p0=j * 16, np_=16), reads=both("Ysb") + ["Ysb"], writes=[("YDP", j)])
            dump("YDP%d" % l, DA(YDP, 0, [[NTOK, 256], [1, NTOK]]), [256, NTOK], F32)
            k.release()
            k.mark()
            uTp2 = k.sb("uTp2", [128, 2, 8, 288], BF16)
            for cc in range(2):
                k.dma("sp", uTp2.v(cc * 2304, [[1, 2304]]), DA(UDP, cc * 128 * NTOK, [[NTOK, 128], [1, NTOK]]), reads=["UDP"], writes=["uTp2"])
            yTp = k.sb("yTp", [128, 2, 8, 288], F32)
            for cc in range(2):
                k.dma("sp", yTp.v(cc * 2304, [[1, 2304]]), DA(YDP, cc * 128 * NTOK, [[NTOK, 128], [1, NTOK]]), reads=["YDP"], writes=["yTp"])
            DSK = k.sb("DSK", [128, 2], F32)
            k.dma("sp", DSK.v(0, [[1, 2], [1, 1]]), DA(I["s5_d"], l * 256, [[1, 128], [128, 2], [1, 1]]), writes=["DSK"], allow_slow_non_contiguous=True)
            GB = k.sb("GB", [128, 2], F32)
            k.dma("sp", GB.v(0, [[1, 2], [1, 1]]), DA(I["s5_glu_b"], l * 256, [[1, 128], [128, 2], [1, 1]]), writes=["GB"], allow_slow_non_contiguous=True)
            GW = k.sb("GW", [128, 2, 256], BF16)
            wload(GW[:], I["s5_glu_w"], l * 65536, 256, 2, 256, "GW")
            yl = k.sb("yl", [128, NTOK], F32)
            tt = k.sb("tt", [128, NTOK], F32)
            gT = k.sb("gT", [128, 2, NTOK], BF16)
            YA = k.sb("YA", [128, 2, NTOK], BF16)
            for cc in range(2):
                k.op("dve", lambda e, cc=cc: e.scalar_tensor_tensor(
                    yl.v(0, [[8, 288], [1, 8]]), uTp2.v(cc * 2304, [[1, 288], [288, 8]]), DSK[:, cc:cc + 1],
                    yTp.v(cc * 2304, [[1, 288], [288, 8]]), ALU.mult, ALU.add),
                    reads=["uTp2", "DSK", "yTp"], writes=["yl"])
                k.op("dve", lambda e: e.tensor_tensor(tt[:], yl[:], yl[:], ALU.mult), reads=["yl"], writes=["tt"])
                k.op("dve", lambda e: e.tensor_scalar(tt[:], tt[:], 0.044715, 1.0, ALU.mult, ALU.add), reads=["tt"], writes=["tt"])
                k.op("dve", lambda e: e.tensor_tensor(tt[:], tt[:], yl[:], ALU.mult), reads=["tt", "yl"], writes=["tt"])
                k.op("act", lambda e: e.activation(tt[:], tt[:], AF.Sigmoid, scale=1.5957691216057308), reads=["tt"], writes=["tt"])
                k.op("dve", lambda e, cc=cc: e.tensor_tensor(gT[:, cc, :], yl[:], tt[:], ALU.mult), reads=["yl", "tt"], writes=["gT"])
            sg = [k.sb("sg%d" % i, [128, 512], F32) for i in range(2)]
            n_ = 0
            for co in range(2):
                for (tok0, ntok) in MT:
                    ps, pk = k.bank()
                    for cc in range(2):
                        k.op("pe", lambda e, ps=ps, cc=cc, co=co, tok0=tok0, ntok=ntok: e.matmul(
                            ps[:, 0:ntok], GW[:, cc, co * 128:(co + 1) * 128], gT[:, cc, tok0:tok0 + ntok],
                            start=(cc == 0), stop=(cc == 1)), reads=["GW", "gT"], writes=[pk], count=(cc == 1))
                    s_ = sg[n_ % 2]; sk = ("sg", n_ % 2); n_ += 1
                    k.op("act", lambda e, ps=ps, s_=s_, co=co, ntok=ntok: e.activation(
                        s_[:, 0:ntok], ps[:, 0:ntok], AF.Sigmoid, bias=GB[:, co:co + 1]), reads=[pk, "GB"], writes=[sk])
                    k.op("dve", lambda e, s_=s_, co=co, tok0=tok0, ntok=ntok: e.tensor_tensor(
                        YA[:, co, tok0:tok0 + ntok], gT[:, co, tok0:tok0 + ntok], s_[:, 0:ntok], ALU.mult),
                        reads=[sk, "gT"], writes=["YA"])
            for cc in range(2):
                k.dma("sp", DA(YT, (0 * 256 + cc * 128) * NTOK, [[NTOK, 128], [1, NTOK]]), YA[:, cc, :], reads=["YA"], writes=[("YT", 0)])
            k.release()

        def conv_branch(l):
            k.mark()
            S3 = [k.sb("cv%d" % i, [128, 2, NTOK], BF16) for i in range(3)]

            def ev(ci, tok0, ntok, ps, pk):
                s, cc = ci // 2, ci % 2
                copy_evac(S3[s][:, cc, tok0:tok0 + ntok], ps[:, 0:ntok], [pk], [("cv", s)])
            proj_fm(l, 1024, 768, ev)
            CW = k.sb("CW", [128, 2, 3], F32)
            for cc in range(2):
                k.dma("sp", CW.v(cc * 3, [[1, 3], [1, 1]]), DA(I["conv_w"], l * 768 + cc * 128, [[1, 128], [256, 3], [1, 1]]),
                      writes=["CW"], allow_slow_non_contiguous=True)
            zz = k.sb("zz", [128, NTOK], F32)
            yy = k.sb("yy", [128, NTOK], F32)
            YC = k.sb("YC", [128, 2, NTOK], BF16)
            for cc in range(2):
                k.op("dve", lambda e, cc=cc: e.tensor_tensor(zz[:], S3[1][:, cc, :], S3[2][:, cc, :], ALU.mult),
                     reads=both(("cv", 1)) + both(("cv", 2)), writes=["zz"])
                k.op("dve", lambda e, cc=cc: e.tensor_scalar(yy[:], zz[:], CW[:, cc, 1:2], None, ALU.mult), reads=["zz", "CW"], writes=["yy"])
                for (a, b) in ((0, 256), (256, NTOK)):
                    k.op("dve", lambda e, cc=cc, a=a, b=b: e.scalar_tensor_tensor(
                        yy[:, a + 1:b], zz[:, a:b - 1], CW[:, cc, 0:1], yy[:, a + 1:b], ALU.mult, ALU.add),
                        reads=["zz", "CW", "yy"], writes=["yy"])
                    k.op("dve", lambda e, cc=cc, a=a, b=b: e.scalar_tensor_tensor(
                        yy[:, a:b - 1], zz[:, a + 1:b], CW[:, cc, 2:3], yy[:, a:b - 1], ALU.mult, ALU.add),
                        reads=["zz", "CW", "yy"], writes=["yy"])
                k.op("dve", lambda e, cc=cc: e.tensor_tensor(YC[:, cc, :], S3[0][:, cc, :], yy[:], ALU.mult),
                     reads=both(("cv", 0)) + ["yy"], writes=["YC"])
            for cc in range(2):
                k.dma("sp", DA(YT, (2 * 256 + cc * 128) * NTOK, [[NTOK, 128], [1, NTOK]]), YC[:, cc, :], reads=["YC"], writes=[("YT", 2)])
            k.release()

        def sgu_branch(l):
            k.mark()
            uT = k.sb("sguT", [128, 2, NTOK], BF16)

            def ev(ci, tok0, ntok, ps, pk):
                copy_evac(uT[:, ci, tok0:tok0 + ntok], ps[:, 0:ntok], [pk], ["sguT"])
            proj_fm(l, 1792, 256, ev)
            WV = k.sb("WV", [128, 8, 256], BF16)
            wload(WV[:], I["w_in"], l * 1024 * 6400 + 2048, 6400, 8, 256, "WV")
            LG = k.sb("LG", [128, 256], F32); LB = k.sb("LB", [128, 256], F32)
            k.dma("sp", LG[:], DA(I["sgu_ln_g"], l * 256, [[0, 128], [1, 256]]), writes=["LG"])
            k.dma("sp", LB[:], DA(I["sgu_ln_b"], l * 256, [[0, 128], [1, 256]]), writes=["LB"])
            SGB = k.sb("SGB", [128, 4, 128], F32)
            k.dma("sp", SGB[:], DA(I["sgu_b"], l * 512, [[0, 128], [1, 512]]), writes=["SGB"])
            wsn = k.sb("wsn", [128, 4, 128], F32)
            k.dma("sp", wsn[:], DA(I["sgu_w"], l * 4 * 16384, [[128, 128], [16384, 4], [1, 128]]), writes=["wsn"])
            WST = k.sb("WST", [128, 4, 128], BF16)
            ps, pk = k.bank()
            for g in range(4):
                k.op("pe", lambda e, ps=ps, g=g: e.matmul(ps[:, g * 128:(g + 1) * 128], wsn[:, g, :], ident_f[:], start=True, stop=True),
                     reads=["wsn", "ident_f"], writes=[pk])
            k.op("dve", lambda e, ps=ps: e.tensor_copy(WST[:], ps[:, 0:512]), reads=[pk], writes=["WST"])
            VN = k.sb("VN", [128, NTILE, 256], BF16)
            st_ = k.sb("st_", [128, NTILE, 4], F32)
            junk = k.sb("junk2", [128, 256], F32)
            VT = k.sb("VT", [128, NTILE, 256], F32)
            k.op("dve", lambda e: e.memset(st_[:], 0.0), writes=["st_"])
            for t in range(NTILE):
                ps, pk = k.bank()
                for kc in range(8):
                    k.op("pe", lambda e, ps=ps, kc=kc, t=t: e.matmul(ps[:, 0:256], alT[:, kc, t * 128:(t + 1) * 128], WV[:, kc, :],
                                                                    start=(kc == 0), stop=(kc == 7)),
                         reads=tkeys("alT", t * 128, 128) + ["WV"], writes=[pk], count=(kc == 7))
                k.op("act", lambda e, ps=ps, t=t: e.activation(VT[:, t, :], ps[:, 0:256], AF.Copy, accum_out=st_[:, t, 0:1]),
                     reads=[pk, "st_"], writes=[("VT", t), "st_"])
                k.op("act", lambda e, t=t: e.activation(junk[:], VT[:, t, :], AF.Square, accum_out=st_[:, t, 1:2]),
                     reads=[("VT", t), "st_"], writes=["junk2", "st_"])
            def sv_(c0, n=1):
                return st_.v(c0, [[4, NTILE], [1, n]])
            k.op("dve", lambda e: e.tensor_scalar(sv_(0, 2), sv_(0, 2), 1.0 / 256, None, ALU.mult), reads=["st_"], writes=["st_"])
            k.op("dve", lambda e: e.tensor_tensor(sv_(2), sv_(0), sv_(0), ALU.mult), reads=["st_"], writes=["st_"])
            k.op("dve", lambda e: e.tensor_tensor(sv_(2), sv_(1), sv_(2), ALU.subtract), reads=["st_"], writes=["st_"])
            k.op("dve", lambda e: e.tensor_scalar(sv_(2), sv_(2), 1e-6, None, ALU.add), reads=["st_"], writes=["st_"])
            k.op("act", lambda e: e.activation(sv_(2), sv_(2), AF.Sqrt), reads=["st_"], writes=["st_"])
            k.op("dve", lambda e: e.reciprocal(sv_(2), sv_(2)), reads=["st_"], writes=["st_"])
            for t in range(NTILE):
                k.op("dve", lambda e, t=t: e.tensor_scalar(VT[:, t, :], VT[:, t, :], st_[:, t, 0:1], st_[:, t, 2:3], ALU.subtract, ALU.mult),
                     reads=[("VT", t), "st_"], writes=[("VT", t)])
                k.op("dve", lambda e, t=t: e.tensor_tensor(VT[:, t, :], VT[:, t, :], LG[:], ALU.mult), reads=[("VT", t), "LG"], writes=[("VT", t)])
                k.op("dve", lambda e, t=t: e.tensor_tensor(VN[:, t, :], VT[:, t, :], LB[:], ALU.add), reads=[("VT", t), "LB"], writes=[("VN", t)])
            YD = k.sb("YD", [128, 2, NTOK], BF16)
            zt = [k.sb("zt%d" % i, [128, 128], F32) for i in range(2)]
            n_ = 0
            for t in range(NTILE):
                for cc in range(2):
                    ps, pk = k.bank()
                    for gl in range(2):
                        g = 2 * cc + gl
                        k.op("pe", lambda e, ps=ps, gl=gl, g=g, t=t, cc=cc: e.matmul(
                            ps[:, gl * 128:(gl + 1) * 128], VN[:, t, cc * 128:(cc + 1) * 128], WST[:, g, :], start=True, stop=True),
                            reads=[("VN", t), "WST"], writes=[pk])
                    z_ = zt[n_ % 2]; zk = ("zt", n_ % 2); n_ += 1
                    for gl in range(2):
                        g = 2 * cc + gl
                        p0 = 64 * gl
                        k.op("dve", lambda e, ps=ps, z_=z_, gl=gl, g=g, p0=p0: e.tensor_tensor(
                            z_[p0:p0 + 64, :], ps[p0:p0 + 64, gl * 128:(gl + 1) * 128], SGB[p0:p0 + 64, g, :], ALU.add),
                            reads=[pk, "SGB"], writes=[zk])
                    k.op("dve", lambda e, z_=z_, t=t, cc=cc: e.tensor_tensor(
                        YD[:, cc, t * 128:(t + 1) * 128], z_[:], uT[:, cc, t * 128:(t + 1) * 128], ALU.mult),
                        reads=[zk] + both("sguT"), writes=["YD"])
            for cc in range(2):
                k.dma("sp", DA(YT, (3 * 256 + cc * 128) * NTOK, [[NTOK, 128], [1, NTOK]]), YD[:, cc, :], reads=["YD"], writes=[("YT", 3)])
            k.release()

        def rp_setup():
            k.mark()
            negt = k.sb("negt", [121, 160], F32)
            k.op("dve", lambda e: e.memset(negt[:], NEG), writes=["negt"])
            for hlf in range(2):
                k.dma("sp", DA(RP, hlf * 120 * 160, [[160, 121], [1, 160]]), negt[:], reads=["negt"], writes=["RP"])
            k.dma("sp", DA(RP, 64, [[160, 240], [1, 31]]), DA(I["na_rpb"], 0, [[31, 240], [1, 31]]), reads=["RP"], writes=["RP"])
            k.release()

        def attn_branch(l):
            k.mark()
            qT = k.sb("qT", [128, 2, NTOK], BF16)
            kT = k.sb("kT", [128, 2, NTOK], BF16)

            def ev(ci, tok0, ntok, ps, pk):
                if ci < 2:
                    copy_evac(qT[:, ci, tok0:tok0 + ntok], ps[:, 0:ntok], [pk], ["qT"], scale=0.125)
                else:
                    copy_evac(kT[:, ci - 2, tok0:tok0 + ntok], ps[:, 0:ntok], [pk], ["kT"])
            proj_fm(l, 256, 512, ev)
            if CUT == 21:
                k.release(); return
            WV = k.sb("WVa", [128, 8, 256], BF16)
            wload(WV[:], I["w_in"], l * 1024 * 6400 + 768, 6400, 8, 256, "WVa")
            NVT = 18 + 15
            VP = k.sb("VP", [128, NVT, 4, 128], BF16)
            k.op("dve", lambda e: e.memset(VP[:], 0.0), writes=both("VP"))
            if CUT == 25:
                k.release(); return
            starts = [t * 128 for t in range(18)] + [320 + 128 * m for m in range(15)]
            if CUT == 26:
                starts = starts[:18]
            if CUT == 27:
                starts = starts[:1]
            for vi, s0 in enumerate(starts):
                ps, pk = k.bank()
                for kc in range(8):
                    k.op("pe", lambda e, ps=ps, kc=kc, s0=s0: e.matmul(ps[:, 0:256], alT[:, kc, s0:s0 + 128], WV[:, kc, :],
                                                                      start=(kc == 0), stop=(kc == 7)),
                         reads=tkeys("alT", s0, 128) + ["WVa"], writes=[pk], count=(kc == 7))
                copy_evac(VP.v(vi * 512, [[256, 2], [1, 64]]), ps.v(0, [[128, 2], [1, 64]]), [pk], ["VP"])
                copy_evac(VP.v(vi * 512 + 128 + 64, [[256, 2], [1, 64]]), ps.v(64, [[128, 2], [1, 64]]), [pk], ["VP"])
            if CUT == 22:
                k.release(); return
            ONP = k.sb("ONP", [128, 2, 128], BF16)
            k.op("dve", lambda e: e.memset(ONP[:], 0.0), writes=["ONP"])
            k.op("dve", lambda e: e.memset(ONP[:, 0, 0:64], 1.0), writes=["ONP"])
            k.op("dve", lambda e: e.memset(ONP[:, 1, 64:128], 1.0), writes=["ONP"])
            Wt = k.sb("Wt", [128, 60, 64], F32)
            k.dma("sp", Wt.v(0, [[64, 60], [1, 64]], p0=0, np_=64), DA(RP, l * 60 * 160 + 16, [[1, 64], [160, 60], [1, 64]]), reads=["RP"], writes=["Wt"])
            k.dma("sp", Wt.v(0, [[64, 60], [1, 64]], p0=64, np_=64), DA(RP, l * 60 * 160 + 160 + 16, [[1, 64], [160, 60], [1, 64]]), reads=["RP"], writes=["Wt"])
            cmk = k.sb("cmk", [128, 64], F32)
            k.dma("sp", cmk[:], I["k_colmask"].ap(), writes=["cmk"])
            RPT = k.sb("RPT", [128, 60, 64], F32)
            k.op("dve", lambda e: e.tensor_tensor(RPT.v(0, [[64, 60], [1, 64]]), Wt.v(63, [[64, 60], [-1, 64]]), cmk.v(0, [[0, 60], [1, 64]]), ALU.add),
                 reads=["Wt", "cmk"], writes=["RPT"])
            if CUT == 23:
                k.release(); return
            YB = k.sb("YB", [128, 2, NTOK], BF16)
            tmpS = [k.sb("tmpS%d" % i, [128, 4, 64], F32) for i in range(2)]
            Pm = [k.sb("Pm%d" % i, [128, 6, 64], BF16) for i in range(2)]
            rec = [k.sb("rec%d" % i, [128, 2, 64], F32) for i in range(2)]
            n_ = 0

            def a_scores(r, h):
                q0 = 256 + 64 * r
                rs_ = min(max(r - 4, 0), 24)
                cc, ph = h // 2, 64 * (h % 2)
                psS, pkS = k.bank()
                ktoks = [256 + 64 * (rs_ + 2 * c) for c in range(4)] + [0, 128]
                for c in range(6):
                    k.op("pe", lambda e, psS=psS, c=c, cc=cc, ph=ph, kt=ktoks[c], q0=q0: e.matmul(
                        psS[:, c * 64:(c + 1) * 64], kT[ph:ph + 64, cc, kt:kt + 128], qT[ph:ph + 64, cc, q0:q0 + 64],
                        start=True, stop=True), reads=both("kT") + both("qT"), writes=[pkS], count=(c == 5))
                return (r, h, psS, pkS, ktoks)

            rowst = {}

            def a_rest(st, b):
                r, h, psS, pkS, ktoks = st
                q0 = 256 + 64 * r
                rs_ = min(max(r - 4, 0), 24)
                dr0 = rs_ - r + 7
                cc = h // 2
                if h == 0:
                    rowst[r] = (k.bank(), k.bank())
                (psO, pkO), (psD, pkD) = rowst[r]
                k.op("dve", lambda e: e.tensor_tensor(
                    tmpS[b].v(0, [[64, 4], [1, 64]]), psS.v(0, [[64, 4], [1, 64]]),
                    RPT.v((h * 15 + dr0) * 64, [[128, 4], [1, 64]]), ALU.add),
                    reads=[pkS, "RPT"], writes=[("tmpS", b)])
                k.op("act", lambda e: e.activation(Pm[b].v(0, [[1, 256]]), tmpS[b].v(0, [[1, 256]]), AF.Exp),
                     reads=[("tmpS", b)], writes=[("Pm", b)])
                k.op("act", lambda e: e.activation(Pm[b].v(256, [[1, 128]]), psS[:, 256:384], AF.Exp),
                     reads=[pkS], writes=[("Pm", b)])
                for c in range(6):
                    kt = ktoks[c]
                    if kt < 256:
                        vi = kt // 128
                    elif (kt - 256) % 128 == 0:
                        vi = kt // 128
                    else:
                        vi = 18 + (kt - 320) // 128
                    first = (h % 2 == 0 and c == 0)
                    last = (h % 2 == 1 and c == 5)
                    k.op("pe", lambda e, c=c, vi=vi, first=first, last=last: e.matmul(
                        psO[:, cc * 64:cc * 64 + 64], VP[:, vi, h, :], Pm[b][:, c, :], start=first, stop=last),
                        reads=both("VP") + [("Pm", b)], writes=[pkO], count=last)
                    k.op("pe", lambda e, c=c, first=first, last=last: e.matmul(
                        psD[:, cc * 64:cc * 64 + 64], ONP[:, h % 2, :], Pm[b][:, c, :], start=first, stop=last),
                        reads=["ONP", ("Pm", b)], writes=[pkD], count=last)
                if h == 3:
                    rb = r % 2
                    k.op("dve", lambda e: e.reciprocal(rec[rb].v(0, [[1, 128]]), psD[:, 0:128]),
                         reads=[pkD], writes=[("rec", rb)])
                    k.op("dve", lambda e: e.tensor_tensor(
                        YB.v(q0, [[NTOK, 2], [1, 64]]), psO.v(0, [[64, 2], [1, 64]]), rec[rb].v(0, [[64, 2], [1, 64]]), ALU.mult),
                        reads=[pkO, ("rec", rb)], writes=["YB"])

            items = [(r, h) for r in range(32) for h in range(4)]
            st = a_scores(*items[0])
            for i in range(len(items)):
                nxt = a_scores(*items[i + 1]) if i + 1 < len(items) else None
                a_rest(st, i % 2)
                st = nxt
            if CUT == 24:
                k.release(); return
            q0 = None
            Pc = [k.sb("Pc%d" % i, [128, 2, 256], BF16) for i in range(2)]
            recc = k.sb("recc", [128, 256], F32)
            for cc in range(2):
                psO, pkO = k.bank()
                psD, pkD = k.bank()
                for hh in range(2):
                    h = cc * 2 + hh
                    ph = 64 * hh
                    psS, pkS = k.bank()
                    for c in range(2):
                        k.op("pe", lambda e, psS=psS, c=c, cc=cc, ph=ph: e.matmul(
                            psS[:, c * 256:(c + 1) * 256], kT[ph:ph + 64, cc, c * 128:(c + 1) * 128], qT[ph:ph + 64, cc, 0:256],
                            start=True, stop=True), reads=both("kT") + both("qT"), writes=[pkS], count=(c == 1))
                    k.op("act", lambda e, hh=hh, psS=psS: e.activation(Pc[hh].v(0, [[1, 512]]), psS[:, 0:512], AF.Exp),
                         reads=[pkS], writes=[("Pc", hh)])
                    for c in range(2):
                        first = (hh == 0 and c == 0)
                        last = (hh == 1 and c == 1)
                        k.op("pe", lambda e, psO=psO, hh=hh, c=c, h=h, first=first, last=last: e.matmul(
                            psO[:, 0:256], VP[:, c, h, :], Pc[hh][:, c, :], start=first, stop=last),
                            reads=both("VP") + [("Pc", hh)], writes=[pkO], count=last)
                        k.op("pe", lambda e, psD=psD, hh=hh, c=c, first=first, last=last: e.matmul(
                            psD[:, 0:256], ONP[:, hh, :], Pc[hh][:, c, :], start=first, stop=last),
                            reads=["ONP", ("Pc", hh)], writes=[pkD], count=last)
                k.op("dve", lambda e, psD=psD: e.reciprocal(recc[:], psD[:, 0:256]), reads=[pkD], writes=["recc"])
                k.op("dve", lambda e, psO=psO, cc=cc: e.tensor_tensor(YB[:, cc, 0:256], psO[:, 0:256], recc[:], ALU.mult),
                     reads=[pkO, "recc"], writes=["YB"])
            for cc in range(2):
                k.dma("sp", DA(YT, (1 * 256 + cc * 128) * NTOK, [[NTOK, 128], [1, NTOK]]), YB[:, cc, :], reads=["YB"], writes=[("YT", 1)])
            k.release()

        def merge_phase(l):
            k.mark()
            YS = k.sb("YS", [128, 8, NTOK], BF16)
            for br in range(4):
                for cc in range(2):
                    k.dma("sp", YS[:, br * 2 + cc, :], DA(YT, (br * 256 + cc * 128) * NTOK, [[NTOK, 128], [1, NTOK]]),
                          reads=[("YT", br)], writes=["YS"])
            mT = k.sb("mT", [128, 8, NTOK], BF16)
            Wg = [k.sb("Wg%d" % i, [128, 8, 4, 128], BF16) for i in range(2)]
            Wb = [k.sb("Wb%d" % i, [128, 4, 2, 128], BF16) for i in range(2)]
            sgt = [k.sb("sgt%d" % i, [128, 512], F32) for i in range(2)]
            acc = [k.sb("acc%d" % i, [128, 512], F32) for i in range(2)]
            n_ = 0
            def load_dc(dc):
                i = dc % 2
                for br in range(4):
                    k.dma("pool", Wg[i].v(br * 128, [[512, 8], [1, 128]]),
                          DA(I["w_in"], l * 1024 * 6400 + 2304 + br * 1024 + dc * 128, [[6400, 128], [128 * 6400, 8], [1, 128]]),
                          writes=[("Wg", i)])
                k.dma("pool", Wb[i].v(0, [[128, 8], [1, 128]]),
                      DA(I["w_branch"], l * 4 * 256 * 1024 + dc * 128, [[1024, 128], [128 * 1024, 8], [1, 128]]), writes=[("Wb", i)])
            load_dc(0)
            for dc in range(8):
                i = dc % 2
                if dc + 1 < 8:
                    load_dc(dc + 1)
                for mi, (tok0, ntok) in enumerate(MT):
                    a_ = acc[mi % 2]; ak = ("acc", mi % 2)
                    for br in range(4):
                        psA, pkA = k.bank()
                        for kc in range(8):
                            k.op("pe", lambda e, psA=psA, i=i, kc=kc, br=br, tok0=tok0, ntok=ntok: e.matmul(
                                psA[:, 0:ntok], Wg[i][:, kc, br, :], alT[:, kc, tok0:tok0 + ntok], start=(kc == 0), stop=(kc == 7)),
                                reads=[("Wg", i)] + tkeys("alT", tok0, ntok), writes=[pkA], count=(kc == 7))
                        psB, pkB = k.bank()
                        for cc in range(2):
                            k.op("pe", lambda e, psB=psB, i=i, cc=cc, br=br, tok0=tok0, ntok=ntok: e.matmul(
                                psB[:, 0:ntok], Wb[i][:, br, cc, :], YS[:, br * 2 + cc, tok0:tok0 + ntok], start=(cc == 0), stop=(cc == 1)),
                                reads=[("Wb", i), "YS"], writes=[pkB], count=(cc == 1))
                        s_ = sgt[n_ % 2]; sk = ("sgt", n_ % 2); n_ += 1
                        k.op("act", lambda e, psA=psA, s_=s_, ntok=ntok: e.activation(s_[:, 0:ntok], psA[:, 0:ntok], AF.Sigmoid),
                             reads=[pkA], writes=[sk])
                        if br == 0:
                            k.op("dve", lambda e, psB=psB, s_=s_, a_=a_, ntok=ntok: e.tensor_tensor(a_[:, 0:ntok], psB[:, 0:ntok], s_[:, 0:ntok], ALU.mult),
                                 reads=[pkB, sk], writes=[ak])
                        else:
                            k.op("dve", lambda e, psB=psB, s_=s_, ntok=ntok: e.tensor_tensor(s_[:, 0:ntok], psB[:, 0:ntok], s_[:, 0:ntok], ALU.mult),
                                 reads=[pkB, sk], writes=[sk])
                            if br < 3:
                                k.op("dve", lambda e, s_=s_, a_=a_, ntok=ntok: e.tensor_tensor(a_[:, 0:ntok], a_[:, 0:ntok], s_[:, 0:ntok], ALU.add),
                                     reads=[ak, sk], writes=[ak])
                            else:
                                k.op("dve", lambda e, s_=s_, a_=a_, dc=dc, tok0=tok0, ntok=ntok: e.tensor_tensor(
                                    mT[:, dc, tok0:tok0 + ntok], a_[:, 0:ntok], s_[:, 0:ntok], ALU.add),
                                    reads=[ak, sk], writes=tkeys("mT", tok0, ntok))
            WO = k.sb("WO", [128, 8, 1024], BF16)
            for hh in range(2):
                wload(WO.v(hh * 512, [[1024, 8], [1, 512]]), I["w_out"], l * 1024 * 1024 + hh * 512, 1024, 8, 512, "WO")
            GATE = k.sb("GATE", [128, 2, 1024], F32)
            for w in range(2):
                k.dma("sp", GATE[:, w, :], DA(MODROW, (l * 2 + w) * 6144 + 2 * 1024, [[0, 128], [1, 1024]]), reads=[("MODROW", l)], writes=["GATE"])
            HT = [k.sb("HT%d" % i, [128, 1024], F32) for i in range(2)]
            tm = [k.sb("tm%d" % i, [128, 512], F32) for i in range(2)]
            n_ = 0
            for t in range(NTILE):
                b = t % 2
                w = 1 if t < 2 else 0
                k.dma("sp", HT[b][:], DA(H, t * 128 * 1024, [[1024, 128], [1, 1024]]), reads=[("H", t)], writes=[("HT", b)])
                for hh in range(2):
                    ps, pk = k.bank()
                    for dc in range(8):
                        k.op("pe", lambda e, ps=ps, dc=dc, t=t, hh=hh: e.matmul(
                            ps[:, 0:512], mT[:, dc, t * 128:(t + 1) * 128], WO[:, dc, hh * 512:(hh + 1) * 512], start=(dc == 0), stop=(dc == 7)),
                            reads=tkeys("mT", t * 128, 128) + ["WO"], writes=[pk], count=(dc == 7))
                    tq = tm[n_ % 2]; tk = ("tm", n_ % 2); n_ += 1
                    k.op("dve", lambda e, ps=ps, tq=tq, w=w, hh=hh: e.tensor_tensor(tq[:], ps[:, 0:512], GATE[:, w, hh * 512:(hh + 1) * 512], ALU.mult),
                         reads=[pk, "GATE"], writes=[tk])
                    k.op("dve", lambda e, tq=tq, b=b, hh=hh: e.tensor_tensor(HT[b][:, hh * 512:(hh + 1) * 512], HT[b][:, hh * 512:(hh + 1) * 512], tq[:], ALU.add),
                         reads=[tk, ("HT", b)], writes=[("HT", b)])
                k.dma("sp", DA(H, t * 128 * 1024, [[1024, 128], [1, 1024]]), HT[b][:], reads=[("HT", b)], writes=[("H", t)])
            k.release()

        def ffn_phase(l):
            k.mark()
            W1 = [k.sb("W1_%d" % i, [128, 8, 2048], BF16) for i in range(2)]
            W2 = [k.sb("W2_%d" % i, [128, 16, 1024], BF16) for i in range(2)]

            def load_w1(fh, bi):
                for q in range(4):
                    wload(W1[bi].v(q * 512, [[2048, 8], [1, 512]]), I["w_ff1"], l * 1024 * 4096 + fh * 2048 + q * 512, 4096, 8, 512, ("W1", bi))

            def load_w2(fh, bi):
                for q in range(2):
                    wload(W2[bi].v(q * 512, [[1024, 16], [1, 512]]), I["w_ff2"], l * 4096 * 1024 + fh * 2048 * 1024 + q * 512, 1024, 16, 512, ("W2", bi))
            load_w1(0, 0)
            load_w2(0, 0)
            norm_phase(l, 1, alT, "blT")
            GATE = k.sb("GATE5", [128, 2, 1024], F32)
            for w in range(2):
                k.dma("sp", GATE[:, w, :], DA(MODROW, (l * 2 + w) * 6144 + 5 * 1024, [[0, 128], [1, 1024]]), reads=[("MODROW", l)], writes=["GATE5"])
            hT = k.sb("hT", [128, 16, 512], BF16)
            rl = [k.sb("rl%d" % i, [128, 512], BF16) for i in range(2)]
            rd = [k.sb("rd%d" % i, [128, 512], BF16) for i in range(2)]
            HT = [k.sb("HTf%d" % i, [128, 1024], F32) for i in range(2)]
            tm = [k.sb("tmf%d" % i, [128, 512], F32) for i in range(2)]
            n_ = 0
            n2 = 0
            for fh in range(2):
                W1c = W1[fh]
                w1k = ("W1", fh)
                W2c = W2[fh]
                w2k = ("W2", fh)
                if fh == 0:
                    load_w1(1, 1)
                    load_w2(1, 1)
                for (tok0, ntok) in MT:
                    for fc in range(16):
                        ps, pk = k.bank()
                        for kc in range(8):
                            k.op("pe", lambda e, ps=ps, kc=kc, fc=fc, tok0=tok0, ntok=ntok, W1c=W1c: e.matmul(
                                ps[:, 0:ntok], W1c[:, kc, fc * 128:(fc + 1) * 128], alT[:, kc, tok0:tok0 + ntok], start=(kc == 0), stop=(kc == 7)),
                                reads=[w1k] + tkeys("blT", tok0, ntok), writes=[pk], count=(kc == 7))
                        if fc % 2 == 0:
                            rd_ = rd[(fc // 2) % 2]; rdk = ("rd", (fc // 2) % 2)
                            k.op("dve", lambda e, ps=ps, rd_=rd_, ntok=ntok: e.tensor_scalar(rd_[:, 0:ntok], ps[:, 0:ntok], 0.0, None, ALU.max), reads=[pk], writes=[rdk])
                            k.op("dve", lambda e, rd_=rd_, fc=fc, ntok=ntok: e.tensor_tensor(hT[:, fc, 0:ntok], rd_[:, 0:ntok], rd_[:, 0:ntok], ALU.mult),
                                 reads=[rdk], writes=[("hT", 0)])
                        else:
                            r_ = rl[n_ % 2]; rk = ("rl", n_ % 2); n_ += 1
                            k.op("act", lambda e, ps=ps, r_=r_, ntok=ntok: e.activation(r_[:, 0:ntok], ps[:, 0:ntok], AF.Relu), reads=[pk], writes=[rk])
                            k.op("act", lambda e, r_=r_, fc=fc, ntok=ntok: e.activation(hT[:, fc, 0:ntok], r_[:, 0:ntok], AF.Square),
                                 reads=[rk], writes=[("hT", 1)])
                    for tt_ in range(ntok // 128):
                        t = tok0 // 128 + tt_
                        b = t % 2
                        w = 1 if t < 2 else 0
                        k.dma("sp", HT[b][:], DA(H, t * 128 * 1024, [[1024, 128], [1, 1024]]), reads=[("H", t)], writes=[("HTf", b)])
                        for hh in range(2):
                            ps, pk = k.bank()
                            for fc in range(16):
                                k.op("pe", lambda e, ps=ps, fc=fc, tt_=tt_, hh=hh, W2c=W2c: e.matmul(
                                    ps[:, 0:512], hT[:, fc, tt_ * 128:(tt_ + 1) * 128], W2c[:, fc, hh * 512:(hh + 1) * 512],
                                    start=(fc == 0), stop=(fc == 15)), reads=both("hT") + [w2k], writes=[pk], count=(fc == 15))
                            tq = tm[n2 % 2]; tk = ("tmf", n2 % 2); n2 += 1
                            k.op("dve", lambda e, ps=ps, tq=tq, w=w, hh=hh: e.tensor_tensor(tq[:], ps[:, 0:512], GATE[:, w, hh * 512:(hh + 1) * 512], ALU.mult),
                                 reads=[pk, "GATE5"], writes=[tk])
                            k.op("dve", lambda e, tq=tq, b=b, hh=hh: e.tensor_tensor(HT[b][:, hh * 512:(hh + 1) * 512], HT[b][:, hh * 512:(hh + 1) * 512], tq[:], ALU.add),
                                 reads=[tk, ("HTf", b)], writes=[("HTf", b)])
                        k.dma("sp", DA(H, t * 128 * 1024, [[1024, 128], [1, 1024]]), HT[b][:], reads=[("HTf", b)], writes=[("H", t)])
            k.release()

        def final_phase():
            k.mark()
            FG = k.sb("FG", [128, 1024], F32)
            k.dma("sp", FG[:], DA(I["final_g"], 0, [[0, 128], [1, 1024]]), writes=["FG"])
            XT = [k.sb("XTf%d" % i, [128, 1024], F32) for i in range(2)]
            junk = k.sb("junkf", [128, 1024], BF16)
            ss = k.sb("ssf", [128, NTILE], F32)
            k.op("dve", lambda e: e.memset(ss[:], 0.0), writes=["ssf"])
            for t in range(2, NTILE):
                b = t % 2
                xt = XT[b]
                k.dma("sp", xt[:], DA(H, t * 128 * 1024, [[1024, 128], [1, 1024]]), reads=[("H", t)], writes=[("xtf", b)])
                k.op("act", lambda e, xt=xt, t=t: e.activation(junk[:], xt[:], AF.Square, accum_out=ss[:, t:t + 1]),
                     reads=[("xtf", b), "ssf"], writes=["junkf", "ssf"])
                k.op("dve", lambda e, t=t: e.tensor_scalar(ss[:, t:t + 1], ss[:, t:t + 1], 1.0 / 1024, 1e-6, ALU.mult, ALU.add), reads=["ssf"], writes=["ssf"])
                k.op("act", lambda e, t=t: e.activation(ss[:, t:t + 1], ss[:, t:t + 1], AF.Sqrt), reads=["ssf"], writes=["ssf"])
                k.op("dve", lambda e, t=t: e.reciprocal(ss[:, t:t + 1], ss[:, t:t + 1]), reads=["ssf"], writes=["ssf"])
                k.op("dve", lambda e, xt=xt, t=t: e.scalar_tensor_tensor(xt[:], xt[:], ss[:, t:t + 1], FG[:], ALU.mult, ALU.mult),
                     reads=[("xtf", b), "ssf", "FG"], writes=[("xtf", b)])
                k.dma("sp", DA(OUT, (t - 2) * 128 * 1024, [[1024, 128], [1, 1024]]), xt[:], reads=[("xtf", b)], writes=[("OUT", t)])
            k.release()

        def program():
            stages = [("M", phase_M), ("s5setup", s5_setup), ("rpsetup", rp_setup)]
            for l in range(nl):
                stages += [
                    ("norm1_%d" % l, lambda l=l: (norm_phase(l, 0, alT, "alT"), dump("alT%d" % l, alT.v(0, [[1, 8 * NTOK]]), [128, 8 * NTOK], BF16))),
                    ("s5_%d" % l, lambda l=l: (s5_branch(l), dump("YT%d_s5" % l, DA(YT, 0, [[NTOK, 1024], [1, NTOK]]), [1024, NTOK], BF16))),
                    ("attn_%d" % l, lambda l=l: (attn_branch(l), dump("YT%d_attn" % l, DA(YT, 0, [[NTOK, 1024], [1, NTOK]]), [1024, NTOK], BF16))),
                    ("conv_%d" % l, lambda l=l: (conv_branch(l), dump("YT%d_conv" % l, DA(YT, 0, [[NTOK, 1024], [1, NTOK]]), [1024, NTOK], BF16))),
                    ("sgu_%d" % l, lambda l=l: (sgu_branch(l), dump("YT%d_sgu" % l, DA(YT, 0, [[NTOK, 1024], [1, NTOK]]), [1024, NTOK], BF16))),
                    ("merge_%d" % l, lambda l=l: (merge_phase(l), dump("hmid%d" % l, DA(H, 0, [[1024, NTOK], [1, 1024]]), [NTOK, 1024], F32))),
                    ("ffn_%d" % l, lambda l=l: (ffn_phase(l), dump("hend%d" % l, DA(H, 0, [[1024, NTOK], [1, 1024]]), [NTOK, 1024], F32))),
                ]
            stages.append(("final", final_phase))
            for name, fn in stages:
                if skip and name.split("_")[0] in skip:
                    continue
                fn()
                if name == stop:
                    break
        program()
        k.emit()
    return nc, dbg_out


_CACHE = {}


def kernel(**inputs):
    n = 8
    consts = host_constants()
    if "nc" not in _CACHE:
        _CACHE["nc"] = build()[0]
    nc = _CACHE["nc"]
    shared = {}
    for name in INPUT_SHAPES:
        if name in ("x", "c", "ctx"):
            continue
        src = consts[name] if name in consts else inputs[name]
        shared[name] = np.ascontiguousarray(np.asarray(src, dtype=np.float32))
    in_maps = []
    for b in range(n):
        m = dict(shared)
        m["x"] = np.ascontiguousarray(np.asarray(inputs["x"][b], dtype=np.float32))
        m["c"] = np.ascontiguousarray(np.asarray(inputs["c"][b], dtype=np.float32))
        m["ctx"] = np.ascontiguousarray(np.asarray(inputs["ctx"][b], dtype=np.float32))
        in_maps.append(m)
    res = run_bass_kernel_spmd(nc, in_maps, core_ids=list(range(n)))
    return np.stack([np.asarray(r["out"], dtype=np.float32) for r in res.results], axis=0)
```
